# Optimizing a Trainium2 kernel written in Bass

```python
import jax, jax.numpy as jnp
from jax import lax
import numpy as np

D_MODEL = 1024
BATCH = 8
SEQ = 4096
DEPTH = 1

G_GROUPS = 8
G_WIDTH = 512
G_HEAD = G_WIDTH // G_GROUPS
CHUNK = 128
R_WIDTH = 512
R_HEAD = 64
R_HEADS = R_WIDTH // R_HEAD
DECAY_LORA = 32
AAA_LORA = 32
GATE_LORA = 96
D_FF = 4 * D_MODEL
ALPHA = (2.0 * DEPTH) ** 0.25
BETA = (8.0 * DEPTH) ** -0.25
LN_EPS = 1e-5
GN_EPS = 64e-5
R_SHIFT_COLS = 3 * R_WIDTH + DECAY_LORA + AAA_LORA + GATE_LORA
IN_COLS = 2 * G_WIDTH + R_SHIFT_COLS + 2 * D_MODEL

kernel_name = 'hybrid_gmlp_rwkv7_deepnorm_adaln'


def _layer_norm(x, g, b, eps):
    xf = x.astype(jnp.float32)
    mu = xf.mean(-1, keepdims=True)
    var = jnp.square(xf - mu).mean(-1, keepdims=True)
    return (xf - mu) * lax.rsqrt(var + eps) * g + b


def _gmlp_branch(z, g_ln_v, b_ln_v, w_spatial, b_spatial):
    B, S, _ = z.shape
    z = jax.nn.gelu(z)
    u, v = jnp.split(z, 2, axis=-1)
    v = _layer_norm(v, g_ln_v, b_ln_v, LN_EPS)
    v = v.reshape(B, S // CHUNK, CHUNK, G_GROUPS, G_HEAD)
    mask = jnp.tril(jnp.ones((CHUNK, CHUNK), dtype=bool))
    ws = jnp.where(mask[None], w_spatial, 0.0)
    s = jnp.einsum('gts,bcsgd->bctgd', ws, v) + b_spatial.T[:, :, None]
    return u * s.reshape(B, S, G_WIDTH)


def _rwkv7_branch(z, mu_shift, w0, w_decay_up, a0, w_aaa_up, w_gate_up,
                  k_k, k_a, r_k, gn_gain, gn_bias):
    B, S, _ = z.shape
    prev = jnp.pad(z, ((0, 0), (1, 0), (0, 0)))[:, :-1]
    z = z + (prev - z) * mu_shift
    i1 = R_WIDTH
    i2 = 2 * R_WIDTH
    i3 = 3 * R_WIDTH
    i4 = i3 + DECAY_LORA
    i5 = i4 + AAA_LORA
    r, k, v, xw, xa, xg = jnp.split(z, [i1, i2, i3, i4, i5], axis=-1)
    w_log = -jax.nn.softplus(-(w0 + jnp.tanh(xw) @ w_decay_up)) - 0.5
    decay = jnp.exp(-jnp.exp(w_log.astype(jnp.float32)))
    a = jax.nn.sigmoid(a0 + xa @ w_aaa_up)
    g = jax.nn.sigmoid(xg) @ w_gate_up

    def heads(t):
        return t.astype(jnp.float32).reshape(B, S, R_HEADS, R_HEAD)

    kk = heads(k * k_k)
    kk = kk / jnp.maximum(jnp.linalg.norm(kk, axis=-1, keepdims=True), 1e-12)
    k = k * (1.0 + (a - 1.0) * k_a)
    r_h, k_h, v_h, a_h = heads(r), heads(k), heads(v), heads(a)

    def tm(t):
        return jnp.transpose(t, (1, 0, 2, 3))

    def step(state, inp):
        rt, wt, kt, vt, at, bt = inp
        sa = jnp.einsum('bhij,bhj->bhi', state, at)
        state = (state * wt[:, :, None, :] + sa[..., None] * bt[:, :, None, :]
                 + vt[..., None] * kt[:, :, None, :])
        yt = jnp.einsum('bhij,bhj->bhi', state, rt)
        return state, yt

    s0 = jnp.zeros((B, R_HEADS, R_HEAD, R_HEAD), jnp.float32)
    _, y = lax.scan(step, s0, (tm(r_h), tm(heads(decay)), tm(k_h), tm(v_h),
                               tm(-kk), tm(kk * a_h)))
    y = jnp.transpose(y, (1, 0, 2, 3))
    mu = y.mean(-1, keepdims=True)
    var = jnp.square(y - mu).mean(-1, keepdims=True)
    y = ((y - mu) * lax.rsqrt(var + GN_EPS)).reshape(B, S, R_WIDTH) * gn_gain + gn_bias
    bonus = (r_h * k_h * r_k).sum(-1, keepdims=True) * v_h
    return (y + bonus.reshape(B, S, R_WIDTH)) * g


def setup_inputs(seed: int = 0) -> dict:
    key = jax.random.key(seed)
    ks = jax.random.split(key, 36)
    L = DEPTH

    def nrm(k, shape, scale):
        return jax.random.normal(k, shape, jnp.float32) * scale

    w_in = nrm(ks[4], (L, D_MODEL, IN_COLS), D_MODEL ** -0.5)
    v_lo = 2 * G_WIDTH + 2 * R_WIDTH
    w_in = w_in.at[:, :, v_lo:v_lo + R_WIDTH].multiply(BETA)
    return {
        'x': nrm(ks[0], (BATCH, SEQ, D_MODEL), 1.0),
        'c': nrm(ks[1], (BATCH, D_MODEL), 1.0),
        'w_ada': nrm(ks[2], (L, D_MODEL, 6 * D_MODEL), 0.5 * D_MODEL ** -0.5),
        'b_ada': nrm(ks[3], (L, 6 * D_MODEL), 0.02),
        'w_in': w_in,
        'b_in': nrm(ks[5], (L, IN_COLS), 0.02),
        'g_ln_v': 1.0 + nrm(ks[6], (L, G_WIDTH), 0.05),
        'b_ln_v': nrm(ks[7], (L, G_WIDTH), 0.02),
        'w_spatial': nrm(ks[8], (L, G_GROUPS, CHUNK, CHUNK), CHUNK ** -0.5),
        'b_spatial': 1.0 + nrm(ks[9], (L, G_GROUPS, CHUNK), 0.1),
        'mu_shift': jax.random.uniform(ks[10], (L, R_SHIFT_COLS), jnp.float32),
        'w0': jax.random.uniform(ks[11], (L, R_WIDTH), jnp.float32, minval=-5.0, maxval=0.0),
        'w_decay_up': nrm(ks[12], (L, DECAY_LORA, R_WIDTH), DECAY_LORA ** -0.5),
        'a0': nrm(ks[13], (L, R_WIDTH), 0.1),
        'w_aaa_up': nrm(ks[14], (L, AAA_LORA, R_WIDTH), AAA_LORA ** -0.5),
        'w_gate_up': nrm(ks[15], (L, GATE_LORA, R_WIDTH), GATE_LORA ** -0.5),
        'k_k': 0.85 + nrm(ks[16], (L, R_WIDTH), 0.05),
        'k_a': 1.0 + nrm(ks[17], (L, R_WIDTH), 0.05),
        'r_k': nrm(ks[18], (L, R_HEADS, R_HEAD), 0.1),
        'gn_gain': 1.0 + nrm(ks[19], (L, R_WIDTH), 0.05),
        'gn_bias': nrm(ks[20], (L, R_WIDTH), 0.02),
        'w_branch_a': nrm(ks[21], (L, G_WIDTH, D_MODEL), BETA * G_WIDTH ** -0.5),
        'w_branch_b': nrm(ks[22], (L, R_WIDTH, D_MODEL), BETA * R_WIDTH ** -0.5),
        'w_out': nrm(ks[23], (L, D_MODEL, D_MODEL), BETA * D_MODEL ** -0.5),
        'b_out': nrm(ks[24], (L, D_MODEL), 0.01),
        'ln1_g': 1.0 + nrm(ks[25], (L, D_MODEL), 0.05),
        'ln1_b': nrm(ks[26], (L, D_MODEL), 0.02),
        'w_ff1': nrm(ks[27], (L, D_MODEL, D_FF), BETA * D_MODEL ** -0.5),
        'b_ff1': nrm(ks[28], (L, D_FF), 0.01),
        'w_ff2': nrm(ks[29], (L, D_FF, D_MODEL), BETA * D_FF ** -0.5),
        'b_ff2': nrm(ks[30], (L, D_MODEL), 0.01),
        'ln2_g': 1.0 + nrm(ks[31], (L, D_MODEL), 0.05),
        'ln2_b': nrm(ks[32], (L, D_MODEL), 0.02),
    }


def reference(x, c, w_ada, b_ada, w_in, b_in, g_ln_v, b_ln_v, w_spatial, b_spatial,
              mu_shift, w0, w_decay_up, a0, w_aaa_up, w_gate_up, k_k, k_a, r_k,
              gn_gain, gn_bias, w_branch_a, w_branch_b, w_out, b_out, ln1_g, ln1_b,
              w_ff1, b_ff1, w_ff2, b_ff2, ln2_g, ln2_b):
    out_dtype = x.dtype
    h_res = x.astype(jnp.float32)
    c_act = jax.nn.silu(c.astype(jnp.float32))
    g_end = 2 * G_WIDTH
    r_end = g_end + R_SHIFT_COLS
    for l in range(DEPTH):
        mod = c_act @ w_ada[l] + b_ada[l]
        sh1, sc1, gt1, sh2, sc2, gt2 = [m[:, None, :] for m in jnp.split(mod, 6, axis=-1)]
        h = h_res * (1.0 + sc1) + sh1
        proj = h @ w_in[l] + b_in[l]
        y_a = _gmlp_branch(proj[..., :g_end], g_ln_v[l], b_ln_v[l], w_spatial[l], b_spatial[l])
        y_b = _rwkv7_branch(proj[..., g_end:r_end], mu_shift[l], w0[l], w_decay_up[l], a0[l],
                            w_aaa_up[l], w_gate_up[l], k_k[l], k_a[l], r_k[l],
                            gn_gain[l], gn_bias[l])
        gate_a, gate_b = jnp.split(proj[..., r_end:], 2, axis=-1)
        merged = (jax.nn.sigmoid(gate_a) * (y_a @ w_branch_a[l])
                  + jax.nn.sigmoid(gate_b) * (y_b @ w_branch_b[l]))
        mix = merged @ w_out[l] + b_out[l]
        h_res = _layer_norm(ALPHA * h_res + gt1 * mix, ln1_g[l], ln1_b[l], LN_EPS)
        h = h_res * (1.0 + sc2) + sh2
        ff = jnp.square(jax.nn.relu(h @ w_ff1[l] + b_ff1[l])) @ w_ff2[l] + b_ff2[l]
        h_res = _layer_norm(ALPHA * h_res + gt2 * ff, ln2_g[l], ln2_b[l], LN_EPS)
    return h_res.astype(out_dtype)
```

```python
import contextlib
import os
import numpy as np
import concourse.bass as bass
import concourse.mybir as mybir
from concourse.bass_utils import run_bass_kernel_spmd

F32 = mybir.dt.float32
BF16 = mybir.dt.bfloat16
AF = mybir.ActivationFunctionType
ALU = mybir.AluOpType

D = 1024
NCORES = 8
ALPHA = 2.0 ** 0.25
LN_EPS = 1e-5
GN_EPS = 64e-5
EXPM05 = 0.6065306597126334
IN_COLS = 4768
PC_BU, PC_BRKV, PC_BXWXA, PC_BXG, PC_BGATE = 0, 4, 16, 17, 18
PC_MURKV, PC_MUXWXA, PC_MUXG = 34, 46, 47
PC_W0, PC_A0, PC_KK, PC_KA, PC_RK, PC_GNG, PC_GNB, PC_GLN, PC_BLN = 48, 52, 56, 60, 64, 68, 72, 76, 80
PC_BFF1, PC_BADA, NPC = 84, 116, 164
CS_ID, CS_M4, CS_ML4, CS_BD4, CS_ONE, NCS = 0, 128, 640, 1152, 1664, 1792


class StopBuild(Exception):
    pass


class Buf:
    __slots__ = ("name", "writer", "readers", "dsem", "dcount", "ro", "vp")

    def __init__(self, name):
        self.name = name
        self.writer = None
        self.readers = []
        self.dsem = None
        self.dcount = 0
        self.ro = False
        self.vp = None


class Op:
    __slots__ = ("eng", "fn", "deps", "cost", "rid", "epoch", "idx", "dma", "succ", "npend", "ready", "fin", "vps")

    def __init__(self, eng, fn, deps, cost, rid, epoch, dma=None):
        self.eng = eng
        self.fn = fn
        self.deps = deps
        self.cost = cost
        self.rid = rid
        self.epoch = epoch
        self.idx = None
        self.dma = dma
        self.succ = []
        self.npend = 0
        self.ready = 0.0
        self.fin = 0.0
        self.vps = ()


class VP:
    class _V16:
        def __init__(self, vp):
            self.vp = vp

        def __getitem__(self, idx):
            return self.vp.sched.ps16_real[self.vp._k()][idx]

    def __init__(self, sched, n):
        self.sched = sched
        self.b = Buf("vps%d" % n)
        self.b.vp = self
        self.bank = None
        self.remaining = 0
        self.ops = []
        self.as16 = VP._V16(self)

    probing = False

    def _k(self):
        if self.bank is None:
            assert VP.probing, "virtual PSUM bank used before scheduling"
            return 0
        return self.bank

    def __getitem__(self, idx):
        return self.sched.ps_real[self._k()][idx]


DEF_COST = {"pe": 0.6, "act": 0.6, "dve": 0.6, "pool": 1.3, "sp": 3.0}


def _fsize(ap):
    try:
        v = ap.free_size
        v = v() if callable(v) else v
        return int(v)
    except Exception:
        try:
            n = 1
            for d in list(ap.shape)[1:]:
                n *= int(d)
            return n
        except Exception:
            return 512


class _CostProbe:
    def __init__(self, eng):
        self.eng = eng
        self.t = 0.0

    def __getattr__(self, name):
        def f(*a, **kw):
            try:
                if self.eng == "pe":
                    if name == "transpose":
                        self.t += 0.1
                    else:
                        rhs = kw.get("rhs", a[2] if len(a) > 2 else None)
                        n = _fsize(rhs)
                        fp32 = 4.0 if str(getattr(rhs, "dtype", "")).endswith("float32") else 1.0
                        self.t += (0.036 + 0.00036 * max(n, 64)) * fp32
                else:
                    src = kw.get("in_", kw.get("in0", kw.get("data1", kw.get("ap", a[0] if a else None))))
                    n = _fsize(src)
                    if self.eng == "pool":
                        self.t += 0.3 + 0.0019 * n
                    else:
                        k = 0.0021 if name in ("tensor_tensor_scan",) else (0.0064 if name == "reciprocal" else 0.00105)
                        self.t += 0.2 + k * n
            except Exception:
                self.t += DEF_COST.get(self.eng, 0.6)
            return None
        return f


class Sched:
    def __init__(self, nc, stack):
        self.nc = nc
        self.stack = stack
        self.E = {"pe": nc.tensor, "act": nc.scalar, "dve": nc.vector, "pool": nc.gpsimd, "sp": nc.sync}
        self.cnt = {k: 0 for k in self.E}
        self.sem = {k: nc.alloc_semaphore("sem_" + k) for k in ("pe", "act", "dve", "pool")}
        self.waited = {k: {} for k in self.E}
        self.dsems = []
        self.dpool = [nc.alloc_semaphore("dsem%d" % i) for i in range(56)]
        for sm in list(self.sem.values()) + self.dpool:
            nc.gpsimd.sem_clear(sm)
        nc.all_engine_barrier()
        self.ops = []
        self.epoch = 0
        self.rid = 0
        self.prio = "cp"
        self.reserve = 4
        self.bank_free = [0.0] * 8
        self.bank_last = [[] for _ in range(8)]
        self.nvp = 0
        self.ps_real = None
        self.ps16_real = None

    def vbank(self):
        self.nvp += 1
        return VP(self, self.nvp)

    def _deps(self, reads, writes):
        d = []
        self._kinds = {}
        for b in reads:
            if b.writer is not None and b.writer.epoch == self.epoch:
                d.append(b.writer)
                self._kinds[id(b.writer)] = "raw:" + b.name
        nowar = os.environ.get("KSCHED_NOWAR")
        for b in writes:
            if nowar and ((nowar == "1" and not b.name.startswith("ps")) or any(b.name.startswith(p) for p in nowar.split(","))):
                continue
            if b.writer is not None and b.writer.epoch == self.epoch:
                d.append(b.writer)
                self._kinds.setdefault(id(b.writer), "waw:" + b.name)
            for r in b.readers:
                if r.epoch == self.epoch:
                    d.append(r)
                    self._kinds.setdefault(id(r), "war:" + b.name)
        return d

    def _mark(self, op, reads, writes):
        for b in reads:
            if not b.ro:
                b.readers.append(op)
        for b in writes:
            b.writer = op
            b.readers = []

    def op(self, eng, fn, reads=(), writes=(), cost=None):
        ex = [b for b in reads if b.vp is not None]
        if ex:
            reads = [b for b in reads if b.vp is None]
            writes = list(writes) + [b for b in ex if b not in writes]
        self.rid += 1
        if cost is None:
            pr = _CostProbe(eng)
            VP.probing = True
            try:
                fn(pr)
                cost = max(pr.t, 0.1)
            except Exception:
                cost = DEF_COST[eng]
            VP.probing = False
        o = Op(eng, fn, self._deps(reads, writes), cost, self.rid, self.epoch)
        o.succ = self._kinds
        o.vps = list({id(b.vp): b.vp for b in list(reads) + list(writes) if b.vp is not None}.values())
        for vp in o.vps:
            vp.remaining += 1
        self.ops.append(o)
        self._mark(o, reads, writes)

    def dma(self, out, in_, owner, reads=(), writes=(), q="sp", cost=None):
        if owner.dsem is None:
            owner.dsem = self.dpool.pop()
            self.dsems.append(owner)
        owner.dcount += 16
        self.rid += 1
        o = Op(q, None, self._deps(reads, writes), cost if cost is not None else DEF_COST["sp"], self.rid, self.epoch,
               dma=(out, in_, owner.dsem, owner.dcount))
        self.ops.append(o)
        self._mark(o, reads, writes)

    def barrier(self):
        pass

    def _schedule(self):
        ops = self.ops
        kinds = {}
        for o in ops:
            if isinstance(o.succ, dict):
                kinds[id(o)] = o.succ
            o.succ = []
        for o in ops:
            o.deps = list({id(d): d for d in o.deps}.values())
            o.npend = len(o.deps)
            o.ready = 0.0
            for d in o.deps:
                d.succ.append(o)
        prio_mode = os.environ.get("KSCHED_PRIO", self.prio)
        for o in reversed(ops):
            o.fin = o.cost + max([s_.fin + 0.25 for s_ in o.succ] + [0.0])
        tailp = {id(o): (o.fin if prio_mode == "cp" else 0.0) for o in ops}
        if os.environ.get("KSCHED_DBG"):
            print("sched: critical path (infinite engines) = %.1f us" % max([o.fin for o in ops] + [0.0]))
        if os.environ.get("KSCHED_DCP") and len(ops) > 500:
            o = max(ops, key=lambda o: o.fin)
            cnt = {}
            while o.succ:
                nx = max(o.succ, key=lambda s_: s_.fin)
                kd = kinds.get(id(nx), {}).get(id(o), "?")
                if not kd.startswith("raw"):
                    cnt[kd] = cnt.get(kd, 0) + 1
                o = nx
            print("sched: non-RAW edges on dependency critical path:", sorted(cnt.items(), key=lambda kv: -kv[1])[:25])
        order = {k: [] for k in self.E}
        free_at = {k: 0.0 for k in self.E}
        rel = {k: [] for k in self.E}
        spq = [o for o in ops if o.eng == "sp"]
        sp_next = 0
        for o in ops:
            if o.npend == 0 and o.eng != "sp":
                rel[o.eng].append(o)
        nleft = len(ops)
        LAT = 0.25
        bank_free = self.bank_free
        bank_last = self.bank_last
        bank_occ = [None] * 8

        first = {}
        for o in ops:
            for vp in o.vps:
                first.setdefault(id(vp), o)
        allocq = sorted({id(o): o for o in first.values()}.values(), key=lambda o: o.rid)
        apos = {id(o): i for i, o in enumerate(allocq)}
        adone = [False] * len(allocq)
        anext = [0]
        RESERVE = int(os.environ.get("KSCHED_RESERVE", str(self.reserve)))

        def bank_time(o):
            need = [vp for vp in o.vps if vp.bank is None]
            if not need:
                return 0.0
            if id(o) in apos and apos[id(o)] != anext[0]:
                nfree = sum(1 for k in range(8) if bank_occ[k] is None)
                if nfree - len(need) < RESERVE:
                    return None
            free = sorted(bank_free[k] for k in range(8) if bank_occ[k] is None)
            if len(free) < len(need):
                return None
            return free[len(need) - 1]
        while nleft:
            best = None
            for e in self.E:
                if e == "sp":
                    if sp_next < len(spq) and spq[sp_next].npend == 0:
                        o = spq[sp_next]
                        cand = (max(free_at[e], o.ready), o.rid, o)
                    else:
                        continue
                else:
                    if not rel[e]:
                        continue
                    now = free_at[e]
                    est = []
                    for o in rel[e]:
                        bt = bank_time(o)
                        if bt is not None:
                            est.append((max(now, o.ready, bt), o))
                    if not est:
                        continue
                    ready_now = [o for (t_, o) in est if t_ <= now]
                    if ready_now:
                        o = min(ready_now, key=lambda o: (-tailp[id(o)], o.rid))
                        cand = (now, o.rid, o)
                    else:
                        t_, o = min(est, key=lambda to: (to[0], to[1].rid))
                        cand = (t_, o.rid, o)
                if best is None or cand[:2] < best[:2]:
                    best = cand
            assert best is not None, "scheduler deadlock (PSUM banks)"
            st, _, o = best
            e = o.eng
            if id(o) in apos:
                adone[apos[id(o)]] = True
                while anext[0] < len(allocq) and adone[anext[0]]:
                    anext[0] += 1
            for vp in o.vps:
                if vp.bank is None:
                    k = min((k for k in range(8) if bank_occ[k] is None), key=lambda k: bank_free[k])
                    vp.bank = k
                    bank_occ[k] = vp
                    o.deps = o.deps + [d for d in bank_last[k] if d.epoch == self.epoch]
            if e == "sp":
                sp_next += 1
                free_at[e] = st + 0.06
                o.fin = st + o.cost
            else:
                rel[e].remove(o)
                free_at[e] = st + o.cost
                o.fin = st + o.cost
            order[e].append(o)
            nleft -= 1
            for vp in o.vps:
                vp.ops.append(o)
                vp.remaining -= 1
                if vp.remaining == 0:
                    k = vp.bank
                    bank_free[k] = max(a.fin for a in vp.ops)
                    bank_last[k] = list(vp.ops)
                    bank_occ[k] = None
            for s_ in o.succ:
                s_.ready = max(s_.ready, o.fin + LAT)
                s_.npend -= 1
                if s_.npend == 0 and s_.eng != "sp":
                    rel[s_.eng].append(s_)
        if os.environ.get("KSCHED_DBG"):
            print("sched: n=%d makespan=%.1f us busy=%s" % (len(ops), max([o.fin for o in ops] + [0.0]),
                  {k: round(sum(o.cost for o in v), 1) for k, v in order.items()}))
        if os.environ.get("KSCHED_CP") and len(ops) > 500:
            prev_on_eng = {}
            for e, lst in order.items():
                for a, b in zip(lst, lst[1:]):
                    prev_on_eng[id(b)] = a
            o = max(ops, key=lambda o: o.fin)
            path = []
            while o is not None and len(path) < int(os.environ["KSCHED_CP"]):
                st = o.fin - o.cost
                why, nxt = "start", None
                best = None
                for d in o.deps:
                    if best is None or d.fin > best.fin:
                        best = d
                pe_ = prev_on_eng.get(id(o))
                if best is not None and abs((best.fin + 0.25) - st) < 1e-6:
                    why, nxt = "dep", best
                elif pe_ is not None:
                    why, nxt = "eng", pe_
                elif best is not None:
                    why, nxt = "dep?", best
                ln = o.fn.__code__.co_firstlineno if o.fn is not None else -1
                path.append("%8.1f %-4s cost=%5.2f line=%d via=%s" % (st, o.eng, o.cost, ln, why))
                o = nxt
            print("\n".join(path))
        return order

    def flush(self):
        nc = self.nc
        order = self._schedule()
        prog = {k: [] for k in self.E}
        for e, lst in order.items():
            for o in lst:
                if o.dma is None:
                    self.cnt[e] += 1
                    o.idx = self.cnt[e]
        for e, lst in order.items():
            eng = self.E[e]
            for o in lst:
                need = {}
                for d in o.deps:
                    if d.dma is None:
                        key, sm, val = ("eng", d.eng), self.sem[d.eng], d.idx
                    else:
                        key, sm, val = ("dma", id(d.dma[2])), d.dma[2], d.dma[3]
                    if self.waited[e].get(key, 0) >= val:
                        continue
                    if key not in need or need[key][1] < val:
                        need[key] = (sm, val)
                waits = []
                for key, (sm, val) in need.items():
                    self.waited[e][key] = val
                    waits.append((sm, val))
                if o.dma is None:
                    def run(eng=eng, waits=waits, fn=o.fn, sem=self.sem[e]):
                        for (sm, v) in waits:
                            eng.wait_ge(sm, v)
                        fn(eng).then_inc(sem, 1)
                else:
                    def run(eng=eng, waits=waits, d=o.dma):
                        for (sm, v) in waits:
                            eng.wait_ge(sm, v)
                        eng.dma_start(out=d[0], in_=d[1]).then_inc(d[2], 16)
                prog[e].append(run)
        snap = [(self.sem[k], self.cnt[k], ("eng", k)) for k in self.sem if self.cnt[k] > 0]
        snap += [(o.dsem, o.dcount, ("dma", id(o.dsem))) for o in self.dsems]
        for e in self.E:
            eng = self.E[e]
            ws = []
            for (sm, v, key) in snap:
                if self.waited[e].get(key, 0) < v:
                    self.waited[e][key] = v
                    ws.append((sm, v))

            def runb(eng=eng, ws=ws):
                for (sm, v) in ws:
                    eng.wait_ge(sm, v)
            prog[e].append(runb)
        with nc.Block() as block:
            @block.sync
            def _(e):
                for f in prog["sp"]:
                    f()

            @block.tensor
            def _(e):
                for f in prog["pe"]:
                    f()

            @block.scalar
            def _(e):
                for f in prog["act"]:
                    f()

            @block.vector
            def _(e):
                for f in prog["dve"]:
                    f()

            @block.gpsimd
            def _(e):
                for f in prog["pool"]:
                    f()
        self.ops = []
        self.epoch += 1
        self.bank_free = [0.0] * 8
        self.bank_last = [[] for _ in range(8)]


class T:
    def __init__(self, stack, nc, name, shape, dt, psum=False):
        mk = nc.psum_tensor if psum else nc.sbuf_tensor
        self.t = stack.enter_context(mk("s_" + name, list(shape), dt))
        self.b = Buf(name)

    def __getitem__(self, idx):
        return self.t[idx]


def _build(S, n_tiles_c, nc_box):
    NT = S // 128
    NSUP = NT // n_tiles_c
    TC = n_tiles_c * 128
    nc = bass.Bass("TRN2", target_bir_lowering=False)
    nc_box.append(nc)
    dram = lambda name, shape, dt=F32, kind="ExternalInput": nc.dram_tensor(name, list(shape), dt, kind=kind).ap()
    x_d = dram("x", [S, D])
    ccol_d = dram("ccol", [128, 16])
    wada_d = dram("w_ada", [D, 6 * D])
    win_d = dram("w_in", [D, IN_COLS])
    pcol_d = dram("pcol", [128, NPC])
    cst_d = dram("cst", [128, NCS])
    brow_d = dram("brow", [1, 2560])
    lnrow_d = dram("lnrows", [4, D])
    wst_d = dram("wsT", [128, 1024])
    bspb_d = dram("bspb", [128, 512])
    lw_d = dram("lw", [64, 512])
    wg_d = dram("wg", [96, 512])
    wa_d = dram("w_branch_a", [512, D])
    wb_d = dram("w_branch_b", [512, D])
    wout_d = dram("w_out", [D, D])
    wff1_d = dram("w_ff1", [D, 4 * D])
    wff2_d = dram("w_ff2", [4 * D, D])
    out_d = dram("out", [S, D], kind="ExternalOutput")
    yab_d = nc.dram_tensor("yab_s", [S, D], BF16).ap()
    h1_d = nc.dram_tensor("h1_s", [S, D], F32).ap()
    yab_db = [Buf("yabd%d" % i) for i in range(NT)]
    h1_db = [Buf("h1d%d" % i) for i in range(NT)]

    with contextlib.ExitStack() as G:
        G.enter_context(nc.cleanup_on_exit())
        G.enter_context(nc.allow_low_precision("bf16 matmuls with fp32 accumulation"))
        sc = Sched(nc, G)
        mkT = lambda st, name, shape, dt=F32: T(st, nc, name, shape, dt)

        def ck(tag):
            if os.environ.get("KSTOP") == tag:
                sc.barrier()
                sc.flush()
                raise StopBuild()
        cst = mkT(G, "cst", [128, NCS])
        pcol = mkT(G, "pcol", [128, NPC])
        modT = mkT(G, "modT", [128, 64])
        identb = mkT(G, "identb", [128, 128], BF16)
        onesb = mkT(G, "onesb", [1, 128], BF16)
        browb = mkT(G, "browb", [1, 2560], BF16)
        psr = [T(G, nc, "ps%d" % i, [128, 512], F32, psum=True) for i in range(8)]
        sc.ps_real = [p.t for p in psr]
        sc.ps16_real = [p.t.bitcast(BF16) for p in psr]
        vbank = sc.vbank

        pc = lambda j, p0=0, p1=128: pcol[p0:p1, j:j + 1]
        ident = cst[:, CS_ID:CS_ID + 128]
        SH1, GT1, SH2, GT2, OPS1, OPS2 = 0, 16, 24, 40, 48, 56

        sc.dma(cst[:, :], cst_d[:, :], cst.b, writes=[cst.b])
        sc.dma(pcol[:, :], pcol_d[:, :], pcol.b, writes=[pcol.b])
        sc.op("dve", lambda e: e.tensor_copy(out=identb[:, :], in_=ident), reads=[cst.b], writes=[identb.b])
        sc.op("dve", lambda e: e.memset(onesb[:, :], 1.0), writes=[onesb.b])

        with contextlib.ExitStack() as P0:
            ccol = mkT(P0, "ccol", [128, 16])
            cact = mkT(P0, "cact", [128, 16])
            csig = mkT(P0, "csig", [128, 16])
            browf = mkT(P0, "browf", [1, 2560])
            stg = [mkT(P0, "stgm%d" % i, [128, 8 * 256]) for i in range(8)]
            sc.dma(ccol[:, :], ccol_d[:, :], ccol.b, writes=[ccol.b])
            sc.dma(browf[:, :], brow_d[:, :], browf.b, writes=[browf.b])
            sc.op("act", lambda e: e.activation(out=csig[:, :], in_=ccol[:, :], func=AF.Sigmoid),
                  reads=[ccol.b], writes=[csig.b])
            sc.op("dve", lambda e: e.tensor_tensor(out=cact[:, :], in0=ccol[:, :], in1=csig[:, :], op=ALU.mult),
                  reads=[ccol.b, csig.b], writes=[cact.b])
            sc.op("dve", lambda e: e.tensor_copy(out=browb[:, :], in_=browf[:, :]), reads=[browf.b], writes=[browb.b])
            wada_v = wada_d.rearrange("(k p) f -> p k f", p=128)
            modrow = mkT(P0, "modrow", [2, 6 * D])
            prow = None
            for piece in range(24):
                sg = stg[piece % 8]
                sgv = sg[:, :].rearrange("p (k f) -> p k f", k=8)
                sc.dma(sgv, wada_v[:, :, piece * 256:(piece + 1) * 256], sg.b, writes=[sg.b])
                if piece % 2 == 0:
                    prow = vbank()

                def f(e, sgv=sgv, piece=piece, prow=prow):
                    r = None
                    for k in range(8):
                        r = e.matmul(prow[0:2, (piece % 2) * 256:(piece % 2 + 1) * 256], lhsT=cact[:, 2 * k:2 * k + 2],
                                     rhs=sgv[:, k, :], start=(k == 0), stop=(k == 7))
                    return r
                sc.op("pe", f, reads=[sg.b, cact.b], writes=[prow.b])
                if piece % 2 == 1:
                    g0 = (piece // 2) * 512
                    sc.op("act", lambda e, prow=prow, g0=g0: e.activation(out=modrow[0:2, g0:g0 + 512], in_=prow[0:2, :], func=AF.Identity),
                          reads=[prow.b, modrow.b], writes=[modrow.b])
            pm = vbank()

            def ftm(e):
                r = None
                for m in range(48):
                    r = e.transpose(out=pm[:, 2 * m:2 * m + 2], in_=modrow[0:2, m * 128:(m + 1) * 128], identity=ident[0:2, 0:2])
                return r
            sc.op("pe", ftm, reads=[modrow.b, cst.b], writes=[pm.b])
            sc.op("dve", lambda e: e.tensor_tensor(out=modT[:, 0:48], in0=pm[:, 0:96].rearrange("p (m two) -> p m two", two=2)[:, :, 0],
                                                   in1=pcol[:, PC_BADA:PC_BADA + 48], op=ALU.add),
                  reads=[pm.b, pcol.b], writes=[modT.b])
            sc.op("dve", lambda e: e.tensor_scalar(out=modT[:, 48:56], in0=modT[:, 8:16], scalar1=1.0, scalar2=None, op0=ALU.add),
                  reads=[modT.b], writes=[modT.b])
            sc.op("dve", lambda e: e.tensor_scalar(out=modT[:, 56:64], in0=modT[:, 32:40], scalar1=1.0, scalar2=None, op0=ALU.add),
                  reads=[modT.b], writes=[modT.b])
            sc.barrier()
            sc.flush()
        if os.environ.get("KSTOP") == "setup":
            return nc
        md = lambda j: modT[:, j:j + 1]

        def load_w_bf16(st, name, src_view, nk, ncols, stgs, engs=("pool", "dve")):
            w = st if isinstance(st, T) else mkT(st, name, [128, nk * ncols], BF16)
            CH = stgs[0].t.shape[1]
            n = 0
            for k in range(nk):
                for c0 in range(0, ncols, CH):
                    cw = min(CH, ncols - c0)
                    sg = stgs[n % len(stgs)]
                    sc.dma(sg[:, 0:cw], src_view[:, k, c0:c0 + cw], sg.b, writes=[sg.b])
                    eng = engs[n % len(engs)]
                    sc.op(eng, lambda e, sg=sg, cw=cw, k=k, c0=c0: (
                        e.tensor_copy(out=w[:, k * ncols + c0:k * ncols + c0 + cw], in_=sg[:, 0:cw]) if hasattr(e, "tensor_copy")
                        else e.activation(out=w[:, k * ncols + c0:k * ncols + c0 + cw], in_=sg[:, 0:cw], func=AF.Identity)),
                        reads=[sg.b], writes=[w.b])
                    n += 1
            return w

        def load_w_cast(w, src_view, nk, ncols):
            chunks = [(k, c0, min(2048, ncols - c0)) for k in range(nk) for c0 in range(0, ncols, 2048)]
            for n_, (k, c0, cw) in enumerate(chunks):
                last = n_ == len(chunks) - 1
                sc.dma(w[:, k * ncols + c0:k * ncols + c0 + cw], src_view[:, k, c0:c0 + cw], w.b,
                       writes=[w.b] if last else [], q="pool", cost=4.0)
            return w

        def x_to_hT(e_pe_reads, xt_ap, xt_b, hT, psb, ops_col, sh_col, ntok=128, tok0=0, col0=0):
            pa, pb = vbank(), vbank()
            W = hT.t.shape[1] // 8

            def f(e):
                r = None
                for k in range(8):
                    p = pa if k < 4 else pb
                    r = e.transpose(out=p[:, (k % 4) * 128:(k % 4 + 1) * 128], in_=xt_ap[:, col0 + k * 128:col0 + (k + 1) * 128], identity=ident)
                return r
            sc.op("pe", f, reads=[xt_b, cst.b], writes=[pa.b, pb.b])

            def g(e):
                r = None
                for k in range(8):
                    p = pa if k < 4 else pb
                    r = e.activation(out=hT[:, k * W + tok0:k * W + tok0 + 128], in_=p[:, (k % 4) * 128:(k % 4 + 1) * 128],
                                     func=AF.Identity, bias=md(sh_col + k), scale=md(ops_col + k))
                return r
            sc.op("act", g, reads=[pa.b, pb.b, modT.b], writes=[hT.b])

        def gelu_from(psrc_fn, nchunk, bias_fn, dst, tmp, parts=128):
            gx, gx2 = tmp[0], tmp[1]
            gt = gt2 = gs = gx2
            pbufs = psrc_fn("bufs")

            def f1(e):
                r = None
                for c in range(nchunk):
                    r = e.activation(out=gx[:, c * 128:(c + 1) * 128], in_=psrc_fn(c), func=AF.Identity, bias=bias_fn(c), scale=1.0)
                return r
            sc.op("act", f1, reads=pbufs + [pcol.b], writes=[gx.b])

            def f2(e):
                r = None
                for c in range(nchunk):
                    r = e.activation(out=gx2[:, c * 128:(c + 1) * 128], in_=psrc_fn(c), func=AF.Square, bias=bias_fn(c), scale=1.0)
                return r
            sc.op("act", f2, reads=pbufs + [pcol.b], writes=[gx2.b])
            n = nchunk * 128
            sc.op("pool", lambda e: e.tensor_scalar(out=gt[:, 0:n], in0=gx2[:, 0:n], scalar1=0.044715, scalar2=1.0, op0=ALU.mult, op1=ALU.add),
                  reads=[gx2.b], writes=[gt.b])
            sc.op("pool", lambda e: e.tensor_tensor(out=gt2[:, 0:n], in0=gt[:, 0:n], in1=gx[:, 0:n], op=ALU.mult),
                  reads=[gt.b, gx.b], writes=[gt2.b])
            sc.op("act", lambda e: e.activation(out=gs[:, 0:n], in_=gt2[:, 0:n], func=AF.Sigmoid, scale=1.5957691216057308),
                  reads=[gt2.b], writes=[gs.b])
            sc.op("pool", lambda e: e.tensor_tensor(out=dst[:, 0:n], in0=gx[:, 0:n], in1=gs[:, 0:n], op=ALU.mult),
                  reads=[gx.b, gs.b], writes=[dst.b])

        def layernorm_rows(src, dst_scaled, gb, bb, tmp_small, out_t, out_eng="pool", c0=0):
            st6, mv, sd, rstd = tmp_small
            sv = lambda a, b: src[:, c0 + a:c0 + b]
            sc.op("dve", lambda e: e.bn_stats(out=st6[:, 0:6], in_=sv(0, 512)), reads=[src.b], writes=[st6.b])
            sc.op("dve", lambda e: e.bn_stats(out=st6[:, 6:12], in_=sv(512, 1024)), reads=[src.b, st6.b], writes=[st6.b])
            sc.op("dve", lambda e: e.bn_aggr(out=mv[:, 0:2], in_=st6[:, 0:12]), reads=[st6.b], writes=[mv.b])
            sc.op("act", lambda e: e.activation(out=sd[:, 0:1], in_=mv[:, 1:2], func=AF.Sqrt, bias=cst[:, CS_ONE + 1:CS_ONE + 2], scale=1.0),
                  reads=[mv.b, cst.b], writes=[sd.b])
            sc.op("dve", lambda e: e.reciprocal(out=rstd[:, 0:1], in_=sd[:, 0:1]), reads=[sd.b], writes=[rstd.b])
            sc.op("dve", lambda e: e.tensor_scalar(out=dst_scaled[:, :], in0=sv(0, 1024), scalar1=mv[:, 0:1], scalar2=rstd[:, 0:1],
                                                   op0=ALU.subtract, op1=ALU.mult), reads=[src.b, mv.b, rstd.b], writes=[dst_scaled.b])
            sc.op(out_eng, lambda e: e.tensor_tensor(out=sv(0, 1024), in0=dst_scaled[:, :], in1=gb[:, :], op=ALU.mult),
                  reads=[dst_scaled.b, gb.b, src.b], writes=[src.b])
            sc.op(out_eng, lambda e: e.tensor_tensor(out=out_t[:, :], in0=sv(0, 1024), in1=bb[:, :], op=ALU.add),
                  reads=[src.b, bb.b], writes=[out_t.b])

        with contextlib.ExitStack() as PA:
            NA = 2720
            winA = mkT(PA, "winA", [128, 8 * NA], BF16)
            wcol = lambda k, c0, m: winA[:, k * NA + c0:k * NA + c0 + m]
            lwb = mkT(PA, "lwb", [64, 512], BF16)
            wgb = mkT(PA, "wgb", [96, 512], BF16)
            wsTb = mkT(PA, "wsTb", [128, 1024], BF16)
            B2 = mkT(PA, "B2", [128, 512])
            with contextlib.ExitStack() as TA:
                stgs = [mkT(TA, "stga%d" % i, [128, 1024]) for i in range(2)]
                load_w_cast(winA, win_d.rearrange("(k p) f -> p k f", p=128)[:, :, 0:NA], 8, NA)
                sc.dma(stgs[0][0:64, 0:512], lw_d[:, :], stgs[0].b, writes=[stgs[0].b])
                sc.op("dve", lambda e: e.tensor_copy(out=lwb[:, :], in_=stgs[0][0:64, 0:512]), reads=[stgs[0].b], writes=[lwb.b])
                sc.dma(stgs[1][0:96, 0:512], wg_d[:, :], stgs[1].b, writes=[stgs[1].b])
                sc.op("dve", lambda e: e.tensor_copy(out=wgb[:, :], in_=stgs[1][0:96, 0:512]), reads=[stgs[1].b], writes=[wgb.b])
                sc.dma(stgs[0][:, 0:1024], wst_d[:, :], stgs[0].b, writes=[stgs[0].b])

                def fws(e):
                    r = None
                    for g in range(8):
                        r = e.tensor_tensor(out=wsTb[:, g * 128:(g + 1) * 128], in0=stgs[0][:, g * 128:(g + 1) * 128],
                                            in1=cst[:, CS_M4 + 128:CS_M4 + 256], op=ALU.mult)
                    return r
                sc.op("dve", fws, reads=[stgs[0].b, cst.b], writes=[wsTb.b])
                sc.dma(B2[:, :], bspb_d[:, :], B2.b, writes=[B2.b])
                sc.barrier()
                sc.flush()
            xt = [mkT(PA, "xtA%d" % i, [128, D]) for i in range(2)]
            hT = mkT(PA, "hTA", [128, 8 * 128], BF16)
            g5 = [mkT(PA, "gel%d" % i, [128, 512]) for i in range(2)]
            uT = mkT(PA, "uT", [128, 512])
            vg = mkT(PA, "vg", [128, 512])
            vn = mkT(PA, "vn", [128, 512], BF16)
            onesv = mkT(PA, "onesv", [128, 512], BF16)
            onesf = mkT(PA, "onesf", [128, 128])
            sm = [mkT(PA, "smA%d" % i, [128, 64]) for i in range(6)]
            tmpa = mkT(PA, "tmpa", [128, 512])
            yab = [mkT(PA, "yab%d" % i, [128, 1024], BF16) for i in range(2)]
            zs = mkT(PA, "zs", [128, 14 * 129])
            zp = mkT(PA, "zp", [128, 14 * 128])
            txa = mkT(PA, "txa", [64, 128], BF16)
            txf = mkT(PA, "txf", [32, 128])
            negh = mkT(PA, "negh", [128, 8])
            sxg = mkT(PA, "sxg", [96, 128], BF16)
            sw = mkT(PA, "sw", [128, 512])
            a_ = mkT(PA, "a_", [128, 512])
            gT_2 = [mkT(PA, "gT_%d" % i_, [128, 512]) for i_ in range(2)]
            cum = mkT(PA, "cum", [128, 4 * 129])
            gam = mkT(PA, "gam", [128, 4 * 129])
            igam = mkT(PA, "igam", [128, 512])
            gC_2 = [mkT(PA, "gC_%d" % i_, [128, 4]) for i_ in range(2)]
            kq = mkT(PA, "kq", [128, 512])
            nrm = mkT(PA, "nrm", [128, 512])
            inv = mkT(PA, "inv", [128, 512])
            kk = mkT(PA, "kk", [128, 512])
            t1 = mkT(PA, "t1", [128, 512])
            kmod = mkT(PA, "kmod", [128, 512])
            t2 = mkT(PA, "t2", [128, 512])
            rkp = mkT(PA, "rkp", [128, 512])
            bonus_2 = [mkT(PA, "bonus_%d" % i_, [128, 512]) for i_ in range(2)]
            RA_2 = [mkT(PA, "RA_%d" % i_, [128, 1024], BF16) for i_ in range(2)]
            KTf_2 = [mkT(PA, "KTf_%d" % i_, [128, 512], BF16) for i_ in range(2)]
            BTf_2 = [mkT(PA, "BTf_%d" % i_, [128, 512], BF16) for i_ in range(2)]
            vb = mkT(PA, "vb", [128, 512], BF16)
            KBt_2 = [mkT(PA, "KBt_%d" % i_, [128, 1024], BF16) for i_ in range(2)]
            Vt_2 = [mkT(PA, "Vt_%d" % i_, [128, 512], BF16) for i_ in range(2)]
            AM_2 = [mkT(PA, "AM_%d" % i_, [128, 8 * 512], BF16) for i_ in range(2)]
            Pp = [mkT(PA, "Pp%d" % i, [128, 1024], BF16) for i in range(7)]
            PTp = [mkT(PA, "PTp%d" % i, [128, 1024], BF16) for i in range(2)]
            Xf = [mkT(PA, "Xf%d" % i, [128, 512]) for i in range(2)]
            Xb = [mkT(PA, "Xb%d" % i, [128, 512], BF16) for i in range(2)]
            Sf = mkT(PA, "Sf", [128, 512])
            Sbd = mkT(PA, "Sbd", [128, 512], BF16)
            st1 = mkT(PA, "st1", [128, 512])
            st2 = mkT(PA, "st2", [128, 512])
            gst = mkT(PA, "gst", [128, 48])
            gmv = mkT(PA, "gmv", [128, 16])
            gsd = mkT(PA, "gsd", [128, 8])
            grs = mkT(PA, "grs", [128, 8])
            ynb = mkT(PA, "ynb", [128, 512], BF16)
            yt = mkT(PA, "yt", [128, 512])
            yt2 = yt
            zsv = zs[:, :].rearrange("p (j t) -> p j t", j=14)
            zpv = zp[:, :].rearrange("p (j t) -> p j t", j=14)
            cumv = cum[:, :].rearrange("p (c t) -> p c t", c=4)
            gamv = gam[:, :].rearrange("p (c t) -> p c t", c=4)
            v3 = lambda t_, c=4: t_[:, :].rearrange("p (c t) -> p c t", c=c)
            for t_ in (zs, cum, Sf, zp):
                sc.op("pool", lambda e, t_=t_: e.memset(t_[:, :], 0.0), writes=[t_.b])
            sc.op("pool", lambda e: e.memset(Sbd[:, :], 0.0), writes=[Sbd.b])
            sc.op("pool", lambda e: e.memset(onesv[:, :], 1.0), writes=[onesv.b])
            sc.op("pool", lambda e: e.memset(onesf[:, :], 1.0), writes=[onesf.b])
            sc.op("pool", lambda e: e.memset(negh[:, :], -0.5), writes=[negh.b])
            m4b = mkT(PA, "m4b", [128, 512], BF16)
            sc.op("dve", lambda e: e.tensor_copy(out=m4b[:, :], in_=cst[:, CS_M4:CS_M4 + 512]), reads=[cst.b], writes=[m4b.b])

            def spatial(src_bf, pbank):
                def f(e):
                    r = None
                    for g in range(8):
                        c, pb = g // 2, (g % 2) * 64
                        r = e.matmul(pbank[pb:pb + 64, c * 128:(c + 1) * 128], lhsT=src_bf[:, g * 64:(g + 1) * 64],
                                     rhs=wsTb[:, g * 128:(g + 1) * 128], start=True, stop=True)
                    return r
                sc.op("pe", f, reads=[src_bf.b, wsTb.b], writes=[pbank.b])
            pb_ = vbank()
            spatial(onesv, pb_)

            def fB2(e):
                r = None
                for c in range(4):
                    r = e.scalar_tensor_tensor(out=B2[:, c * 128:(c + 1) * 128], in0=pb_[:, c * 128:(c + 1) * 128], scalar=pc(PC_BLN + c),
                                               in1=B2[:, c * 128:(c + 1) * 128], op0=ALU.mult, op1=ALU.add)
                return r
            sc.op("dve", fB2, reads=[pb_.b, pcol.b, B2.b], writes=[B2.b])

            def load_x(i):
                xs_ = xt[i % 2]
                sc.dma(xs_[:, :], x_d[i * 128:(i + 1) * 128, :], xs_.b, writes=[xs_.b])

            sl = lambda t_, c, w=128: t_[:, c * w:(c + 1) * w]
            ck("A0")
            hk = lambda k: hT[:, k * 128:(k + 1) * 128]

            def front_h(i):
                xs = xt[i % 2]
                x_to_hT(None, xs, xs.b, hT, 0, OPS1, SH1)
                yield

            def front_g(i):
                RA, KTf, BTf, KBt, Vt, gT, bonus, gC = RA_2[i % 2], KTf_2[i % 2], BTf_2[i % 2], KBt_2[i % 2], Vt_2[i % 2], gT_2[i % 2], bonus_2[i % 2], gC_2[i % 2]
                RAv = RA[:, :].rearrange("p (c q t) -> p c q t", c=4, q=2)
                ya = yab[i % 2]
                pu = vbank()

                def fu(e, pu=pu):
                    r = None
                    for c in range(4):
                        for k in range(8):
                            r = e.matmul(sl(pu, c), lhsT=wcol(k, c * 128, 128), rhs=hk(k), start=(k == 0), stop=(k == 7))
                    return r
                sc.op("pe", fu, reads=[winA.b, hT.b], writes=[pu.b])
                yield
                gelu_from(lambda c, pu=pu: [pu.b] if c == "bufs" else sl(pu, c), 4, lambda c: pc(PC_BU + c), uT, g5)
                yield
                ck("A2")
                pv = vbank()

                def fv(e, pv=pv):
                    for k in range(8):
                        e.matmul(pv[:, :], lhsT=hk(k), rhs=wcol(k, 512, 512), start=(k == 0), stop=False)
                    return e.matmul(pv[:, :], lhsT=onesb[0:1, 0:128], rhs=browb[0:1, 0:512], start=False, stop=True)
                sc.op("pe", fv, reads=[winA.b, hT.b, onesb.b, browb.b], writes=[pv.b])
                yield
                gelu_from(lambda c, pv=pv: [pv.b] if c == "bufs" else sl(pv, c), 4, lambda c: 0.0, vg, g5)
                yield
                st6, mv, sd, rstd = sm[0], sm[1], sm[2], sm[3]
                sc.op("dve", lambda e: e.bn_stats(out=st6[:, 0:6], in_=vg[:, :]), reads=[vg.b], writes=[st6.b])
                yield
                sc.op("dve", lambda e: e.bn_aggr(out=mv[:, 0:2], in_=st6[:, 0:6]), reads=[st6.b], writes=[mv.b])
                yield
                sc.op("pool", lambda e: e.tensor_scalar(out=sd[:, 0:1], in0=mv[:, 1:2], scalar1=LN_EPS, scalar2=None, op0=ALU.add),
                      reads=[mv.b], writes=[sd.b])
                yield
                sc.op("pool", lambda e: e.tensor_tensor(out=rstd[:, 0:1], in0=sd[:, 0:1], in1=negh[:, 0:1], op=ALU.pow), reads=[sd.b, negh.b], writes=[rstd.b])
                yield
                sc.op("dve", lambda e: e.tensor_scalar(out=vn[:, :], in0=vg[:, :], scalar1=mv[:, 0:1], scalar2=rstd[:, 0:1],
                                                       op0=ALU.subtract, op1=ALU.mult), reads=[vg.b, mv.b, rstd.b], writes=[vn.b])
                yield
                ck("A3")
                pS = vbank()
                spatial(vn, pS)
                yield

                def fya(e, pS=pS):
                    r = None
                    for c in range(4):
                        r = e.scalar_tensor_tensor(out=sl(tmpa, c), in0=sl(pS, c), scalar=pc(PC_GLN + c), in1=sl(B2, c), op0=ALU.mult, op1=ALU.add)
                    return r
                sc.op("dve", fya, reads=[pS.b, pcol.b, B2.b], writes=[tmpa.b])
                yield
                sc.op("dve", lambda e, ya=ya: e.tensor_tensor(out=ya[:, 0:512], in0=tmpa[:, :], in1=uT[:, :], op=ALU.mult),
                      reads=[tmpa.b, uT.b], writes=[ya.b])
                yield

            def front_r(i):
                RA, KTf, BTf, KBt, Vt, gT, bonus, gC = RA_2[i % 2], KTf_2[i % 2], BTf_2[i % 2], KBt_2[i % 2], Vt_2[i % 2], gT_2[i % 2], bonus_2[i % 2], gC_2[i % 2]
                RAv = RA[:, :].rearrange("p (c q t) -> p c q t", c=4, q=2)
                ya = yab[i % 2]
                ck("A4")
                pz = [vbank() for _ in range(4)]

                def fz(e, pz=pz):
                    r = None
                    for j in range(12):
                        for k in range(8):
                            r = e.matmul(sl(pz[j // 4], j % 4), lhsT=wcol(k, 1024 + j * 128, 128), rhs=hk(k), start=(k == 0), stop=(k == 7))
                    for k in range(8):
                        r = e.matmul(pz[3][0:64, 0:128], lhsT=wcol(k, 2560, 64), rhs=hk(k), start=(k == 0), stop=(k == 7))
                    for k in range(8):
                        r = e.matmul(pz[3][0:96, 128:256], lhsT=wcol(k, 2624, 96), rhs=hk(k), start=(k == 0), stop=(k == 7))
                    return r
                sc.op("pe", fz, reads=[winA.b, hT.b], writes=[b_.b for b_ in pz])
                yield

                def fzs(e, pz=pz):
                    for j in range(12):
                        e.activation(out=zsv[:, j, 1:129], in_=sl(pz[j // 4], j % 4), func=AF.Identity, bias=pc(PC_BRKV + j), scale=1.0)
                    e.activation(out=zsv[0:64, 12, 1:129], in_=pz[3][0:64, 0:128], func=AF.Identity, bias=pc(PC_BXWXA, 0, 64), scale=1.0)
                    return e.activation(out=zsv[0:96, 13, 1:129], in_=pz[3][0:96, 128:256], func=AF.Identity, bias=pc(PC_BXG, 0, 96), scale=1.0)
                sc.op("act", fzs, reads=[b_.b for b_ in pz] + [pcol.b], writes=[zs.b])
                yield
                sc.op("pool", lambda e: e.tensor_tensor(out=zpv[:, :, :], in0=zsv[:, :, 0:128], in1=zsv[:, :, 1:129], op=ALU.subtract),
                      reads=[zs.b], writes=[zp.b])
                yield

                def fzp(e):
                    for j in range(12):
                        e.scalar_tensor_tensor(out=zpv[:, j, :], in0=zpv[:, j, :], scalar=pc(PC_MURKV + j), in1=zsv[:, j, 1:129], op0=ALU.mult, op1=ALU.add)
                    e.scalar_tensor_tensor(out=zpv[0:64, 12, :], in0=zpv[0:64, 12, :], scalar=pc(PC_MUXWXA, 0, 64), in1=zsv[0:64, 12, 1:129], op0=ALU.mult, op1=ALU.add)
                    return e.scalar_tensor_tensor(out=zpv[0:96, 13, :], in0=zpv[0:96, 13, :], scalar=pc(PC_MUXG, 0, 96), in1=zsv[0:96, 13, 1:129], op0=ALU.mult, op1=ALU.add)
                sc.op("dve", fzp, reads=[zp.b, zs.b, pcol.b], writes=[zp.b])
                yield
                sc.op("pool", lambda e: e.tensor_copy(out=zsv[:, :, 0:1], in_=zsv[:, :, 128:129]), reads=[zs.b], writes=[zs.b])
                yield
                ck("A5")

                def flo(e):
                    e.activation(out=txf[0:32, :], in_=zpv[0:32, 12, :], func=AF.Sigmoid, scale=2.0)
                    e.activation(out=txa[32:64, :], in_=zpv[32:64, 12, :], func=AF.Identity)
                    return e.activation(out=sxg[0:96, :], in_=zpv[0:96, 13, :], func=AF.Sigmoid)
                sc.op("act", flo, reads=[zp.b, txa.b], writes=[txf.b, txa.b, sxg.b])
                yield
                sc.op("pool", lambda e: e.tensor_scalar(out=txa[0:32, :], in0=txf[0:32, :], scalar1=2.0, scalar2=-1.0, op0=ALU.mult, op1=ALU.add),
                      reads=[txf.b, txa.b], writes=[txa.b])
                yield
                ck("A5a")
                pw, pa_, pg = vbank(), vbank(), vbank()

                def flm(e, pw=pw, pa_=pa_, pg=pg):
                    r = None
                    for c in range(4):
                        e.matmul(sl(pw, c), lhsT=lwb[0:32, c * 128:(c + 1) * 128], rhs=txa[0:32, :], start=True, stop=True)
                    for c in range(4):
                        e.matmul(sl(pa_, c), lhsT=lwb[32:64, c * 128:(c + 1) * 128], rhs=txa[32:64, :], start=True, stop=True)
                    for c in range(4):
                        r = e.matmul(sl(pg, c), lhsT=wgb[0:96, c * 128:(c + 1) * 128], rhs=sxg[0:96, :], start=True, stop=True)
                    return r
                sc.op("pe", flm, reads=[lwb.b, wgb.b, txa.b, sxg.b], writes=[pw.b, pa_.b, pg.b])
                yield

                ck("A5b")
                def fsw(e, pw=pw, pa_=pa_, pg=pg):
                    for c in range(4):
                        e.activation(out=sl(sw, c), in_=sl(pw, c), func=AF.Sigmoid, bias=pc(PC_W0 + c), scale=1.0)
                    for c in range(4):
                        e.activation(out=sl(a_, c), in_=sl(pa_, c), func=AF.Sigmoid, bias=pc(PC_A0 + c), scale=1.0)
                    return e.activation(out=gT[:, :], in_=pg[:, :], func=AF.Identity)
                sc.op("act", fsw, reads=[pw.b, pa_.b, pg.b, pcol.b], writes=[sw.b, a_.b, gT.b])
                yield

                ck("A5c")
                def fcum(e):
                    r = None
                    for c in range(4):
                        r = e.tensor_tensor_scan(out=cumv[:, c, 1:129], data0=onesf[:, :], data1=sl(sw, c), initial=0.0, op0=ALU.mult, op1=ALU.add)
                    return r
                sc.op("dve", fcum, reads=[sw.b, onesf.b], writes=[cum.b])
                yield
                ck("A5d")
                sc.op("act", lambda e: e.activation(out=gam[:, :], in_=cum[:, :], func=AF.Exp, scale=-EXPM05), reads=[cum.b], writes=[gam.b])
                yield
                sc.op("act", lambda e: e.activation(out=v3(igam), in_=cumv[:, :, 1:129], func=AF.Exp, scale=EXPM05), reads=[cum.b], writes=[igam.b])
                yield
                ck("A5e")
                sc.op("dve", lambda e: e.tensor_copy(out=gC[:, :], in_=gamv[:, :, 128]), reads=[gam.b], writes=[gC.b])
                yield
                ck("A6")

                def fkq(e):
                    r = None
                    for c in range(4):
                        r = e.activation(out=sl(kq, c), in_=zpv[:, 4 + c, :], func=AF.Square, scale=pc(PC_KK + c))
                    return r
                sc.op("act", fkq, reads=[zp.b, pcol.b], writes=[kq.b])
                yield
                pn = vbank()
                sc.op("pe", lambda e, pn=pn: e.matmul(pn[:, :], lhsT=cst[:, CS_BD4:CS_BD4 + 128], rhs=kq[:, :], start=True, stop=True),
                      reads=[cst.b, kq.b], writes=[pn.b])
                yield
                sc.op("dve", lambda e, pn=pn: e.tensor_scalar(out=nrm[:, :], in0=pn[:, :], scalar1=1e-24, scalar2=None, op0=ALU.max), reads=[pn.b], writes=[nrm.b])
                yield
                sc.op("act", lambda e: e.activation(out=t2[:, :], in_=nrm[:, :], func=AF.Ln), reads=[nrm.b], writes=[t2.b])
                yield
                sc.op("act", lambda e: e.activation(out=inv[:, :], in_=t2[:, :], func=AF.Exp, scale=-0.5), reads=[t2.b], writes=[inv.b])
                yield

                def fkk(e):
                    r = None
                    for c in range(4):
                        r = e.scalar_tensor_tensor(out=sl(kk, c), in0=zpv[:, 4 + c, :], scalar=pc(PC_KK + c), in1=sl(inv, c), op0=ALU.mult, op1=ALU.mult)
                    return r
                sc.op("dve", fkk, reads=[zp.b, pcol.b, inv.b], writes=[kk.b])
                yield

                def ft1(e):
                    r = None
                    for c in range(4):
                        r = e.tensor_scalar(out=sl(t1, c), in0=sl(a_, c), scalar1=-1.0, scalar2=pc(PC_KA + c), op0=ALU.add, op1=ALU.mult)
                    return r
                sc.op("dve", ft1, reads=[a_.b, pcol.b], writes=[t1.b])
                yield
                sc.op("dve", lambda e: e.scalar_tensor_tensor(out=kmod[:, :], in0=t1[:, :], scalar=1.0, in1=zp[:, 512:1024], op0=ALU.add, op1=ALU.mult),
                      reads=[t1.b, zp.b], writes=[kmod.b])
                yield
                ck("A7")
                sc.op("pool", lambda e: e.tensor_tensor(out=RAv[:, :, 1, :], in0=zpv[:, 0:4, :], in1=gamv[:, :, 1:129], op=ALU.mult),
                      reads=[zp.b, gam.b], writes=[RA.b])
                yield
                sc.op("pool", lambda e: e.tensor_tensor(out=RAv[:, :, 0, :], in0=v3(kk), in1=gamv[:, :, 0:128], op=ALU.mult),
                      reads=[kk.b, gam.b, RA.b], writes=[RA.b])
                yield
                sc.op("pool", lambda e: e.tensor_tensor(out=KTf[:, :], in0=kmod[:, :], in1=igam[:, :], op=ALU.mult), reads=[kmod.b, igam.b], writes=[KTf.b])
                yield
                sc.op("pool", lambda e: e.tensor_tensor(out=t2[:, :], in0=kk[:, :], in1=a_[:, :], op=ALU.mult), reads=[kk.b, a_.b], writes=[t2.b])
                yield
                sc.op("dve", lambda e: e.scalar_tensor_tensor(out=BTf[:, :], in0=t2[:, :], scalar=-1.0, in1=igam[:, :], op0=ALU.mult, op1=ALU.mult),
                      reads=[t2.b, igam.b], writes=[BTf.b])
                yield
                sc.op("act", lambda e: e.activation(out=vb[:, :], in_=zp[:, 1024:1536], func=AF.Identity), reads=[zp.b], writes=[vb.b])
                yield

                def frk(e):
                    r = None
                    for c in range(4):
                        r = e.scalar_tensor_tensor(out=sl(rkp, c), in0=zpv[:, c, :], scalar=pc(PC_RK + c), in1=sl(kmod, c), op0=ALU.mult, op1=ALU.mult)
                    return r
                sc.op("dve", frk, reads=[zp.b, pcol.b, kmod.b], writes=[rkp.b])
                yield
                pbo = vbank()
                sc.op("pe", lambda e, pbo=pbo: e.matmul(pbo[:, :], lhsT=cst[:, CS_BD4:CS_BD4 + 128], rhs=rkp[:, :], start=True, stop=True),
                      reads=[cst.b, rkp.b], writes=[pbo.b])
                yield
                sc.op("dve", lambda e, pbo=pbo: e.tensor_tensor(out=bonus[:, :], in0=pbo[:, :], in1=zp[:, 1024:1536], op=ALU.mult),
                      reads=[pbo.b, zp.b], writes=[bonus.b])
                yield
                ck("A8")
                vp16 = vbank()
                p16, p16b = vp16.as16, vp16.b

                def ftr(e, p16=p16):
                    r = None
                    for c in range(4):
                        e.transpose(out=p16[:, c * 128:(c + 1) * 128], in_=sl(KTf, c), identity=identb[:, :])
                    for c in range(4):
                        r = e.transpose(out=p16[:, 512 + c * 128:512 + (c + 1) * 128], in_=sl(BTf, c), identity=identb[:, :])
                    return r
                sc.op("pe", ftr, reads=[KTf.b, BTf.b, identb.b], writes=[p16b])
                yield
                sc.op("act", lambda e, p16=p16: e.activation(out=KBt[:, :], in_=p16[:, 0:1024], func=AF.Identity), reads=[p16b], writes=[KBt.b])
                yield

                def ftv(e, p16=p16):
                    r = None
                    for c in range(4):
                        r = e.transpose(out=p16[:, c * 128:(c + 1) * 128], in_=sl(vb, c), identity=identb[:, :])
                    return r
                sc.op("pe", ftv, reads=[vb.b, identb.b], writes=[p16b])
                yield
                sc.op("dve", lambda e, p16=p16: e.tensor_copy(out=Vt[:, :], in_=p16[:, 0:512]), reads=[p16b], writes=[Vt.b])
                yield

            def tail(i):
                AM = AM_2[i % 2]
                AMv = AM[:, :].rearrange("p (h q t) -> p h q t", h=8, q=4)
                RA, KTf, BTf, KBt, Vt, gT, bonus, gC = RA_2[i % 2], KTf_2[i % 2], BTf_2[i % 2], KBt_2[i % 2], Vt_2[i % 2], gT_2[i % 2], bonus_2[i % 2], gC_2[i % 2]
                RAv = RA[:, :].rearrange("p (c q t) -> p c q t", c=4, q=2)
                ya = yab[i % 2]
                ck("A9")
                for c in range(4):
                    bx, by = vbank(), vbank()

                    def fA(e, c=c, bx=bx, by=by):
                        r = None
                        for (p0, bk) in ((0, bx), (64, by)):
                            rhs = RA[p0:p0 + 64, c * 256:(c + 1) * 256]
                            e.matmul(bk[:, 0:256], lhsT=BTf[p0:p0 + 64, c * 128:(c + 1) * 128], rhs=rhs, start=True, stop=True)
                            r = e.matmul(bk[:, 256:512], lhsT=KTf[p0:p0 + 64, c * 128:(c + 1) * 128], rhs=rhs, start=True, stop=True)
                        return r
                    sc.op("pe", fA, reads=[BTf.b, KTf.b, RA.b], writes=[bx.b, by.b])
                    yield
                    for (h, bk) in ((2 * c, bx), (2 * c + 1, by)):
                        sc.op("act", lambda e, h=h, bk=bk: e.activation(out=AM[:, h * 512:(h + 1) * 512], in_=bk[:, :], func=AF.Identity),
                              reads=[bk.b, AM.b], writes=[AM.b])
                        yield
                        sc.op("pool", lambda e, h=h: e.tensor_tensor(out=AM[:, h * 512:(h + 1) * 512], in0=AM[:, h * 512:(h + 1) * 512],
                                                                    in1=m4b[:, :], op=ALU.mult),
                              reads=[m4b.b, AM.b], writes=[AM.b])
                        yield
                be, bo = vbank(), vbank()

                def fNL(e, be=be, bo=bo):
                    r = None
                    for h in range(8):
                        c, p0 = h // 2, (h % 2) * 64
                        bk = be if h % 2 == 0 else bo
                        r = e.matmul(sl(bk, c), lhsT=RA[p0:p0 + 64, c * 256:c * 256 + 128], rhs=BTf[p0:p0 + 64, c * 128:(c + 1) * 128], start=True, stop=True)
                    return r
                sc.op("pe", fNL, reads=[RA.b, BTf.b], writes=[be.b, bo.b])
                yield
                PT0v = PTp[0][:, :].rearrange("p (c two t) -> p c two t", c=4, two=2)
                ml4 = cst[:, CS_ML4:CS_ML4 + 512].rearrange("p (c t) -> p c t", c=4)
                sc.op("dve", lambda e, be=be: e.tensor_tensor(out=PT0v[:, :, 0, :], in0=v3(be), in1=ml4, op=ALU.mult),
                      reads=[be.b, cst.b, PTp[0].b], writes=[PTp[0].b])
                yield
                sc.op("dve", lambda e, bo=bo: e.tensor_tensor(out=PT0v[:, :, 1, :], in0=v3(bo), in1=ml4, op=ALU.mult),
                      reads=[bo.b, cst.b, PTp[0].b], writes=[PTp[0].b])
                yield
                sc.op("act", lambda e: e.activation(out=v3(Pp[0], 8), in_=AMv[:, :, 0, :], func=AF.Identity), reads=[AM.b], writes=[Pp[0].b])
                yield
                ck("A10")
                pU = vbank()

                KV = os.environ.get("KVAR", "")

                def fU0(e, pU=pU):
                    r = None
                    for c in range(4):
                        if KV != "noS":
                            r = e.matmul(sl(pU, c), lhsT=RA[:, c * 256:c * 256 + 128], rhs=sl(Sbd, c), start=True, stop=(KV == "allstart"))
                        if KV == "noV":
                            continue
                        for h in (2 * c, 2 * c + 1):
                            r = e.matmul(sl(pU, h, 64), lhsT=AM[:, h * 512 + 256:h * 512 + 384], rhs=sl(Vt, h, 64),
                                         start=(KV in ("allstart", "noS")), stop=(h == 2 * c + 1) or KV in ("allstart", "noS"))
                    return r
                sc.op("pe", fU0, reads=[RA.b, Sbd.b, AM.b, Vt.b], writes=[pU.b])
                yield
                if KV == "noevac":
                    ck("A11")
                sc.op("act", lambda e, pU=pU: e.activation(out=Xf[0][:, :], in_=pU[:, :], func=AF.Identity), reads=[pU.b], writes=[Xf[0].b])
                yield
                if KV == "noevac2":
                    ck("A11")
                sc.op("dve", lambda e, pU=pU: e.tensor_copy(out=Xb[0][:, :], in_=pU[:, :]), reads=[pU.b], writes=[Xb[0].b])
                yield
                ck("A11")
                for k in range(7):
                    P, PT = Pp[k], PTp[k % 2]
                    xb_in, xf_in, xb_out, xf_out = Xb[k % 2], Xf[k % 2], Xb[(k + 1) % 2], Xf[(k + 1) % 2]
                    pX = vbank()

                    def fX(e, P=P, xb_in=xb_in, pX=pX):
                        r = None
                        for h in range(8):
                            r = e.matmul(sl(pX, h, 64), lhsT=sl(P, h), rhs=sl(xb_in, h, 64), start=True, stop=True)
                        return r
                    sc.op("pe", fX, reads=[P.b, xb_in.b], writes=[pX.b])
                    yield
                    sc.op("dve", lambda e, pX=pX, xf_in=xf_in, xb_out=xb_out: e.tensor_tensor(out=xb_out[:, :], in0=pX[:, :], in1=xf_in[:, :], op=ALU.add),
                          reads=[pX.b, xf_in.b], writes=[xb_out.b])
                    yield
                    if k < 6:
                        sc.op("dve", lambda e, pX=pX, xf_in=xf_in, xf_out=xf_out: e.tensor_tensor(out=xf_out[:, :], in0=pX[:, :], in1=xf_in[:, :], op=ALU.add),
                              reads=[pX.b, xf_in.b], writes=[xf_out.b])
                        yield
                    if k < 6:
                        Pn = Pp[k + 1]
                        q0, q1 = vbank(), vbank()

                        def fP(e, P=P, PT=PT, q0=q0, q1=q1):
                            r = None
                            for h in range(8):
                                r = e.matmul(sl(q0 if h < 4 else q1, h % 4), lhsT=sl(PT, h), rhs=sl(P, h), start=True, stop=True)
                            return r
                        sc.op("pe", fP, reads=[P.b, PT.b], writes=[q0.b, q1.b])
                        yield
                        sc.op("act", lambda e, Pn=Pn, q0=q0: e.activation(out=Pn[:, 0:512], in_=q0[:, :], func=AF.Identity), reads=[q0.b, Pn.b], writes=[Pn.b])
                        yield
                        sc.op("dve", lambda e, Pn=Pn, q1=q1: e.tensor_copy(out=Pn[:, 512:1024], in_=q1[:, :]), reads=[q1.b, Pn.b], writes=[Pn.b])
                        yield
                    if k < 5:
                        PTn = PTp[(k + 1) % 2]
                        q2, q3 = vbank(), vbank()

                        def fPT(e, P=P, PT=PT, q2=q2, q3=q3):
                            r = None
                            for h in range(8):
                                r = e.matmul(sl(q2 if h < 4 else q3, h % 4), lhsT=sl(P, h), rhs=sl(PT, h), start=True, stop=True)
                            return r
                        sc.op("pe", fPT, reads=[P.b, PT.b], writes=[q2.b, q3.b])
                        yield
                        sc.op("act", lambda e, PTn=PTn, q2=q2: e.activation(out=PTn[:, 0:512], in_=q2[:, :], func=AF.Identity), reads=[q2.b, PTn.b], writes=[PTn.b])
                        yield
                        sc.op("act", lambda e, PTn=PTn, q3=q3: e.activation(out=PTn[:, 512:1024], in_=q3[:, :], func=AF.Identity), reads=[q3.b, PTn.b], writes=[PTn.b])
                        yield
                Ub = Xb[1]
                ck("A12")
                pY = vbank()

                def fY(e, pY=pY):
                    r = None
                    for c in range(4):
                        e.matmul(sl(pY, c), lhsT=RA[:, c * 256 + 128:c * 256 + 256], rhs=sl(Sbd, c), start=True, stop=False)
                        for h in (2 * c, 2 * c + 1):
                            e.matmul(sl(pY, h, 64), lhsT=AM[:, h * 512 + 128:h * 512 + 256], rhs=sl(Ub, h, 64), start=False, stop=False)
                            r = e.matmul(sl(pY, h, 64), lhsT=AM[:, h * 512 + 384:h * 512 + 512], rhs=sl(Vt, h, 64), start=False, stop=(h == 2 * c + 1))
                    return r
                sc.op("pe", fY, reads=[RA.b, Sbd.b, AM.b, Ub.b, Vt.b], writes=[pY.b])
                yield
                pS2 = vbank()

                def fS(e, pS2=pS2):
                    r = None
                    for c in range(4):
                        e.matmul(sl(pS2, c), lhsT=KBt[:, 512 + c * 128:512 + (c + 1) * 128], rhs=sl(Ub, c), start=True, stop=False)
                        r = e.matmul(sl(pS2, c), lhsT=sl(KBt, c), rhs=sl(Vt, c), start=False, stop=True)
                    return r
                sc.op("pe", fS, reads=[KBt.b, Ub.b, Vt.b], writes=[pS2.b])
                yield
                sc.op("dve", lambda e, pS2=pS2: e.tensor_tensor(out=st1[:, :], in0=pS2[:, :], in1=cst[:, CS_BD4:CS_BD4 + 512], op=ALU.mult),
                      reads=[pS2.b, cst.b], writes=[st1.b])
                yield
                sc.op("pool", lambda e: e.tensor_tensor(out=st2[:, :], in0=st1[:, :], in1=Sf[:, :], op=ALU.add), reads=[st1.b, Sf.b], writes=[st2.b])
                yield

                def fSf(e):
                    r = None
                    for c in range(4):
                        r = e.tensor_scalar(out=sl(Sf, c), in0=sl(st2, c), scalar1=gC[:, c:c + 1], scalar2=None, op0=ALU.mult)
                    return r
                sc.op("dve", fSf, reads=[st2.b, gC.b], writes=[Sf.b])
                yield
                sc.op("act", lambda e: e.activation(out=Sbd[:, :], in_=Sf[:, :], func=AF.Identity), reads=[Sf.b], writes=[Sbd.b])
                yield
                ck("A13")

                def fgs(e, pY=pY):
                    r = None
                    for h in range(8):
                        r = e.bn_stats(out=gst[:, h * 6:h * 6 + 6], in_=sl(pY, h, 64))
                    return r
                sc.op("dve", fgs, reads=[pY.b], writes=[gst.b])
                yield

                def fga(e):
                    r = None
                    for h in range(8):
                        r = e.bn_aggr(out=gmv[:, 2 * h:2 * h + 2], in_=gst[:, h * 6:h * 6 + 6])
                    return r
                sc.op("dve", fga, reads=[gst.b], writes=[gmv.b])
                yield
                gmvv = gmv[:, :].rearrange("p (h two) -> p h two", two=2)
                sc.op("pool", lambda e: e.tensor_scalar(out=gsd[:, :], in0=gmvv[:, :, 1], scalar1=GN_EPS, scalar2=None, op0=ALU.add),
                      reads=[gmv.b], writes=[gsd.b])
                yield
                sc.op("pool", lambda e: e.tensor_tensor(out=grs[:, :], in0=gsd[:, :], in1=negh[:, :], op=ALU.pow), reads=[gsd.b, negh.b], writes=[grs.b])
                yield

                def fyn(e, pY=pY):
                    r = None
                    for h in range(8):
                        r = e.tensor_scalar(out=sl(ynb, h, 64), in0=sl(pY, h, 64), scalar1=gmv[:, 2 * h:2 * h + 1], scalar2=grs[:, h:h + 1],
                                            op0=ALU.subtract, op1=ALU.mult)
                    return r
                sc.op("dve", fyn, reads=[pY.b, gmv.b, grs.b], writes=[ynb.b])
                yield
                vp16 = vbank()
                p16, p16b = vp16.as16, vp16.b

                def fty(e, p16=p16):
                    r = None
                    for c in range(4):
                        r = e.transpose(out=p16[:, c * 128:(c + 1) * 128], in_=sl(ynb, c), identity=identb[:, :])
                    return r
                sc.op("pe", fty, reads=[ynb.b, identb.b], writes=[p16b])
                yield

                def fyt(e, p16=p16):
                    r = None
                    for c in range(4):
                        r = e.tensor_scalar(out=sl(yt, c), in0=p16[:, c * 128:(c + 1) * 128], scalar1=pc(PC_GNG + c), scalar2=pc(PC_GNB + c),
                                            op0=ALU.mult, op1=ALU.add)
                    return r
                sc.op("dve", fyt, reads=[p16b, pcol.b], writes=[yt.b])
                yield
                sc.op("pool", lambda e: e.tensor_tensor(out=yt2[:, :], in0=yt[:, :], in1=bonus[:, :], op=ALU.add), reads=[yt.b, bonus.b], writes=[yt2.b])
                yield
                sc.op("pool", lambda e, ya=ya: e.tensor_tensor(out=ya[:, 512:1024], in0=yt2[:, :], in1=gT[:, :], op=ALU.mult),
                      reads=[yt2.b, gT.b, ya.b], writes=[ya.b])
                yield
                sc.dma(yab_d[i * 128:(i + 1) * 128, :], ya[:, :], ya.b, reads=[ya.b], writes=[yab_db[i]])
                yield
                yield

            def rr(gens, steps):
                gens = list(gens)
                steps = list(steps)
                while gens:
                    for j in range(len(gens) - 1, -1, -1):
                        for _ in range(steps[j]):
                            try:
                                next(gens[j])
                            except StopIteration:
                                gens.pop(j)
                                steps.pop(j)
                                break
            load_x(0)
            if NT > 1:
                load_x(1)
            rr([front_h(0)], [1])
            rr([front_g(0), front_r(0)], [1, 2])
            for i in range(NT):
                gens, steps = [tail(i)], [1]
                if i + 1 < NT:
                    rr([front_h(i + 1)], [1])
                    if i + 2 < NT:
                        load_x(i + 2)
                    gens += [front_g(i + 1), front_r(i + 1)]
                    steps += [1, 2]
                rr(gens, steps)
            sc.barrier()
            sc.flush()
        if os.environ.get("KSTOP") == "A":
            return nc

        def bcast_rows(st, tmp, name, col0, dg=None, of_=None):
            tl = st
            dg = dg if dg is not None else mkT(tmp, name + "_dg", [128, D])
            ofT = of_ if of_ is not None else mkT(tmp, name + "_on", [128, 128])

            class _OF:
                b = ofT.b

                def __getitem__(self, idx):
                    return ofT[:, 0:128]
            of = _OF()
            sc.op("pool", lambda e: e.memset(of[:, :], 1.0), writes=[of.b])

            def fdg(e):
                r = None
                for m in range(8):
                    r = e.tensor_scalar(out=dg[:, m * 128:(m + 1) * 128], in0=ident, scalar1=md(col0 + m), scalar2=None, op0=ALU.mult)
                return r
            sc.op("dve", fdg, reads=[cst.b, modT.b], writes=[dg.b])
            q0, q1 = vbank(), vbank()

            def fbc(e):
                r = None
                for m in range(8):
                    r = e.matmul((q0 if m < 4 else q1)[:, (m % 4) * 128:(m % 4 + 1) * 128], lhsT=of[:, :], rhs=dg[:, m * 128:(m + 1) * 128], start=True, stop=True)
                return r
            sc.op("pe", fbc, reads=[of.b, dg.b], writes=[q0.b, q1.b])
            sc.op("dve", lambda e: e.tensor_copy(out=tl[:, 0:512], in_=q0[:, :]), reads=[q0.b], writes=[tl.b])
            sc.op("dve", lambda e: e.tensor_copy(out=tl[:, 512:1024], in_=q1[:, :]), reads=[q1.b, tl.b], writes=[tl.b])
            return tl

        def ln_rows(st, name, r):
            tl = st
            sc.dma(tl[:, :], lnrow_d[r:r + 1, :].partition_broadcast(128), tl.b, writes=[tl.b])
            return tl

        kpf = lambda d_: d_.rearrange("(k p) f -> p k f", p=128)
        sc.prio = "old"
        sc.reserve = 8
        NS = NT // 2
        W2 = 256
        sl = lambda t_, c, w=128: t_[:, c * w:(c + 1) * w]

        NPRE = 5
        w1pre = mkT(G, "w1pre", [128, NPRE * 4096], BF16)
        with contextlib.ExitStack() as PB:
            winG = mkT(PB, "winG", [128, 8 * 2048], BF16)
            wA = mkT(PB, "wA", [128, 4 * 1024], BF16)
            wB = mkT(PB, "wB", [128, 4 * 1024], BF16)
            wO = mkT(PB, "wO", [128, 8 * 1024], BF16)
            gt1bc = mkT(PB, "gt1bc", [128, D])
            ln1g = mkT(PB, "ln1g", [128, D])
            ln1b = mkT(PB, "ln1b", [128, D])
            xt = [mkT(PB, "xtB%d" % i, [128, 2 * D]) for i in range(2)]
            yin = [mkT(PB, "yin%d" % i, [128, 2 * D], BF16) for i in range(2)]
            hT = mkT(PB, "hTB", [128, 8 * W2], BF16)
            sigG = mkT(PB, "sigG", [128, 8 * W2])
            tAB = [mkT(PB, "tAB%d" % i, [128, 8 * W2]) for i in range(2)]
            mg = mkT(PB, "mg", [128, 8 * W2], BF16)
            tM = mkT(PB, "tM", [128, D])
            res = mkT(PB, "res", [128, D])
            h1o = [mkT(PB, "h1o%d" % i, [128, D]) for i in range(2)]
            smb = [mkT(PB, "smB%d" % i, [128, 16]) for i in range(4)]
            load_w_cast(winG, kpf(win_d)[:, :, 2720:4768], 8, 2048)
            load_w_cast(wA, kpf(wa_d), 4, 1024)
            load_w_cast(wB, kpf(wb_d), 4, 1024)
            load_w_cast(wO, kpf(wout_d), 8, 1024)
            bcast_rows(gt1bc, None, "gt1bc", GT1, dg=h1o[0], of_=h1o[1])
            ln_rows(ln1g, "ln1g", 0)
            ln_rows(ln1b, "ln1b", 1)
            xv = lambda t_: t_[:, :].rearrange("p (s d) -> p s d", s=2)

            def load_b(s_):
                sc.dma(xv(xt[s_ % 2]), x_d[s_ * 256:(s_ + 1) * 256, :].rearrange("(s p) d -> p s d", p=128), xt[s_ % 2].b, writes=[xt[s_ % 2].b])
                sc.dma(xv(yin[s_ % 2]), yab_d[s_ * 256:(s_ + 1) * 256, :].rearrange("(s p) d -> p s d", p=128), yin[s_ % 2].b,
                       reads=[yab_db[2 * s_], yab_db[2 * s_ + 1]], writes=[yin[s_ % 2].b])

            def hT_b(s_):
                for sub in range(2):
                    x_to_hT(None, xt[s_ % 2], xt[s_ % 2].b, hT, 0, OPS1, SH1, tok0=sub * 128, col0=sub * D)
            load_b(0)
            hT_b(0)
            load_w_cast(w1pre, kpf(wff1_d)[:, 0:NPRE, :], NPRE, 4096)
            for s_ in range(NS):
                xs, yi = xt[s_ % 2], yin[s_ % 2]
                if s_ + 1 < NS:
                    load_b(s_ + 1)
                hk = lambda k: hT[:, k * W2:(k + 1) * W2]
                for br, (w_, yoff) in enumerate(((wA, 0), (wB, 4))):
                    pgs = [vbank() for _ in range(4)]

                    def fg(e, br=br, pgs=pgs):
                        r = None
                        for j in range(8):
                            for k in range(8):
                                c0 = k * 2048 + (br * 8 + j) * 128
                                r = e.matmul(sl(pgs[j // 2], j % 2, W2), lhsT=winG[:, c0:c0 + 128], rhs=hk(k), start=(k == 0), stop=(k == 7))
                        return r
                    sc.op("pe", fg, reads=[winG.b, hT.b], writes=[p_.b for p_ in pgs])
                    for q in range(4):
                        def fsg(e, br=br, q=q, pgs=pgs):
                            r = None
                            for jj in range(2):
                                j = q * 2 + jj
                                r = e.activation(out=sl(sigG, j, W2), in_=sl(pgs[q], jj, W2), func=AF.Sigmoid, bias=pc(PC_BGATE + br * 8 + j), scale=1.0)
                            return r
                        sc.op("act", fsg, reads=[pgs[q].b, pcol.b, sigG.b], writes=[sigG.b])
                    pbs = [vbank() for _ in range(4)]

                    def fbr(e, w_=w_, yoff=yoff, pbs=pbs, yi=yi):
                        r = None
                        for dch in range(8):
                            for k in range(4):
                                for sub in range(2):
                                    r = e.matmul(pbs[dch // 2][:, (dch % 2) * W2 + sub * 128:(dch % 2) * W2 + (sub + 1) * 128],
                                                 lhsT=w_[:, k * 1024 + dch * 128:k * 1024 + (dch + 1) * 128],
                                                 rhs=yi[:, sub * D + (yoff + k) * 128:sub * D + (yoff + k + 1) * 128], start=(k == 0), stop=(k == 3))
                        return r

                    def fbr2(e, w_=w_, yoff=yoff, pbs=pbs, yi=yi):
                        r = None
                        for dch in range(8):
                            for sub in range(2):
                                for k in range(4):
                                    r = e.matmul(pbs[dch // 2][:, (dch % 2) * W2 + sub * 128:(dch % 2) * W2 + (sub + 1) * 128],
                                                 lhsT=w_[:, k * 1024 + dch * 128:k * 1024 + (dch + 1) * 128],
                                                 rhs=yi[:, sub * D + (yoff + k) * 128:sub * D + (yoff + k + 1) * 128], start=(k == 0), stop=(k == 3))
                        return r
                    sc.op("pe", fbr2, reads=[w_.b, yi.b], writes=[p_.b for p_ in pbs])
                    dst = tAB[br]
                    for q in range(4):
                        sc.op("dve", lambda e, dst=dst, q=q, pbs=pbs: e.tensor_tensor(out=sl(dst, q, 512), in0=pbs[q][:, :], in1=sl(sigG, q, 512), op=ALU.mult),
                              reads=[pbs[q].b, sigG.b, dst.b], writes=[dst.b])
                sc.op("pool", lambda e: e.tensor_tensor(out=mg[:, :], in0=tAB[0][:, :], in1=tAB[1][:, :], op=ALU.add), reads=[tAB[0].b, tAB[1].b], writes=[mg.b])
                if s_ + 1 < NS:
                    hT_b(s_ + 1)
                for sub in range(2):
                    i = 2 * s_ + sub
                    ho = h1o[i % 2]
                    qm = [vbank(), vbank()]

                    def fmx(e, sub=sub, qm=qm):
                        r = None
                        for hf in range(2):
                            for k in range(8):
                                e.matmul(qm[hf][:, :], lhsT=mg[:, k * W2 + sub * 128:k * W2 + (sub + 1) * 128],
                                         rhs=wO[:, k * 1024 + hf * 512:k * 1024 + (hf + 1) * 512], start=(k == 0), stop=False)
                            r = e.matmul(qm[hf][:, :], lhsT=onesb[0:1, 0:128], rhs=browb[0:1, 512 + hf * 512:512 + (hf + 1) * 512], start=False, stop=True)
                        return r
                    sc.op("pe", fmx, reads=[mg.b, wO.b, onesb.b, browb.b], writes=[qm[0].b, qm[1].b])
                    for hf in range(2):
                        sc.op("dve", lambda e, hf=hf, qm=qm: e.tensor_tensor(out=tM[:, hf * 512:(hf + 1) * 512], in0=qm[hf][:, :], in1=gt1bc[:, hf * 512:(hf + 1) * 512], op=ALU.mult),
                              reads=[qm[hf].b, gt1bc.b, tM.b], writes=[tM.b])
                    sc.op("dve", lambda e, xs=xs, sub=sub: e.scalar_tensor_tensor(out=res[:, :], in0=xs[:, sub * D:(sub + 1) * D], scalar=ALPHA, in1=tM[:, :], op0=ALU.mult, op1=ALU.add),
                          reads=[xs.b, tM.b], writes=[res.b])
                    layernorm_rows(res, tM, ln1g, ln1b, smb, ho)
                    sc.dma(h1_d[i * 128:(i + 1) * 128, :], ho[:, :], ho.b, reads=[ho.b], writes=[h1_db[i]])
            sc.barrier()
            sc.flush()
        if os.environ.get("KSTOP") == "B":
            return nc

        with contextlib.ExitStack() as PC:
            w1q = [mkT(PC, "w1q%d" % q_, [128, (8 - NPRE) * 1024], BF16) for q_ in range(4)]
            w2 = mkT(PC, "w2", [128, 32 * 1024], BF16)
            gt2bc = mkT(PC, "gt2bc", [128, D])
            ln2g = mkT(PC, "ln2g", [128, D])
            ln2b = mkT(PC, "ln2b", [128, D])
            hin = [mkT(PC, "hin%d" % i, [128, 2 * D]) for i in range(2)]
            hT2 = mkT(PC, "hT2", [128, 8 * W2], BF16)
            rl = [mkT(PC, "rl%d" % i, [128, 512]) for i in range(2)]
            hid = mkT(PC, "hid", [128, 32 * W2], BF16)
            tC = mkT(PC, "tC", [128, D])
            oo = [mkT(PC, "oo%d" % i, [128, D]) for i in range(2)]
            smc = [mkT(PC, "smC%d" % i, [128, 16]) for i in range(4)]
            for q_ in range(4):
                for kk_ in range(8 - NPRE):
                    sc.dma(w1q[q_][:, kk_ * 1024:(kk_ + 1) * 1024], kpf(wff1_d)[:, NPRE + kk_, q_ * 1024:(q_ + 1) * 1024], w1q[q_].b,
                           writes=[w1q[q_].b] if kk_ == 8 - NPRE - 1 else [], q="pool", cost=2.0)
            load_w_cast(w2, kpf(wff2_d), 32, 1024)
            bcast_rows(gt2bc, None, "gt2bc", GT2, dg=oo[0], of_=oo[1])
            ln_rows(ln2g, "ln2g", 2)
            ln_rows(ln2b, "ln2b", 3)
            xv = lambda t_: t_[:, :].rearrange("p (s d) -> p s d", s=2)

            def load_c(s_):
                sc.dma(xv(hin[s_ % 2]), h1_d[s_ * 256:(s_ + 1) * 256, :].rearrange("(s p) d -> p s d", p=128), hin[s_ % 2].b,
                       reads=[h1_db[2 * s_], h1_db[2 * s_ + 1]], writes=[hin[s_ % 2].b])

            def hT_c(s_):
                for sub in range(2):
                    x_to_hT(None, hin[s_ % 2], hin[s_ % 2].b, hT2, 0, OPS2, SH2, tok0=sub * 128, col0=sub * D)
            load_c(0)
            if NS > 1:
                load_c(1)
            hT_c(0)
            for s_ in range(NS):
                hs = hin[s_ % 2]
                hk = lambda k: hT2[:, k * W2:(k + 1) * W2]
                for fq in range(16):
                    pf = vbank()
                    rr_ = rl[fq % 2]

                    def ff1(e, fq=fq, pf=pf):
                        r = None
                        for j in range(2):
                            f_ = fq * 2 + j
                            for k in range(8):
                                if k < NPRE:
                                    lw_ = w1pre[:, k * 4096 + f_ * 128:k * 4096 + (f_ + 1) * 128]
                                else:
                                    c_ = (k - NPRE) * 1024 + (f_ % 8) * 128
                                    lw_ = w1q[f_ // 8][:, c_:c_ + 128]
                                r = e.matmul(sl(pf, j, W2), lhsT=lw_, rhs=hk(k), start=(k == 0), stop=(k == 7))
                        return r
                    sc.op("pe", ff1, reads=[w1pre.b, w1q[fq // 4].b, hT2.b], writes=[pf.b])

                    def frl(e, fq=fq, pf=pf, rr_=rr_):
                        r = None
                        for j in range(2):
                            r = e.activation(out=sl(rr_, j, W2), in_=sl(pf, j, W2), func=AF.Relu, bias=pc(PC_BFF1 + fq * 2 + j), scale=1.0)
                        return r
                    sc.op("act", frl, reads=[pf.b, pcol.b], writes=[rr_.b])
                    sc.op("pool", lambda e, fq=fq, rr_=rr_: e.tensor_tensor(out=hid[:, fq * 512:(fq + 1) * 512], in0=rr_[:, :], in1=rr_[:, :], op=ALU.mult),
                          reads=[rr_.b, hid.b], writes=[hid.b])
                if s_ + 1 < NS:
                    hT_c(s_ + 1)
                for sub in range(2):
                    i = 2 * s_ + sub
                    ot = oo[i % 2]
                    qo = [vbank(), vbank()]

                    def ff2(e, sub=sub, qo=qo):
                        r = None
                        for hf in range(2):
                            for f_ in range(32):
                                e.matmul(qo[hf][:, :], lhsT=hid[:, f_ * W2 + sub * 128:f_ * W2 + (sub + 1) * 128],
                                         rhs=w2[:, f_ * 1024 + hf * 512:f_ * 1024 + (hf + 1) * 512], start=(f_ == 0), stop=False)
                            r = e.matmul(qo[hf][:, :], lhsT=onesb[0:1, 0:128], rhs=browb[0:1, 1536 + hf * 512:1536 + (hf + 1) * 512], start=False, stop=True)
                        return r
                    sc.op("pe", ff2, reads=[hid.b, w2.b, onesb.b, browb.b], writes=[qo[0].b, qo[1].b])
                    for hf in range(2):
                        sc.op("dve", lambda e, hf=hf, qo=qo: e.tensor_tensor(out=tC[:, hf * 512:(hf + 1) * 512], in0=qo[hf][:, :], in1=gt2bc[:, hf * 512:(hf + 1) * 512], op=ALU.mult),
                              reads=[qo[hf].b, gt2bc.b, tC.b], writes=[tC.b])
                    sc.op("dve", lambda e, hs=hs, sub=sub: e.scalar_tensor_tensor(out=hs[:, sub * D:(sub + 1) * D], in0=hs[:, sub * D:(sub + 1) * D], scalar=ALPHA, in1=tC[:, :], op0=ALU.mult, op1=ALU.add),
                          reads=[hs.b, tC.b], writes=[hs.b])
                    layernorm_rows(hs, tC, ln2g, ln2b, smc, ot, c0=sub * D)
                    sc.dma(out_d[i * 128:(i + 1) * 128, :], ot[:, :], ot.b, reads=[ot.b], writes=[])
                if s_ + 2 < NS:
                    load_c(s_ + 2)
            sc.barrier()
            sc.flush()
            nc.all_engine_barrier()
    return nc


def build(S, n_tiles_c=2):
    box = []
    try:
        return _build(S, n_tiles_c, box)
    except StopBuild:
        return box[0]


def host_prep(inputs):
    f = lambda a: np.ascontiguousarray(np.asarray(a, np.float32))
    cols = lambda v, n: f(v).reshape(n, 128).T
    b_in, mu = f(inputs["b_in"][0]), f(inputs["mu_shift"][0])
    pcol = np.zeros((128, NPC), np.float32)
    pcol[:, PC_BU:PC_BU + 4] = cols(b_in[0:512], 4)
    pcol[:, PC_BRKV:PC_BRKV + 12] = cols(b_in[1024:2560], 12)
    pcol[0:64, PC_BXWXA] = b_in[2560:2624]
    pcol[0:96, PC_BXG] = b_in[2624:2720]
    pcol[:, PC_BGATE:PC_BGATE + 16] = cols(b_in[2720:4768], 16)
    pcol[:, PC_MURKV:PC_MURKV + 12] = cols(mu[0:1536], 12)
    pcol[0:64, PC_MUXWXA] = mu[1536:1600]
    pcol[0:96, PC_MUXG] = mu[1600:1696]
    for nm, c0 in (("w0", PC_W0), ("a0", PC_A0), ("k_k", PC_KK), ("k_a", PC_KA), ("r_k", PC_RK), ("gn_gain", PC_GNG),
                   ("gn_bias", PC_GNB), ("g_ln_v", PC_GLN), ("b_ln_v", PC_BLN)):
        pcol[:, c0:c0 + 4] = cols(f(inputs[nm][0]).reshape(-1), 4)
    pcol[:, PC_BFF1:PC_BFF1 + 32] = cols(inputs["b_ff1"][0], 32)
    pcol[:, PC_BADA:PC_BADA + 48] = cols(inputs["b_ada"][0], 48)
    s = np.arange(128)
    strict = (s[:, None] < s[None, :]).astype(np.float32)
    incl = (s[:, None] <= s[None, :]).astype(np.float32)
    low = (s[:, None] > s[None, :]).astype(np.float32)
    bd = ((s[:, None] // 64) == (s[None, :] // 64)).astype(np.float32)
    cst = np.zeros((128, NCS), np.float32)
    cst[:, CS_ID:CS_ID + 128] = np.eye(128, dtype=np.float32)
    cst[:, CS_M4:CS_M4 + 512] = np.concatenate([strict, incl, strict, incl], axis=1)
    cst[:, CS_ML4:CS_ML4 + 512] = np.concatenate([low] * 4, axis=1)
    cst[:, CS_BD4:CS_BD4 + 512] = np.concatenate([bd] * 4, axis=1)
    cst[:, CS_ONE] = 1.0
    cst[:, CS_ONE + 1] = LN_EPS
    cst[:, CS_ONE + 2] = GN_EPS
    brow = np.concatenate([b_in[512:1024], f(inputs["b_out"][0]), f(inputs["b_ff2"][0])])[None, :]
    lnrows = np.stack([f(inputs["ln1_g"][0]), f(inputs["ln1_b"][0]), f(inputs["ln2_g"][0]), f(inputs["ln2_b"][0])])
    wsT = f(f(inputs["w_spatial"][0]).transpose(2, 0, 1).reshape(128, 1024))
    bsp = f(inputs["b_spatial"][0]).reshape(4, 2, 128)
    bspb = f(np.repeat(bsp, 64, axis=1).transpose(1, 0, 2).reshape(128, 512))
    lw = f(np.concatenate([f(inputs["w_decay_up"][0]), f(inputs["w_aaa_up"][0])], axis=0))
    shared = {
        "w_ada": f(inputs["w_ada"][0]), "w_in": f(inputs["w_in"][0]), "pcol": pcol, "cst": cst, "brow": f(brow),
        "lnrows": f(lnrows), "wsT": wsT, "bspb": bspb, "lw": lw, "wg": f(inputs["w_gate_up"][0]),
        "w_branch_a": f(inputs["w_branch_a"][0]), "w_branch_b": f(inputs["w_branch_b"][0]), "w_out": f(inputs["w_out"][0]),
        "w_ff1": f(inputs["w_ff1"][0]), "w_ff2": f(inputs["w_ff2"][0]),
    }
    x, c = np.asarray(inputs["x"], np.float32), f(inputs["c"])
    maps = []
    for b in range(x.shape[0]):
        m = dict(shared)
        m["x"] = np.ascontiguousarray(x[b])
        m["ccol"] = f(np.repeat(c[b].reshape(8, 128).T, 2, axis=1))
        maps.append(m)
    return maps


def kernel(**inputs):
    x = np.asarray(inputs["x"])
    B, S, _ = x.shape
    maps = host_prep(inputs)
    nc = build(S)
    res = run_bass_kernel_spmd(nc, maps, core_ids=list(range(B)))
    return np.stack([np.asarray(r["out"]) for r in res.results], axis=0).astype(np.float32)
```

```python
import contextlib
import os
import numpy as np
import concourse.bass as bass
import concourse.mybir as mybir
from concourse.bass_utils import run_bass_kernel_spmd

F32 = mybir.dt.float32
BF16 = mybir.dt.bfloat16
AF = mybir.ActivationFunctionType
ALU = mybir.AluOpType

D = 1024
NCORES = 8
ALPHA = 2.0 ** 0.25
LN_EPS = 1e-5
GN_EPS = 64e-5
EXPM05 = 0.6065306597126334
IN_COLS = 4768
PC_BU, PC_BRKV, PC_BXWXA, PC_BXG, PC_BGATE = 0, 4, 16, 17, 18
PC_MURKV, PC_MUXWXA, PC_MUXG = 34, 46, 47
PC_W0, PC_A0, PC_KK, PC_KA, PC_RK, PC_GNG, PC_GNB, PC_GLN, PC_BLN = 48, 52, 56, 60, 64, 68, 72, 76, 80
PC_BFF1, PC_BADA, NPC = 84, 116, 164
CS_ID, CS_M4, CS_ML4, CS_BD4, CS_ONE, NCS = 0, 128, 640, 1152, 1664, 1792


class StopBuild(Exception):
    pass


class Buf:
    __slots__ = ("name", "writer", "readers", "dsem", "dcount", "ro", "vp")

    def __init__(self, name):
        self.name = name
        self.writer = None
        self.readers = []
        self.dsem = None
        self.dcount = 0
        self.ro = False
        self.vp = None


class Op:
    __slots__ = ("eng", "fn", "deps", "cost", "rid", "epoch", "idx", "dma", "succ", "npend", "ready", "fin", "vps")

    def __init__(self, eng, fn, deps, cost, rid, epoch, dma=None):
        self.eng = eng
        self.fn = fn
        self.deps = deps
        self.cost = cost
        self.rid = rid
        self.epoch = epoch
        self.idx = None
        self.dma = dma
        self.succ = []
        self.npend = 0
        self.ready = 0.0
        self.fin = 0.0
        self.vps = ()


class VP:
    class _V16:
        def __init__(self, vp):
            self.vp = vp

        def __getitem__(self, idx):
            return self.vp.sched.ps16_real[self.vp._k()][idx]

    def __init__(self, sched, n):
        self.sched = sched
        self.b = Buf("vps%d" % n)
        self.b.vp = self
        self.bank = None
        self.remaining = 0
        self.ops = []
        self.as16 = VP._V16(self)

    probing = False

    def _k(self):
        if self.bank is None:
            assert VP.probing, "virtual PSUM bank used before scheduling"
            return 0
        return self.bank

    def __getitem__(self, idx):
        return self.sched.ps_real[self._k()][idx]


DEF_COST = {"pe": 0.6, "act": 0.6, "dve": 0.6, "pool": 1.3, "sp": 3.0}


def _fsize(ap):
    try:
        v = ap.free_size
        v = v() if callable(v) else v
        return int(v)
    except Exception:
        try:
            n = 1
            for d in list(ap.shape)[1:]:
                n *= int(d)
            return n
        except Exception:
            return 512


class _CostProbe:
    def __init__(self, eng):
        self.eng = eng
        self.t = 0.0

    def __getattr__(self, name):
        def f(*a, **kw):
            try:
                if self.eng == "pe":
                    if name == "transpose":
                        self.t += 0.1
                    else:
                        rhs = kw.get("rhs", a[2] if len(a) > 2 else None)
                        n = _fsize(rhs)
                        fp32 = 4.0 if str(getattr(rhs, "dtype", "")).endswith("float32") else 1.0
                        self.t += (0.036 + 0.00036 * max(n, 64)) * fp32
                else:
                    src = kw.get("in_", kw.get("in0", kw.get("data1", kw.get("ap", a[0] if a else None))))
                    n = _fsize(src)
                    if self.eng == "pool":
                        self.t += 0.3 + 0.0019 * n
                    else:
                        k = 0.0021 if name in ("tensor_tensor_scan",) else (0.0064 if name == "reciprocal" else 0.00105)
                        self.t += 0.2 + k * n
            except Exception:
                self.t += DEF_COST.get(self.eng, 0.6)
            return None
        return f


class Sched:
    def __init__(self, nc, stack):
        self.nc = nc
        self.stack = stack
        self.E = {"pe": nc.tensor, "act": nc.scalar, "dve": nc.vector, "pool": nc.gpsimd, "sp": nc.sync}
        self.cnt = {k: 0 for k in self.E}
        self.sem = {k: nc.alloc_semaphore("sem_" + k) for k in ("pe", "act", "dve", "pool")}
        self.waited = {k: {} for k in self.E}
        self.dsems = []
        self.dpool = [nc.alloc_semaphore("dsem%d" % i) for i in range(56)]
        for sm in list(self.sem.values()) + self.dpool:
            nc.gpsimd.sem_clear(sm)
        nc.all_engine_barrier()
        self.ops = []
        self.epoch = 0
        self.rid = 0
        self.prio = "cp"
        self.reserve = 4
        self.bank_free = [0.0] * 8
        self.bank_last = [[] for _ in range(8)]
        self.nvp = 0
        self.ps_real = None
        self.ps16_real = None

    def vbank(self):
        self.nvp += 1
        return VP(self, self.nvp)

    def _deps(self, reads, writes):
        d = []
        self._kinds = {}
        for b in reads:
            if b.writer is not None and b.writer.epoch == self.epoch:
                d.append(b.writer)
                self._kinds[id(b.writer)] = "raw:" + b.name
        nowar = os.environ.get("KSCHED_NOWAR")
        for b in writes:
            if nowar and ((nowar == "1" and not b.name.startswith("ps")) or any(b.name.startswith(p) for p in nowar.split(","))):
                continue
            if b.writer is not None and b.writer.epoch == self.epoch:
                d.append(b.writer)
                self._kinds.setdefault(id(b.writer), "waw:" + b.name)
            for r in b.readers:
                if r.epoch == self.epoch:
                    d.append(r)
                    self._kinds.setdefault(id(r), "war:" + b.name)
        return d

    def _mark(self, op, reads, writes):
        for b in reads:
            if not b.ro:
                b.readers.append(op)
        for b in writes:
            b.writer = op
            b.readers = []

    def op(self, eng, fn, reads=(), writes=(), cost=None):
        ex = [b for b in reads if b.vp is not None]
        if ex:
            reads = [b for b in reads if b.vp is None]
            writes = list(writes) + [b for b in ex if b not in writes]
        self.rid += 1
        if cost is None:
            pr = _CostProbe(eng)
            VP.probing = True
            try:
                fn(pr)
                cost = max(pr.t, 0.1)
            except Exception:
                cost = DEF_COST[eng]
            VP.probing = False
        o = Op(eng, fn, self._deps(reads, writes), cost, self.rid, self.epoch)
        o.succ = self._kinds
        o.vps = list({id(b.vp): b.vp for b in list(reads) + list(writes) if b.vp is not None}.values())
        for vp in o.vps:
            vp.remaining += 1
        self.ops.append(o)
        self._mark(o, reads, writes)

    def dma(self, out, in_, owner, reads=(), writes=(), q="sp", cost=None):
        if owner.dsem is None:
            owner.dsem = self.dpool.pop()
            self.dsems.append(owner)
        owner.dcount += 16
        self.rid += 1
        o = Op(q, None, self._deps(reads, writes), cost if cost is not None else DEF_COST["sp"], self.rid, self.epoch,
               dma=(out, in_, owner.dsem, owner.dcount))
        self.ops.append(o)
        self._mark(o, reads, writes)

    def barrier(self):
        pass

    def _schedule(self):
        ops = self.ops
        kinds = {}
        for o in ops:
            if isinstance(o.succ, dict):
                kinds[id(o)] = o.succ
            o.succ = []
        for o in ops:
            o.deps = list({id(d): d for d in o.deps}.values())
            o.npend = len(o.deps)
            o.ready = 0.0
            for d in o.deps:
                d.succ.append(o)
        prio_mode = os.environ.get("KSCHED_PRIO", self.prio)
        for o in reversed(ops):
            o.fin = o.cost + max([s_.fin + 0.25 for s_ in o.succ] + [0.0])
        tailp = {id(o): (o.fin if prio_mode == "cp" else 0.0) for o in ops}
        if os.environ.get("KSCHED_DBG"):
            print("sched: critical path (infinite engines) = %.1f us" % max([o.fin for o in ops] + [0.0]))
        if os.environ.get("KSCHED_DCP") and len(ops) > 500:
            o = max(ops, key=lambda o: o.fin)
            cnt = {}
            while o.succ:
                nx = max(o.succ, key=lambda s_: s_.fin)
                kd = kinds.get(id(nx), {}).get(id(o), "?")
                if not kd.startswith("raw"):
                    cnt[kd] = cnt.get(kd, 0) + 1
                o = nx
            print("sched: non-RAW edges on dependency critical path:", sorted(cnt.items(), key=lambda kv: -kv[1])[:25])
        order = {k: [] for k in self.E}
        free_at = {k: 0.0 for k in self.E}
        rel = {k: [] for k in self.E}
        spq = [o for o in ops if o.eng == "sp"]
        sp_next = 0
        for o in ops:
            if o.npend == 0 and o.eng != "sp":
                rel[o.eng].append(o)
        nleft = len(ops)
        LAT = 0.25
        bank_free = self.bank_free
        bank_last = self.bank_last
        bank_occ = [None] * 8

        first = {}
        for o in ops:
            for vp in o.vps:
                first.setdefault(id(vp), o)
        allocq = sorted({id(o): o for o in first.values()}.values(), key=lambda o: o.rid)
        apos = {id(o): i for i, o in enumerate(allocq)}
        adone = [False] * len(allocq)
        anext = [0]
        RESERVE = int(os.environ.get("KSCHED_RESERVE", str(self.reserve)))

        def bank_time(o):
            need = [vp for vp in o.vps if vp.bank is None]
            if not need:
                return 0.0
            if id(o) in apos and apos[id(o)] != anext[0]:
                nfree = sum(1 for k in range(8) if bank_occ[k] is None)
                if nfree - len(need) < RESERVE:
                    return None
            free = sorted(bank_free[k] for k in range(8) if bank_occ[k] is None)
            if len(free) < len(need):
                return None
            return free[len(need) - 1]
        while nleft:
            best = None
            for e in self.E:
                if e == "sp":
                    if sp_next < len(spq) and spq[sp_next].npend == 0:
                        o = spq[sp_next]
                        cand = (max(free_at[e], o.ready), o.rid, o)
                    else:
                        continue
                else:
                    if not rel[e]:
                        continue
                    now = free_at[e]
                    est = []
                    for o in rel[e]:
                        bt = bank_time(o)
                        if bt is not None:
                            est.append((max(now, o.ready, bt), o))
                    if not est:
                        continue
                    ready_now = [o for (t_, o) in est if t_ <= now]
                    if ready_now:
                        o = min(ready_now, key=lambda o: (-tailp[id(o)], o.rid))
                        cand = (now, o.rid, o)
                    else:
                        t_, o = min(est, key=lambda to: (to[0], to[1].rid))
                        cand = (t_, o.rid, o)
                if best is None or cand[:2] < best[:2]:
                    best = cand
            assert best is not None, "scheduler deadlock (PSUM banks)"
            st, _, o = best
            e = o.eng
            if id(o) in apos:
                adone[apos[id(o)]] = True
                while anext[0] < len(allocq) and adone[anext[0]]:
                    anext[0] += 1
            for vp in o.vps:
                if vp.bank is None:
                    k = min((k for k in range(8) if bank_occ[k] is None), key=lambda k: bank_free[k])
                    vp.bank = k
                    bank_occ[k] = vp
                    o.deps = o.deps + [d for d in bank_last[k] if d.epoch == self.epoch]
            if e == "sp":
                sp_next += 1
                free_at[e] = st + 0.06
                o.fin = st + o.cost
            else:
                rel[e].remove(o)
                free_at[e] = st + o.cost
                o.fin = st + o.cost
            order[e].append(o)
            nleft -= 1
            for vp in o.vps:
                vp.ops.append(o)
                vp.remaining -= 1
                if vp.remaining == 0:
                    k = vp.bank
                    bank_free[k] = max(a.fin for a in vp.ops)
                    bank_last[k] = list(vp.ops)
                    bank_occ[k] = None
            for s_ in o.succ:
                s_.ready = max(s_.ready, o.fin + LAT)
                s_.npend -= 1
                if s_.npend == 0 and s_.eng != "sp":
                    rel[s_.eng].append(s_)
        if os.environ.get("KSCHED_DBG"):
            print("sched: n=%d makespan=%.1f us busy=%s" % (len(ops), max([o.fin for o in ops] + [0.0]),
                  {k: round(sum(o.cost for o in v), 1) for k, v in order.items()}))
        if os.environ.get("KSCHED_CP") and len(ops) > 500:
            prev_on_eng = {}
            for e, lst in order.items():
                for a, b in zip(lst, lst[1:]):
                    prev_on_eng[id(b)] = a
            o = max(ops, key=lambda o: o.fin)
            path = []
            while o is not None and len(path) < int(os.environ["KSCHED_CP"]):
                st = o.fin - o.cost
                why, nxt = "start", None
                best = None
                for d in o.deps:
                    if best is None or d.fin > best.fin:
                        best = d
                pe_ = prev_on_eng.get(id(o))
                if best is not None and abs((best.fin + 0.25) - st) < 1e-6:
                    why, nxt = "dep", best
                elif pe_ is not None:
                    why, nxt = "eng", pe_
                elif best is not None:
                    why, nxt = "dep?", best
                ln = o.fn.__code__.co_firstlineno if o.fn is not None else -1
                path.append("%8.1f %-4s cost=%5.2f line=%d via=%s" % (st, o.eng, o.cost, ln, why))
                o = nxt
            print("\n".join(path))
        return order

    def flush(self):
        nc = self.nc
        order = self._schedule()
        prog = {k: [] for k in self.E}
        for e, lst in order.items():
            for o in lst:
                if o.dma is None:
                    self.cnt[e] += 1
                    o.idx = self.cnt[e]
        for e, lst in order.items():
            eng = self.E[e]
            for o in lst:
                need = {}
                for d in o.deps:
                    if d.dma is None:
                        key, sm, val = ("eng", d.eng), self.sem[d.eng], d.idx
                    else:
                        key, sm, val = ("dma", id(d.dma[2])), d.dma[2], d.dma[3]
                    if self.waited[e].get(key, 0) >= val:
                        continue
                    if key not in need or need[key][1] < val:
                        need[key] = (sm, val)
                waits = []
                for key, (sm, val) in need.items():
                    self.waited[e][key] = val
                    waits.append((sm, val))
                if o.dma is None:
                    def run(eng=eng, waits=waits, fn=o.fn, sem=self.sem[e]):
                        for (sm, v) in waits:
                            eng.wait_ge(sm, v)
                        fn(eng).then_inc(sem, 1)
                else:
                    def run(eng=eng, waits=waits, d=o.dma):
                        for (sm, v) in waits:
                            eng.wait_ge(sm, v)
                        eng.dma_start(out=d[0], in_=d[1]).then_inc(d[2], 16)
                prog[e].append(run)
        snap = [(self.sem[k], self.cnt[k], ("eng", k)) for k in self.sem if self.cnt[k] > 0]
        snap += [(o.dsem, o.dcount, ("dma", id(o.dsem))) for o in self.dsems]
        for e in self.E:
            eng = self.E[e]
            ws = []
            for (sm, v, key) in snap:
                if self.waited[e].get(key, 0) < v:
                    self.waited[e][key] = v
                    ws.append((sm, v))

            def runb(eng=eng, ws=ws):
                for (sm, v) in ws:
                    eng.wait_ge(sm, v)
            prog[e].append(runb)
        with nc.Block() as block:
            @block.sync
            def _(e):
                for f in prog["sp"]:
                    f()

            @block.tensor
            def _(e):
                for f in prog["pe"]:
                    f()

            @block.scalar
            def _(e):
                for f in prog["act"]:
                    f()

            @block.vector
            def _(e):
                for f in prog["dve"]:
                    f()

            @block.gpsimd
            def _(e):
                for f in prog["pool"]:
                    f()
        self.ops = []
        self.epoch += 1
        self.bank_free = [0.0] * 8
        self.bank_last = [[] for _ in range(8)]


class T:
    def __init__(self, stack, nc, name, shape, dt, psum=False):
        mk = nc.psum_tensor if psum else nc.sbuf_tensor
        self.t = stack.enter_context(mk("s_" + name, list(shape), dt))
        self.b = Buf(name)

    def __getitem__(self, idx):
        return self.t[idx]


def _build(S, n_tiles_c, nc_box):
    NT = S // 128
    NSUP = NT // n_tiles_c
    TC = n_tiles_c * 128
    nc = bass.Bass("TRN2", target_bir_lowering=False)
    nc_box.append(nc)
    dram = lambda name, shape, dt=F32, kind="ExternalInput": nc.dram_tensor(name, list(shape), dt, kind=kind).ap()
    x_d = dram("x", [S, D])
    ccol_d = dram("ccol", [128, 16])
    wada_d = dram("w_ada", [D, 6 * D])
    win_d = dram("w_in", [D, IN_COLS])
    pcol_d = dram("pcol", [128, NPC])
    cst_d = dram("cst", [128, NCS])
    brow_d = dram("brow", [1, 2560])
    lnrow_d = dram("lnrows", [4, D])
    wst_d = dram("wsT", [128, 1024])
    bspb_d = dram("bspb", [128, 512])
    lw_d = dram("lw", [64, 512])
    wg_d = dram("wg", [96, 512])
    wa_d = dram("w_branch_a", [512, D])
    wb_d = dram("w_branch_b", [512, D])
    wout_d = dram("w_out", [D, D])
    wff1_d = dram("w_ff1", [D, 4 * D])
    wff2_d = dram("w_ff2", [4 * D, D])
    out_d = dram("out", [S, D], kind="ExternalOutput")
    yab_d = nc.dram_tensor("yab_s", [S, D], BF16).ap()
    h1_d = nc.dram_tensor("h1_s", [S, D], F32).ap()
    yab_db = [Buf("yabd%d" % i) for i in range(NT)]
    h1_db = [Buf("h1d%d" % i) for i in range(NT)]

    with contextlib.ExitStack() as G:
        G.enter_context(nc.cleanup_on_exit())
        G.enter_context(nc.allow_low_precision("bf16 matmuls with fp32 accumulation"))
        sc = Sched(nc, G)
        mkT = lambda st, name, shape, dt=F32: T(st, nc, name, shape, dt)

        def ck(tag):
            if os.environ.get("KSTOP") == tag:
                sc.barrier()
                sc.flush()
                raise StopBuild()
        cst = mkT(G, "cst", [128, NCS])
        pcol = mkT(G, "pcol", [128, NPC])
        modT = mkT(G, "modT", [128, 64])
        identb = mkT(G, "identb", [128, 128], BF16)
        onesb = mkT(G, "onesb", [1, 128], BF16)
        neghG = mkT(G, "neghG", [128, 2])
        browb = mkT(G, "browb", [1, 2560], BF16)
        psr = [T(G, nc, "ps%d" % i, [128, 512], F32, psum=True) for i in range(8)]
        sc.ps_real = [p.t for p in psr]
        sc.ps16_real = [p.t.bitcast(BF16) for p in psr]
        vbank = sc.vbank

        pc = lambda j, p0=0, p1=128: pcol[p0:p1, j:j + 1]
        ident = cst[:, CS_ID:CS_ID + 128]
        SH1, GT1, SH2, GT2, OPS1, OPS2 = 0, 16, 24, 40, 48, 56

        sc.dma(cst[:, :], cst_d[:, :], cst.b, writes=[cst.b])
        sc.dma(pcol[:, :], pcol_d[:, :], pcol.b, writes=[pcol.b])
        sc.op("dve", lambda e: e.tensor_copy(out=identb[:, :], in_=ident), reads=[cst.b], writes=[identb.b])
        sc.op("dve", lambda e: e.memset(onesb[:, :], 1.0), writes=[onesb.b])
        sc.op("dve", lambda e: e.memset(neghG[:, :], -0.5), writes=[neghG.b])

        with contextlib.ExitStack() as P0:
            ccol = mkT(P0, "ccol", [128, 16])
            cact = mkT(P0, "cact", [128, 16])
            csig = mkT(P0, "csig", [128, 16])
            browf = mkT(P0, "browf", [1, 2560])
            stg = [mkT(P0, "stgm%d" % i, [128, 8 * 256]) for i in range(8)]
            sc.dma(ccol[:, :], ccol_d[:, :], ccol.b, writes=[ccol.b])
            sc.dma(browf[:, :], brow_d[:, :], browf.b, writes=[browf.b])
            sc.op("act", lambda e: e.activation(out=csig[:, :], in_=ccol[:, :], func=AF.Sigmoid),
                  reads=[ccol.b], writes=[csig.b])
            sc.op("dve", lambda e: e.tensor_tensor(out=cact[:, :], in0=ccol[:, :], in1=csig[:, :], op=ALU.mult),
                  reads=[ccol.b, csig.b], writes=[cact.b])
            sc.op("dve", lambda e: e.tensor_copy(out=browb[:, :], in_=browf[:, :]), reads=[browf.b], writes=[browb.b])
            wada_v = wada_d.rearrange("(k p) f -> p k f", p=128)
            modrow = mkT(P0, "modrow", [2, 6 * D])
            prow = None
            for piece in range(24):
                sg = stg[piece % 8]
                sgv = sg[:, :].rearrange("p (k f) -> p k f", k=8)
                sc.dma(sgv, wada_v[:, :, piece * 256:(piece + 1) * 256], sg.b, writes=[sg.b])
                if piece % 2 == 0:
                    prow = vbank()

                def f(e, sgv=sgv, piece=piece, prow=prow):
                    r = None
                    for k in range(8):
                        r = e.matmul(prow[0:2, (piece % 2) * 256:(piece % 2 + 1) * 256], lhsT=cact[:, 2 * k:2 * k + 2],
                                     rhs=sgv[:, k, :], start=(k == 0), stop=(k == 7))
                    return r
                sc.op("pe", f, reads=[sg.b, cact.b], writes=[prow.b])
                if piece % 2 == 1:
                    g0 = (piece // 2) * 512
                    sc.op("act", lambda e, prow=prow, g0=g0: e.activation(out=modrow[0:2, g0:g0 + 512], in_=prow[0:2, :], func=AF.Identity),
                          reads=[prow.b, modrow.b], writes=[modrow.b])
            pm = vbank()

            def ftm(e):
                r = None
                for m in range(48):
                    r = e.transpose(out=pm[:, 2 * m:2 * m + 2], in_=modrow[0:2, m * 128:(m + 1) * 128], identity=ident[0:2, 0:2])
                return r
            sc.op("pe", ftm, reads=[modrow.b, cst.b], writes=[pm.b])
            sc.op("dve", lambda e: e.tensor_tensor(out=modT[:, 0:48], in0=pm[:, 0:96].rearrange("p (m two) -> p m two", two=2)[:, :, 0],
                                                   in1=pcol[:, PC_BADA:PC_BADA + 48], op=ALU.add),
                  reads=[pm.b, pcol.b], writes=[modT.b])
            sc.op("dve", lambda e: e.tensor_scalar(out=modT[:, 48:56], in0=modT[:, 8:16], scalar1=1.0, scalar2=None, op0=ALU.add),
                  reads=[modT.b], writes=[modT.b])
            sc.op("dve", lambda e: e.tensor_scalar(out=modT[:, 56:64], in0=modT[:, 32:40], scalar1=1.0, scalar2=None, op0=ALU.add),
                  reads=[modT.b], writes=[modT.b])
            sc.barrier()
            sc.flush()
        if os.environ.get("KSTOP") == "setup":
            return nc
        md = lambda j: modT[:, j:j + 1]

        def load_w_bf16(st, name, src_view, nk, ncols, stgs, engs=("pool", "dve")):
            w = st if isinstance(st, T) else mkT(st, name, [128, nk * ncols], BF16)
            CH = stgs[0].t.shape[1]
            n = 0
            for k in range(nk):
                for c0 in range(0, ncols, CH):
                    cw = min(CH, ncols - c0)
                    sg = stgs[n % len(stgs)]
                    sc.dma(sg[:, 0:cw], src_view[:, k, c0:c0 + cw], sg.b, writes=[sg.b])
                    eng = engs[n % len(engs)]
                    sc.op(eng, lambda e, sg=sg, cw=cw, k=k, c0=c0: (
                        e.tensor_copy(out=w[:, k * ncols + c0:k * ncols + c0 + cw], in_=sg[:, 0:cw]) if hasattr(e, "tensor_copy")
                        else e.activation(out=w[:, k * ncols + c0:k * ncols + c0 + cw], in_=sg[:, 0:cw], func=AF.Identity)),
                        reads=[sg.b], writes=[w.b])
                    n += 1
            return w

        def load_w_cast(w, src_view, nk, ncols):
            chunks = [(k, c0, min(2048, ncols - c0)) for k in range(nk) for c0 in range(0, ncols, 2048)]
            for n_, (k, c0, cw) in enumerate(chunks):
                last = n_ == len(chunks) - 1
                sc.dma(w[:, k * ncols + c0:k * ncols + c0 + cw], src_view[:, k, c0:c0 + cw], w.b,
                       writes=[w.b] if last else [], q="pool", cost=4.0)
            return w

        def x_to_hT(e_pe_reads, xt_ap, xt_b, hT, psb, ops_col, sh_col, ntok=128, tok0=0, col0=0):
            pa, pb = vbank(), vbank()
            W = hT.t.shape[1] // 8

            def f(e):
                r = None
                for k in range(8):
                    p = pa if k < 4 else pb
                    r = e.transpose(out=p[:, (k % 4) * 128:(k % 4 + 1) * 128], in_=xt_ap[:, col0 + k * 128:col0 + (k + 1) * 128], identity=ident)
                return r
            sc.op("pe", f, reads=[xt_b, cst.b], writes=[pa.b, pb.b])

            def g(e):
                r = None
                for k in range(8):
                    p = pa if k < 4 else pb
                    r = e.activation(out=hT[:, k * W + tok0:k * W + tok0 + 128], in_=p[:, (k % 4) * 128:(k % 4 + 1) * 128],
                                     func=AF.Identity, bias=md(sh_col + k), scale=md(ops_col + k))
                return r
            sc.op("act", g, reads=[pa.b, pb.b, modT.b], writes=[hT.b])

        def gelu_from(psrc_fn, nchunk, bias_fn, dst, tmp, parts=128):
            gx, gx2 = tmp[0], tmp[1]
            gt = gt2 = gs = gx2
            pbufs = psrc_fn("bufs")

            def f1(e):
                r = None
                for c in range(nchunk):
                    r = e.activation(out=gx[:, c * 128:(c + 1) * 128], in_=psrc_fn(c), func=AF.Identity, bias=bias_fn(c), scale=1.0)
                return r
            sc.op("act", f1, reads=pbufs + [pcol.b], writes=[gx.b])

            def f2(e):
                r = None
                for c in range(nchunk):
                    r = e.activation(out=gx2[:, c * 128:(c + 1) * 128], in_=psrc_fn(c), func=AF.Square, bias=bias_fn(c), scale=1.0)
                return r
            sc.op("act", f2, reads=pbufs + [pcol.b], writes=[gx2.b])
            n = nchunk * 128
            sc.op("pool", lambda e: e.tensor_scalar(out=gt[:, 0:n], in0=gx2[:, 0:n], scalar1=0.044715, scalar2=1.0, op0=ALU.mult, op1=ALU.add),
                  reads=[gx2.b], writes=[gt.b])
            sc.op("pool", lambda e: e.tensor_tensor(out=gt2[:, 0:n], in0=gt[:, 0:n], in1=gx[:, 0:n], op=ALU.mult),
                  reads=[gt.b, gx.b], writes=[gt2.b])
            sc.op("act", lambda e: e.activation(out=gs[:, 0:n], in_=gt2[:, 0:n], func=AF.Sigmoid, scale=1.5957691216057308),
                  reads=[gt2.b], writes=[gs.b])
            sc.op("pool", lambda e: e.tensor_tensor(out=dst[:, 0:n], in0=gx[:, 0:n], in1=gs[:, 0:n], op=ALU.mult),
                  reads=[gx.b, gs.b], writes=[dst.b])

        def layernorm_rows(src, dst_scaled, gb, bb, tmp_small, out_t, out_eng="pool", c0=0):
            st6, mv, sd, rstd = tmp_small
            sv = lambda a, b: src[:, c0 + a:c0 + b]
            sc.op("dve", lambda e: e.bn_stats(out=st6[:, 0:6], in_=sv(0, 512)), reads=[src.b], writes=[st6.b])
            sc.op("dve", lambda e: e.bn_stats(out=st6[:, 6:12], in_=sv(512, 1024)), reads=[src.b, st6.b], writes=[st6.b])
            sc.op("dve", lambda e: e.bn_aggr(out=mv[:, 0:2], in_=st6[:, 0:12]), reads=[st6.b], writes=[mv.b])
            sc.op("pool", lambda e: e.tensor_scalar(out=sd[:, 0:1], in0=mv[:, 1:2], scalar1=LN_EPS, scalar2=None, op0=ALU.add),
                  reads=[mv.b], writes=[sd.b])
            sc.op("pool", lambda e: e.tensor_tensor(out=rstd[:, 0:1], in0=sd[:, 0:1], in1=neghG[:, 0:1], op=ALU.pow),
                  reads=[sd.b, neghG.b], writes=[rstd.b])
            sc.op("dve", lambda e: e.tensor_scalar(out=dst_scaled[:, :], in0=sv(0, 1024), scalar1=mv[:, 0:1], scalar2=rstd[:, 0:1],
                                                   op0=ALU.subtract, op1=ALU.mult), reads=[src.b, mv.b, rstd.b], writes=[dst_scaled.b])
            sc.op(out_eng, lambda e: e.tensor_tensor(out=sv(0, 1024), in0=dst_scaled[:, :], in1=gb[:, :], op=ALU.mult),
                  reads=[dst_scaled.b, gb.b, src.b], writes=[src.b])
            sc.op(out_eng, lambda e: e.tensor_tensor(out=out_t[:, :], in0=sv(0, 1024), in1=bb[:, :], op=ALU.add),
                  reads=[src.b, bb.b], writes=[out_t.b])

        with contextlib.ExitStack() as PA:
            NA = 2720
            winA = mkT(PA, "winA", [128, 8 * NA], BF16)
            wcol = lambda k, c0, m: winA[:, k * NA + c0:k * NA + c0 + m]
            lwb = mkT(PA, "lwb", [64, 512], BF16)
            wgb = mkT(PA, "wgb", [96, 512], BF16)
            wsTb = mkT(PA, "wsTb", [128, 1024], BF16)
            B2 = mkT(PA, "B2", [128, 512])
            with contextlib.ExitStack() as TA:
                stgs = [mkT(TA, "stga%d" % i, [128, 1024]) for i in range(2)]
                load_w_cast(winA, win_d.rearrange("(k p) f -> p k f", p=128)[:, :, 0:NA], 8, NA)
                sc.dma(stgs[0][0:64, 0:512], lw_d[:, :], stgs[0].b, writes=[stgs[0].b])
                sc.op("dve", lambda e: e.tensor_copy(out=lwb[:, :], in_=stgs[0][0:64, 0:512]), reads=[stgs[0].b], writes=[lwb.b])
                sc.dma(stgs[1][0:96, 0:512], wg_d[:, :], stgs[1].b, writes=[stgs[1].b])
                sc.op("dve", lambda e: e.tensor_copy(out=wgb[:, :], in_=stgs[1][0:96, 0:512]), reads=[stgs[1].b], writes=[wgb.b])
                sc.dma(stgs[0][:, 0:1024], wst_d[:, :], stgs[0].b, writes=[stgs[0].b])

                def fws(e):
                    r = None
                    for g in range(8):
                        r = e.tensor_tensor(out=wsTb[:, g * 128:(g + 1) * 128], in0=stgs[0][:, g * 128:(g + 1) * 128],
                                            in1=cst[:, CS_M4 + 128:CS_M4 + 256], op=ALU.mult)
                    return r
                sc.op("dve", fws, reads=[stgs[0].b, cst.b], writes=[wsTb.b])
                sc.dma(B2[:, :], bspb_d[:, :], B2.b, writes=[B2.b])
                sc.barrier()
                sc.flush()
            xt = [mkT(PA, "xtA%d" % i, [128, D]) for i in range(2)]
            hT = mkT(PA, "hTA", [128, 8 * 128], BF16)
            g5 = [mkT(PA, "gel%d" % i, [128, 512]) for i in range(2)]
            uT = mkT(PA, "uT", [128, 512])
            vg = mkT(PA, "vg", [128, 512])
            vn = mkT(PA, "vn", [128, 512], BF16)
            onesv = mkT(PA, "onesv", [128, 512], BF16)
            onesf = mkT(PA, "onesf", [128, 128])
            sm = [mkT(PA, "smA%d" % i, [128, 64]) for i in range(6)]
            tmpa = mkT(PA, "tmpa", [128, 512])
            yab = [mkT(PA, "yab%d" % i, [128, 1024], BF16) for i in range(2)]
            zs = mkT(PA, "zs", [128, 14 * 129])
            zp = mkT(PA, "zp", [128, 14 * 128])
            txa = mkT(PA, "txa", [64, 128], BF16)
            txf = mkT(PA, "txf", [32, 128])
            negh = mkT(PA, "negh", [128, 8])
            sxg = mkT(PA, "sxg", [96, 128], BF16)
            sw = mkT(PA, "sw", [128, 512])
            a_ = mkT(PA, "a_", [128, 512])
            gT_2 = [mkT(PA, "gT_%d" % i_, [128, 512]) for i_ in range(2)]
            cum = mkT(PA, "cum", [128, 4 * 129])
            gam = mkT(PA, "gam", [128, 4 * 129])
            igam = mkT(PA, "igam", [128, 512])
            gC_2 = [mkT(PA, "gC_%d" % i_, [128, 4]) for i_ in range(2)]
            kq = mkT(PA, "kq", [128, 512])
            nrm = mkT(PA, "nrm", [128, 512])
            inv = mkT(PA, "inv", [128, 512])
            kk = mkT(PA, "kk", [128, 512])
            t1 = mkT(PA, "t1", [128, 512])
            kmod = mkT(PA, "kmod", [128, 512])
            t2 = mkT(PA, "t2", [128, 512])
            rkp = mkT(PA, "rkp", [128, 512])
            bonus_2 = [mkT(PA, "bonus_%d" % i_, [128, 512]) for i_ in range(2)]
            RA_2 = [mkT(PA, "RA_%d" % i_, [128, 1024], BF16) for i_ in range(2)]
            KTf_2 = [mkT(PA, "KTf_%d" % i_, [128, 512], BF16) for i_ in range(2)]
            BTf_2 = [mkT(PA, "BTf_%d" % i_, [128, 512], BF16) for i_ in range(2)]
            vb = mkT(PA, "vb", [128, 512], BF16)
            KBt_2 = [mkT(PA, "KBt_%d" % i_, [128, 1024], BF16) for i_ in range(2)]
            Vt_2 = [mkT(PA, "Vt_%d" % i_, [128, 512], BF16) for i_ in range(2)]
            AM_2 = [mkT(PA, "AM_%d" % i_, [128, 8 * 512], BF16) for i_ in range(2)]
            Pp = [mkT(PA, "Pp%d" % i, [128, 1024], BF16) for i in range(7)]
            PTp = [mkT(PA, "PTp%d" % i, [128, 1024], BF16) for i in range(2)]
            Xf = [mkT(PA, "Xf%d" % i, [128, 512]) for i in range(2)]
            Xb = [mkT(PA, "Xb%d" % i, [128, 512], BF16) for i in range(2)]
            Sf = mkT(PA, "Sf", [128, 512])
            Sbd = mkT(PA, "Sbd", [128, 512], BF16)
            st1 = mkT(PA, "st1", [128, 512])
            st2 = mkT(PA, "st2", [128, 512])
            gst = mkT(PA, "gst", [128, 48])
            gmv = mkT(PA, "gmv", [128, 16])
            gsd = mkT(PA, "gsd", [128, 8])
            grs = mkT(PA, "grs", [128, 8])
            ynb = mkT(PA, "ynb", [128, 512], BF16)
            yt = mkT(PA, "yt", [128, 512])
            yt2 = yt
            zsv = zs[:, :].rearrange("p (j t) -> p j t", j=14)
            zpv = zp[:, :].rearrange("p (j t) -> p j t", j=14)
            cumv = cum[:, :].rearrange("p (c t) -> p c t", c=4)
            gamv = gam[:, :].rearrange("p (c t) -> p c t", c=4)
            v3 = lambda t_, c=4: t_[:, :].rearrange("p (c t) -> p c t", c=c)
            for t_ in (zs, cum, Sf, zp):
                sc.op("pool", lambda e, t_=t_: e.memset(t_[:, :], 0.0), writes=[t_.b])
            sc.op("pool", lambda e: e.memset(Sbd[:, :], 0.0), writes=[Sbd.b])
            sc.op("pool", lambda e: e.memset(onesv[:, :], 1.0), writes=[onesv.b])
            sc.op("pool", lambda e: e.memset(onesf[:, :], 1.0), writes=[onesf.b])
            sc.op("pool", lambda e: e.memset(negh[:, :], -0.5), writes=[negh.b])
            m4b = mkT(PA, "m4b", [128, 512], BF16)
            sc.op("dve", lambda e: e.tensor_copy(out=m4b[:, :], in_=cst[:, CS_M4:CS_M4 + 512]), reads=[cst.b], writes=[m4b.b])

            def spatial(src_bf, pbank):
                def f(e):
                    r = None
                    for g in range(8):
                        c, pb = g // 2, (g % 2) * 64
                        r = e.matmul(pbank[pb:pb + 64, c * 128:(c + 1) * 128], lhsT=src_bf[:, g * 64:(g + 1) * 64],
                                     rhs=wsTb[:, g * 128:(g + 1) * 128], start=True, stop=True)
                    return r
                sc.op("pe", f, reads=[src_bf.b, wsTb.b], writes=[pbank.b])
            pb_ = vbank()
            spatial(onesv, pb_)

            def fB2(e):
                r = None
                for c in range(4):
                    r = e.scalar_tensor_tensor(out=B2[:, c * 128:(c + 1) * 128], in0=pb_[:, c * 128:(c + 1) * 128], scalar=pc(PC_BLN + c),
                                               in1=B2[:, c * 128:(c + 1) * 128], op0=ALU.mult, op1=ALU.add)
                return r
            sc.op("dve", fB2, reads=[pb_.b, pcol.b, B2.b], writes=[B2.b])

            def load_x(i):
                xs_ = xt[i % 2]
                sc.dma(xs_[:, :], x_d[i * 128:(i + 1) * 128, :], xs_.b, writes=[xs_.b])

            sl = lambda t_, c, w=128: t_[:, c * w:(c + 1) * w]
            ck("A0")
            hk = lambda k: hT[:, k * 128:(k + 1) * 128]

            def front_h(i):
                xs = xt[i % 2]
                x_to_hT(None, xs, xs.b, hT, 0, OPS1, SH1)
                yield

            def front_g(i):
                RA, KTf, BTf, KBt, Vt, gT, bonus, gC = RA_2[i % 2], KTf_2[i % 2], BTf_2[i % 2], KBt_2[i % 2], Vt_2[i % 2], gT_2[i % 2], bonus_2[i % 2], gC_2[i % 2]
                RAv = RA[:, :].rearrange("p (c q t) -> p c q t", c=4, q=2)
                ya = yab[i % 2]
                pu = vbank()

                def fu(e, pu=pu):
                    r = None
                    for c in range(4):
                        for k in range(8):
                            r = e.matmul(sl(pu, c), lhsT=wcol(k, c * 128, 128), rhs=hk(k), start=(k == 0), stop=(k == 7))
                    return r
                sc.op("pe", fu, reads=[winA.b, hT.b], writes=[pu.b])
                yield
                gelu_from(lambda c, pu=pu: [pu.b] if c == "bufs" else sl(pu, c), 4, lambda c: pc(PC_BU + c), uT, g5)
                yield
                ck("A2")
                pv = vbank()

                def fv(e, pv=pv):
                    for k in range(8):
                        e.matmul(pv[:, :], lhsT=hk(k), rhs=wcol(k, 512, 512), start=(k == 0), stop=False)
                    return e.matmul(pv[:, :], lhsT=onesb[0:1, 0:128], rhs=browb[0:1, 0:512], start=False, stop=True)
                sc.op("pe", fv, reads=[winA.b, hT.b, onesb.b, browb.b], writes=[pv.b])
                yield
                gelu_from(lambda c, pv=pv: [pv.b] if c == "bufs" else sl(pv, c), 4, lambda c: 0.0, vg, g5)
                yield
                st6, mv, sd, rstd = sm[0], sm[1], sm[2], sm[3]
                sc.op("dve", lambda e: e.bn_stats(out=st6[:, 0:6], in_=vg[:, :]), reads=[vg.b], writes=[st6.b])
                yield
                sc.op("dve", lambda e: e.bn_aggr(out=mv[:, 0:2], in_=st6[:, 0:6]), reads=[st6.b], writes=[mv.b])
                yield
                sc.op("pool", lambda e: e.tensor_scalar(out=sd[:, 0:1], in0=mv[:, 1:2], scalar1=LN_EPS, scalar2=None, op0=ALU.add),
                      reads=[mv.b], writes=[sd.b])
                yield
                sc.op("pool", lambda e: e.tensor_tensor(out=rstd[:, 0:1], in0=sd[:, 0:1], in1=negh[:, 0:1], op=ALU.pow), reads=[sd.b, negh.b], writes=[rstd.b])
                yield
                sc.op("dve", lambda e: e.tensor_scalar(out=vn[:, :], in0=vg[:, :], scalar1=mv[:, 0:1], scalar2=rstd[:, 0:1],
                                                       op0=ALU.subtract, op1=ALU.mult), reads=[vg.b, mv.b, rstd.b], writes=[vn.b])
                yield
                ck("A3")
                pS = vbank()
                spatial(vn, pS)
                yield

                def fya(e, pS=pS):
                    r = None
                    for c in range(4):
                        r = e.scalar_tensor_tensor(out=sl(tmpa, c), in0=sl(pS, c), scalar=pc(PC_GLN + c), in1=sl(B2, c), op0=ALU.mult, op1=ALU.add)
                    return r
                sc.op("dve", fya, reads=[pS.b, pcol.b, B2.b], writes=[tmpa.b])
                yield
                sc.op("dve", lambda e, ya=ya: e.tensor_tensor(out=ya[:, 0:512], in0=tmpa[:, :], in1=uT[:, :], op=ALU.mult),
                      reads=[tmpa.b, uT.b], writes=[ya.b])
                yield

            def front_r(i):
                RA, KTf, BTf, KBt, Vt, gT, bonus, gC = RA_2[i % 2], KTf_2[i % 2], BTf_2[i % 2], KBt_2[i % 2], Vt_2[i % 2], gT_2[i % 2], bonus_2[i % 2], gC_2[i % 2]
                RAv = RA[:, :].rearrange("p (c q t) -> p c q t", c=4, q=2)
                ya = yab[i % 2]
                ck("A4")
                pz = [vbank() for _ in range(4)]

                def fz(e, pz=pz):
                    r = None
                    for j in range(12):
                        for k in range(8):
                            r = e.matmul(sl(pz[j // 4], j % 4), lhsT=wcol(k, 1024 + j * 128, 128), rhs=hk(k), start=(k == 0), stop=(k == 7))
                    for k in range(8):
                        r = e.matmul(pz[3][0:64, 0:128], lhsT=wcol(k, 2560, 64), rhs=hk(k), start=(k == 0), stop=(k == 7))
                    for k in range(8):
                        r = e.matmul(pz[3][0:96, 128:256], lhsT=wcol(k, 2624, 96), rhs=hk(k), start=(k == 0), stop=(k == 7))
                    return r
                sc.op("pe", fz, reads=[winA.b, hT.b], writes=[b_.b for b_ in pz])
                yield

                def fzs(e, pz=pz):
                    for j in range(12):
                        e.activation(out=zsv[:, j, 1:129], in_=sl(pz[j // 4], j % 4), func=AF.Identity, bias=pc(PC_BRKV + j), scale=1.0)
                    e.activation(out=zsv[0:64, 12, 1:129], in_=pz[3][0:64, 0:128], func=AF.Identity, bias=pc(PC_BXWXA, 0, 64), scale=1.0)
                    return e.activation(out=zsv[0:96, 13, 1:129], in_=pz[3][0:96, 128:256], func=AF.Identity, bias=pc(PC_BXG, 0, 96), scale=1.0)
                sc.op("act", fzs, reads=[b_.b for b_ in pz] + [pcol.b], writes=[zs.b])
                yield
                sc.op("pool", lambda e: e.tensor_tensor(out=zpv[:, :, :], in0=zsv[:, :, 0:128], in1=zsv[:, :, 1:129], op=ALU.subtract),
                      reads=[zs.b], writes=[zp.b])
                yield

                def fzp(e):
                    for j in range(12):
                        e.scalar_tensor_tensor(out=zpv[:, j, :], in0=zpv[:, j, :], scalar=pc(PC_MURKV + j), in1=zsv[:, j, 1:129], op0=ALU.mult, op1=ALU.add)
                    e.scalar_tensor_tensor(out=zpv[0:64, 12, :], in0=zpv[0:64, 12, :], scalar=pc(PC_MUXWXA, 0, 64), in1=zsv[0:64, 12, 1:129], op0=ALU.mult, op1=ALU.add)
                    return e.scalar_tensor_tensor(out=zpv[0:96, 13, :], in0=zpv[0:96, 13, :], scalar=pc(PC_MUXG, 0, 96), in1=zsv[0:96, 13, 1:129], op0=ALU.mult, op1=ALU.add)
                sc.op("dve", fzp, reads=[zp.b, zs.b, pcol.b], writes=[zp.b])
                yield
                sc.op("pool", lambda e: e.tensor_copy(out=zsv[:, :, 0:1], in_=zsv[:, :, 128:129]), reads=[zs.b], writes=[zs.b])
                yield
                ck("A5")

                def flo(e):
                    e.activation(out=txf[0:32, :], in_=zpv[0:32, 12, :], func=AF.Sigmoid, scale=2.0)
                    e.activation(out=txa[32:64, :], in_=zpv[32:64, 12, :], func=AF.Identity)
                    return e.activation(out=sxg[0:96, :], in_=zpv[0:96, 13, :], func=AF.Sigmoid)
                sc.op("act", flo, reads=[zp.b, txa.b], writes=[txf.b, txa.b, sxg.b])
                yield
                sc.op("pool", lambda e: e.tensor_scalar(out=txa[0:32, :], in0=txf[0:32, :], scalar1=2.0, scalar2=-1.0, op0=ALU.mult, op1=ALU.add),
                      reads=[txf.b, txa.b], writes=[txa.b])
                yield
                ck("A5a")
                pw, pa_, pg = vbank(), vbank(), vbank()

                def flm(e, pw=pw, pa_=pa_, pg=pg):
                    r = None
                    for c in range(4):
                        e.matmul(sl(pw, c), lhsT=lwb[0:32, c * 128:(c + 1) * 128], rhs=txa[0:32, :], start=True, stop=True)
                    for c in range(4):
                        e.matmul(sl(pa_, c), lhsT=lwb[32:64, c * 128:(c + 1) * 128], rhs=txa[32:64, :], start=True, stop=True)
                    for c in range(4):
                        r = e.matmul(sl(pg, c), lhsT=wgb[0:96, c * 128:(c + 1) * 128], rhs=sxg[0:96, :], start=True, stop=True)
                    return r
                sc.op("pe", flm, reads=[lwb.b, wgb.b, txa.b, sxg.b], writes=[pw.b, pa_.b, pg.b])
                yield

                ck("A5b")
                def fsw(e, pw=pw, pa_=pa_, pg=pg):
                    for c in range(4):
                        e.activation(out=sl(sw, c), in_=sl(pw, c), func=AF.Sigmoid, bias=pc(PC_W0 + c), scale=1.0)
                    for c in range(4):
                        e.activation(out=sl(a_, c), in_=sl(pa_, c), func=AF.Sigmoid, bias=pc(PC_A0 + c), scale=1.0)
                    return e.activation(out=gT[:, :], in_=pg[:, :], func=AF.Identity)
                sc.op("act", fsw, reads=[pw.b, pa_.b, pg.b, pcol.b], writes=[sw.b, a_.b, gT.b])
                yield

                ck("A5c")
                def fcum(e):
                    r = None
                    for c in range(4):
                        r = e.tensor_tensor_scan(out=cumv[:, c, 1:129], data0=onesf[:, :], data1=sl(sw, c), initial=0.0, op0=ALU.mult, op1=ALU.add)
                    return r
                sc.op("dve", fcum, reads=[sw.b, onesf.b], writes=[cum.b])
                yield
                ck("A5d")
                sc.op("act", lambda e: e.activation(out=gam[:, :], in_=cum[:, :], func=AF.Exp, scale=-EXPM05), reads=[cum.b], writes=[gam.b])
                yield
                sc.op("act", lambda e: e.activation(out=v3(igam), in_=cumv[:, :, 1:129], func=AF.Exp, scale=EXPM05), reads=[cum.b], writes=[igam.b])
                yield
                ck("A5e")
                sc.op("dve", lambda e: e.tensor_copy(out=gC[:, :], in_=gamv[:, :, 128]), reads=[gam.b], writes=[gC.b])
                yield
                ck("A6")

                def fkq(e):
                    r = None
                    for c in range(4):
                        r = e.activation(out=sl(kq, c), in_=zpv[:, 4 + c, :], func=AF.Square, scale=pc(PC_KK + c))
                    return r
                sc.op("act", fkq, reads=[zp.b, pcol.b], writes=[kq.b])
                yield
                pn = vbank()
                sc.op("pe", lambda e, pn=pn: e.matmul(pn[:, :], lhsT=cst[:, CS_BD4:CS_BD4 + 128], rhs=kq[:, :], start=True, stop=True),
                      reads=[cst.b, kq.b], writes=[pn.b])
                yield
                sc.op("dve", lambda e, pn=pn: e.tensor_scalar(out=nrm[:, :], in0=pn[:, :], scalar1=1e-24, scalar2=None, op0=ALU.max), reads=[pn.b], writes=[nrm.b])
                yield
                sc.op("act", lambda e: e.activation(out=t2[:, :], in_=nrm[:, :], func=AF.Ln), reads=[nrm.b], writes=[t2.b])
                yield
                sc.op("act", lambda e: e.activation(out=inv[:, :], in_=t2[:, :], func=AF.Exp, scale=-0.5), reads=[t2.b], writes=[inv.b])
                yield

                def fkk(e):
                    r = None
                    for c in range(4):
                        r = e.scalar_tensor_tensor(out=sl(kk, c), in0=zpv[:, 4 + c, :], scalar=pc(PC_KK + c), in1=sl(inv, c), op0=ALU.mult, op1=ALU.mult)
                    return r
                sc.op("dve", fkk, reads=[zp.b, pcol.b, inv.b], writes=[kk.b])
                yield

                def ft1(e):
                    r = None
                    for c in range(4):
                        r = e.tensor_scalar(out=sl(t1, c), in0=sl(a_, c), scalar1=-1.0, scalar2=pc(PC_KA + c), op0=ALU.add, op1=ALU.mult)
                    return r
                sc.op("dve", ft1, reads=[a_.b, pcol.b], writes=[t1.b])
                yield
                sc.op("dve", lambda e: e.scalar_tensor_tensor(out=kmod[:, :], in0=t1[:, :], scalar=1.0, in1=zp[:, 512:1024], op0=ALU.add, op1=ALU.mult),
                      reads=[t1.b, zp.b], writes=[kmod.b])
                yield
                ck("A7")
                sc.op("pool", lambda e: e.tensor_tensor(out=RAv[:, :, 1, :], in0=zpv[:, 0:4, :], in1=gamv[:, :, 1:129], op=ALU.mult),
                      reads=[zp.b, gam.b], writes=[RA.b])
                yield
                sc.op("pool", lambda e: e.tensor_tensor(out=RAv[:, :, 0, :], in0=v3(kk), in1=gamv[:, :, 0:128], op=ALU.mult),
                      reads=[kk.b, gam.b, RA.b], writes=[RA.b])
                yield
                sc.op("pool", lambda e: e.tensor_tensor(out=KTf[:, :], in0=kmod[:, :], in1=igam[:, :], op=ALU.mult), reads=[kmod.b, igam.b], writes=[KTf.b])
                yield
                sc.op("pool", lambda e: e.tensor_tensor(out=t2[:, :], in0=kk[:, :], in1=a_[:, :], op=ALU.mult), reads=[kk.b, a_.b], writes=[t2.b])
                yield
                sc.op("dve", lambda e: e.scalar_tensor_tensor(out=BTf[:, :], in0=t2[:, :], scalar=-1.0, in1=igam[:, :], op0=ALU.mult, op1=ALU.mult),
                      reads=[t2.b, igam.b], writes=[BTf.b])
                yield
                sc.op("act", lambda e: e.activation(out=vb[:, :], in_=zp[:, 1024:1536], func=AF.Identity), reads=[zp.b], writes=[vb.b])
                yield

                def frk(e):
                    r = None
                    for c in range(4):
                        r = e.scalar_tensor_tensor(out=sl(rkp, c), in0=zpv[:, c, :], scalar=pc(PC_RK + c), in1=sl(kmod, c), op0=ALU.mult, op1=ALU.mult)
                    return r
                sc.op("dve", frk, reads=[zp.b, pcol.b, kmod.b], writes=[rkp.b])
                yield
                pbo = vbank()
                sc.op("pe", lambda e, pbo=pbo: e.matmul(pbo[:, :], lhsT=cst[:, CS_BD4:CS_BD4 + 128], rhs=rkp[:, :], start=True, stop=True),
                      reads=[cst.b, rkp.b], writes=[pbo.b])
                yield
                sc.op("dve", lambda e, pbo=pbo: e.tensor_tensor(out=bonus[:, :], in0=pbo[:, :], in1=zp[:, 1024:1536], op=ALU.mult),
                      reads=[pbo.b, zp.b], writes=[bonus.b])
                yield
                ck("A8")
                vp16 = vbank()
                p16, p16b = vp16.as16, vp16.b

                def ftr(e, p16=p16):
                    r = None
                    for c in range(4):
                        e.transpose(out=p16[:, c * 128:(c + 1) * 128], in_=sl(KTf, c), identity=identb[:, :])
                    for c in range(4):
                        r = e.transpose(out=p16[:, 512 + c * 128:512 + (c + 1) * 128], in_=sl(BTf, c), identity=identb[:, :])
                    return r
                sc.op("pe", ftr, reads=[KTf.b, BTf.b, identb.b], writes=[p16b])
                yield
                sc.op("act", lambda e, p16=p16: e.activation(out=KBt[:, :], in_=p16[:, 0:1024], func=AF.Identity), reads=[p16b], writes=[KBt.b])
                yield

                def ftv(e, p16=p16):
                    r = None
                    for c in range(4):
                        r = e.transpose(out=p16[:, c * 128:(c + 1) * 128], in_=sl(vb, c), identity=identb[:, :])
                    return r
                sc.op("pe", ftv, reads=[vb.b, identb.b], writes=[p16b])
                yield
                sc.op("dve", lambda e, p16=p16: e.tensor_copy(out=Vt[:, :], in_=p16[:, 0:512]), reads=[p16b], writes=[Vt.b])
                yield

            def tail(i):
                AM = AM_2[i % 2]
                AMv = AM[:, :].rearrange("p (h q t) -> p h q t", h=8, q=4)
                RA, KTf, BTf, KBt, Vt, gT, bonus, gC = RA_2[i % 2], KTf_2[i % 2], BTf_2[i % 2], KBt_2[i % 2], Vt_2[i % 2], gT_2[i % 2], bonus_2[i % 2], gC_2[i % 2]
                RAv = RA[:, :].rearrange("p (c q t) -> p c q t", c=4, q=2)
                ya = yab[i % 2]
                ck("A9")
                for c in range(4):
                    bx, by = vbank(), vbank()

                    def fA(e, c=c, bx=bx, by=by):
                        r = None
                        for (p0, bk) in ((0, bx), (64, by)):
                            rhs = RA[p0:p0 + 64, c * 256:(c + 1) * 256]
                            e.matmul(bk[:, 0:256], lhsT=BTf[p0:p0 + 64, c * 128:(c + 1) * 128], rhs=rhs, start=True, stop=True)
                            r = e.matmul(bk[:, 256:512], lhsT=KTf[p0:p0 + 64, c * 128:(c + 1) * 128], rhs=rhs, start=True, stop=True)
                        return r
                    sc.op("pe", fA, reads=[BTf.b, KTf.b, RA.b], writes=[bx.b, by.b])
                    yield
                    for (h, bk) in ((2 * c, bx), (2 * c + 1, by)):
                        sc.op("act", lambda e, h=h, bk=bk: e.activation(out=AM[:, h * 512:(h + 1) * 512], in_=bk[:, :], func=AF.Identity),
                              reads=[bk.b, AM.b], writes=[AM.b])
                        yield
                        sc.op("pool", lambda e, h=h: e.tensor_tensor(out=AM[:, h * 512:(h + 1) * 512], in0=AM[:, h * 512:(h + 1) * 512],
                                                                    in1=m4b[:, :], op=ALU.mult),
                              reads=[m4b.b, AM.b], writes=[AM.b])
                        yield
                be, bo = vbank(), vbank()

                def fNL(e, be=be, bo=bo):
                    r = None
                    for h in range(8):
                        c, p0 = h // 2, (h % 2) * 64
                        bk = be if h % 2 == 0 else bo
                        r = e.matmul(sl(bk, c), lhsT=RA[p0:p0 + 64, c * 256:c * 256 + 128], rhs=BTf[p0:p0 + 64, c * 128:(c + 1) * 128], start=True, stop=True)
                    return r
                sc.op("pe", fNL, reads=[RA.b, BTf.b], writes=[be.b, bo.b])
                yield
                PT0v = PTp[0][:, :].rearrange("p (c two t) -> p c two t", c=4, two=2)
                ml4 = cst[:, CS_ML4:CS_ML4 + 512].rearrange("p (c t) -> p c t", c=4)
                sc.op("dve", lambda e, be=be: e.tensor_tensor(out=PT0v[:, :, 0, :], in0=v3(be), in1=ml4, op=ALU.mult),
                      reads=[be.b, cst.b, PTp[0].b], writes=[PTp[0].b])
                yield
                sc.op("dve", lambda e, bo=bo: e.tensor_tensor(out=PT0v[:, :, 1, :], in0=v3(bo), in1=ml4, op=ALU.mult),
                      reads=[bo.b, cst.b, PTp[0].b], writes=[PTp[0].b])
                yield
                sc.op("act", lambda e: e.activation(out=v3(Pp[0], 8), in_=AMv[:, :, 0, :], func=AF.Identity), reads=[AM.b], writes=[Pp[0].b])
                yield
                ck("A10")
                pU = vbank()

                KV = os.environ.get("KVAR", "")

                def fU0(e, pU=pU):
                    r = None
                    for c in range(4):
                        if KV != "noS":
                            r = e.matmul(sl(pU, c), lhsT=RA[:, c * 256:c * 256 + 128], rhs=sl(Sbd, c), start=True, stop=(KV == "allstart"))
                        if KV == "noV":
                            continue
                        for h in (2 * c, 2 * c + 1):
                            r = e.matmul(sl(pU, h, 64), lhsT=AM[:, h * 512 + 256:h * 512 + 384], rhs=sl(Vt, h, 64),
                                         start=(KV in ("allstart", "noS")), stop=(h == 2 * c + 1) or KV in ("allstart", "noS"))
                    return r
                sc.op("pe", fU0, reads=[RA.b, Sbd.b, AM.b, Vt.b], writes=[pU.b])
                yield
                if KV == "noevac":
                    ck("A11")
                sc.op("act", lambda e, pU=pU: e.activation(out=Xf[0][:, :], in_=pU[:, :], func=AF.Identity), reads=[pU.b], writes=[Xf[0].b])
                yield
                if KV == "noevac2":
                    ck("A11")
                sc.op("dve", lambda e, pU=pU: e.tensor_copy(out=Xb[0][:, :], in_=pU[:, :]), reads=[pU.b], writes=[Xb[0].b])
                yield
                ck("A11")
                for k in range(7):
                    P, PT = Pp[k], PTp[k % 2]
                    xb_in, xf_in, xb_out, xf_out = Xb[k % 2], Xf[k % 2], Xb[(k + 1) % 2], Xf[(k + 1) % 2]
                    pX = vbank()

                    def fX(e, P=P, xb_in=xb_in, pX=pX):
                        r = None
                        for h in range(8):
                            r = e.matmul(sl(pX, h, 64), lhsT=sl(P, h), rhs=sl(xb_in, h, 64), start=True, stop=True)
                        return r
                    sc.op("pe", fX, reads=[P.b, xb_in.b], writes=[pX.b])
                    yield
                    sc.op("dve", lambda e, pX=pX, xf_in=xf_in, xb_out=xb_out: e.tensor_tensor(out=xb_out[:, :], in0=pX[:, :], in1=xf_in[:, :], op=ALU.add),
                          reads=[pX.b, xf_in.b], writes=[xb_out.b])
                    yield
                    if k < 6:
                        sc.op("dve", lambda e, pX=pX, xf_in=xf_in, xf_out=xf_out: e.tensor_tensor(out=xf_out[:, :], in0=pX[:, :], in1=xf_in[:, :], op=ALU.add),
                              reads=[pX.b, xf_in.b], writes=[xf_out.b])
                        yield
                    if k < 6:
                        Pn = Pp[k + 1]
                        q0, q1 = vbank(), vbank()

                        def fP(e, P=P, PT=PT, q0=q0, q1=q1):
                            r = None
                            for h in range(8):
                                r = e.matmul(sl(q0 if h < 4 else q1, h % 4), lhsT=sl(PT, h), rhs=sl(P, h), start=True, stop=True)
                            return r
                        sc.op("pe", fP, reads=[P.b, PT.b], writes=[q0.b, q1.b])
                        yield
                        sc.op("act", lambda e, Pn=Pn, q0=q0: e.activation(out=Pn[:, 0:512], in_=q0[:, :], func=AF.Identity), reads=[q0.b, Pn.b], writes=[Pn.b])
                        yield
                        sc.op("dve", lambda e, Pn=Pn, q1=q1: e.tensor_copy(out=Pn[:, 512:1024], in_=q1[:, :]), reads=[q1.b, Pn.b], writes=[Pn.b])
                        yield
                    if k < 5:
                        PTn = PTp[(k + 1) % 2]
                        q2, q3 = vbank(), vbank()

                        def fPT(e, P=P, PT=PT, q2=q2, q3=q3):
                            r = None
                            for h in range(8):
                                r = e.matmul(sl(q2 if h < 4 else q3, h % 4), lhsT=sl(P, h), rhs=sl(PT, h), start=True, stop=True)
                            return r
                        sc.op("pe", fPT, reads=[P.b, PT.b], writes=[q2.b, q3.b])
                        yield
                        sc.op("act", lambda e, PTn=PTn, q2=q2: e.activation(out=PTn[:, 0:512], in_=q2[:, :], func=AF.Identity), reads=[q2.b, PTn.b], writes=[PTn.b])
                        yield
                        sc.op("act", lambda e, PTn=PTn, q3=q3: e.activation(out=PTn[:, 512:1024], in_=q3[:, :], func=AF.Identity), reads=[q3.b, PTn.b], writes=[PTn.b])
                        yield
                Ub = Xb[1]
                ck("A12")
                pY = vbank()

                def fY(e, pY=pY):
                    r = None
                    for c in range(4):
                        e.matmul(sl(pY, c), lhsT=RA[:, c * 256 + 128:c * 256 + 256], rhs=sl(Sbd, c), start=True, stop=False)
                        for h in (2 * c, 2 * c + 1):
                            e.matmul(sl(pY, h, 64), lhsT=AM[:, h * 512 + 128:h * 512 + 256], rhs=sl(Ub, h, 64), start=False, stop=False)
                            r = e.matmul(sl(pY, h, 64), lhsT=AM[:, h * 512 + 384:h * 512 + 512], rhs=sl(Vt, h, 64), start=False, stop=(h == 2 * c + 1))
                    return r
                sc.op("pe", fY, reads=[RA.b, Sbd.b, AM.b, Ub.b, Vt.b], writes=[pY.b])
                yield
                pS2 = vbank()

                def fS(e, pS2=pS2):
                    r = None
                    for c in range(4):
                        e.matmul(sl(pS2, c), lhsT=KBt[:, 512 + c * 128:512 + (c + 1) * 128], rhs=sl(Ub, c), start=True, stop=False)
                        r = e.matmul(sl(pS2, c), lhsT=sl(KBt, c), rhs=sl(Vt, c), start=False, stop=True)
                    return r
                sc.op("pe", fS, reads=[KBt.b, Ub.b, Vt.b], writes=[pS2.b])
                yield
                sc.op("dve", lambda e, pS2=pS2: e.tensor_tensor(out=st1[:, :], in0=pS2[:, :], in1=cst[:, CS_BD4:CS_BD4 + 512], op=ALU.mult),
                      reads=[pS2.b, cst.b], writes=[st1.b])
                yield
                sc.op("pool", lambda e: e.tensor_tensor(out=st2[:, :], in0=st1[:, :], in1=Sf[:, :], op=ALU.add), reads=[st1.b, Sf.b], writes=[st2.b])
                yield

                def fSf(e):
                    r = None
                    for c in range(4):
                        r = e.tensor_scalar(out=sl(Sf, c), in0=sl(st2, c), scalar1=gC[:, c:c + 1], scalar2=None, op0=ALU.mult)
                    return r
                sc.op("dve", fSf, reads=[st2.b, gC.b], writes=[Sf.b])
                yield
                sc.op("act", lambda e: e.activation(out=Sbd[:, :], in_=Sf[:, :], func=AF.Identity), reads=[Sf.b], writes=[Sbd.b])
                yield
                ck("A13")

                def fgs(e, pY=pY):
                    r = None
                    for h in range(8):
                        r = e.bn_stats(out=gst[:, h * 6:h * 6 + 6], in_=sl(pY, h, 64))
                    return r
                sc.op("dve", fgs, reads=[pY.b], writes=[gst.b])
                yield

                def fga(e):
                    r = None
                    for h in range(8):
                        r = e.bn_aggr(out=gmv[:, 2 * h:2 * h + 2], in_=gst[:, h * 6:h * 6 + 6])
                    return r
                sc.op("dve", fga, reads=[gst.b], writes=[gmv.b])
                yield
                gmvv = gmv[:, :].rearrange("p (h two) -> p h two", two=2)
                sc.op("pool", lambda e: e.tensor_scalar(out=gsd[:, :], in0=gmvv[:, :, 1], scalar1=GN_EPS, scalar2=None, op0=ALU.add),
                      reads=[gmv.b], writes=[gsd.b])
                yield
                sc.op("pool", lambda e: e.tensor_tensor(out=grs[:, :], in0=gsd[:, :], in1=negh[:, :], op=ALU.pow), reads=[gsd.b, negh.b], writes=[grs.b])
                yield

                def fyn(e, pY=pY):
                    r = None
                    for h in range(8):
                        r = e.tensor_scalar(out=sl(ynb, h, 64), in0=sl(pY, h, 64), scalar1=gmv[:, 2 * h:2 * h + 1], scalar2=grs[:, h:h + 1],
                                            op0=ALU.subtract, op1=ALU.mult)
                    return r
                sc.op("dve", fyn, reads=[pY.b, gmv.b, grs.b], writes=[ynb.b])
                yield
                vp16 = vbank()
                p16, p16b = vp16.as16, vp16.b

                def fty(e, p16=p16):
                    r = None
                    for c in range(4):
                        r = e.transpose(out=p16[:, c * 128:(c + 1) * 128], in_=sl(ynb, c), identity=identb[:, :])
                    return r
                sc.op("pe", fty, reads=[ynb.b, identb.b], writes=[p16b])
                yield

                def fyt(e, p16=p16):
                    r = None
                    for c in range(4):
                        r = e.tensor_scalar(out=sl(yt, c), in0=p16[:, c * 128:(c + 1) * 128], scalar1=pc(PC_GNG + c), scalar2=pc(PC_GNB + c),
                                            op0=ALU.mult, op1=ALU.add)
                    return r
                sc.op("dve", fyt, reads=[p16b, pcol.b], writes=[yt.b])
                yield
                sc.op("pool", lambda e: e.tensor_tensor(out=yt2[:, :], in0=yt[:, :], in1=bonus[:, :], op=ALU.add), reads=[yt.b, bonus.b], writes=[yt2.b])
                yield
                sc.op("pool", lambda e, ya=ya: e.tensor_tensor(out=ya[:, 512:1024], in0=yt2[:, :], in1=gT[:, :], op=ALU.mult),
                      reads=[yt2.b, gT.b, ya.b], writes=[ya.b])
                yield
                sc.dma(yab_d[i * 128:(i + 1) * 128, :], ya[:, :], ya.b, reads=[ya.b], writes=[yab_db[i]])
                yield
                yield

            def rr(gens, steps):
                gens = list(gens)
                steps = list(steps)
                while gens:
                    for j in range(len(gens) - 1, -1, -1):
                        for _ in range(steps[j]):
                            try:
                                next(gens[j])
                            except StopIteration:
                                gens.pop(j)
                                steps.pop(j)
                                break
            load_x(0)
            if NT > 1:
                load_x(1)
            rr([front_h(0)], [1])
            rr([front_g(0), front_r(0)], [1, 2])
            for i in range(NT):
                gens, steps = [tail(i)], [1]
                if i + 1 < NT:
                    rr([front_h(i + 1)], [1])
                    if i + 2 < NT:
                        load_x(i + 2)
                    gens += [front_g(i + 1), front_r(i + 1)]
                    steps += [1, 2]
                rr(gens, steps)
            sc.barrier()
            sc.flush()
        if os.environ.get("KSTOP") == "A":
            return nc

        def bcast_rows(st, tmp, name, col0, dg=None, of_=None):
            tl = st
            dg = dg if dg is not None else mkT(tmp, name + "_dg", [128, D])
            ofT = of_ if of_ is not None else mkT(tmp, name + "_on", [128, 128])

            class _OF:
                b = ofT.b

                def __getitem__(self, idx):
                    return ofT[:, 0:128]
            of = _OF()
            sc.op("pool", lambda e: e.memset(of[:, :], 1.0), writes=[of.b])

            def fdg(e):
                r = None
                for m in range(8):
                    r = e.tensor_scalar(out=dg[:, m * 128:(m + 1) * 128], in0=ident, scalar1=md(col0 + m), scalar2=None, op0=ALU.mult)
                return r
            sc.op("dve", fdg, reads=[cst.b, modT.b], writes=[dg.b])
            q0, q1 = vbank(), vbank()

            def fbc(e):
                r = None
                for m in range(8):
                    r = e.matmul((q0 if m < 4 else q1)[:, (m % 4) * 128:(m % 4 + 1) * 128], lhsT=of[:, :], rhs=dg[:, m * 128:(m + 1) * 128], start=True, stop=True)
                return r
            sc.op("pe", fbc, reads=[of.b, dg.b], writes=[q0.b, q1.b])
            sc.op("dve", lambda e: e.tensor_copy(out=tl[:, 0:512], in_=q0[:, :]), reads=[q0.b], writes=[tl.b])
            sc.op("dve", lambda e: e.tensor_copy(out=tl[:, 512:1024], in_=q1[:, :]), reads=[q1.b, tl.b], writes=[tl.b])
            return tl

        def ln_rows(st, name, r):
            tl = st
            sc.dma(tl[:, :], lnrow_d[r:r + 1, :].partition_broadcast(128), tl.b, writes=[tl.b])
            return tl

        kpf = lambda d_: d_.rearrange("(k p) f -> p k f", p=128)
        sc.prio = "old"
        sc.reserve = 8
        NS = NT // 2
        W2 = 256
        sl = lambda t_, c, w=128: t_[:, c * w:(c + 1) * w]

        NPRE = 5
        w1pre = mkT(G, "w1pre", [128, NPRE * 4096], BF16)
        with contextlib.ExitStack() as PB:
            winG = mkT(PB, "winG", [128, 8 * 2048], BF16)
            wA = mkT(PB, "wA", [128, 4 * 1024], BF16)
            wB = mkT(PB, "wB", [128, 4 * 1024], BF16)
            wO = mkT(PB, "wO", [128, 8 * 1024], BF16)
            gt1bc = mkT(PB, "gt1bc", [128, D])
            ln1g = mkT(PB, "ln1g", [128, D])
            ln1b = mkT(PB, "ln1b", [128, D])
            xt = [mkT(PB, "xtB%d" % i, [128, 2 * D]) for i in range(2)]
            yin = [mkT(PB, "yin%d" % i, [128, 2 * D], BF16) for i in range(2)]
            hT = mkT(PB, "hTB", [128, 8 * W2], BF16)
            sigG = mkT(PB, "sigG", [128, 8 * W2])
            tAB = [mkT(PB, "tAB%d" % i, [128, 8 * W2]) for i in range(2)]
            mg = mkT(PB, "mg", [128, 8 * W2], BF16)
            tM = mkT(PB, "tM", [128, D])
            res = mkT(PB, "res", [128, D])
            h1o = [mkT(PB, "h1o%d" % i, [128, D]) for i in range(2)]
            smb = [mkT(PB, "smB%d" % i, [128, 16]) for i in range(4)]
            load_w_cast(winG, kpf(win_d)[:, :, 2720:4768], 8, 2048)
            load_w_cast(wA, kpf(wa_d), 4, 1024)
            load_w_cast(wB, kpf(wb_d), 4, 1024)
            load_w_cast(wO, kpf(wout_d), 8, 1024)
            bcast_rows(gt1bc, None, "gt1bc", GT1, dg=h1o[0], of_=h1o[1])
            ln_rows(ln1g, "ln1g", 0)
            ln_rows(ln1b, "ln1b", 1)
            xv = lambda t_: t_[:, :].rearrange("p (s d) -> p s d", s=2)

            def load_b(s_):
                sc.dma(xv(xt[s_ % 2]), x_d[s_ * 256:(s_ + 1) * 256, :].rearrange("(s p) d -> p s d", p=128), xt[s_ % 2].b, writes=[xt[s_ % 2].b])
                sc.dma(xv(yin[s_ % 2]), yab_d[s_ * 256:(s_ + 1) * 256, :].rearrange("(s p) d -> p s d", p=128), yin[s_ % 2].b,
                       reads=[yab_db[2 * s_], yab_db[2 * s_ + 1]], writes=[yin[s_ % 2].b])

            def hT_b(s_):
                for sub in range(2):
                    x_to_hT(None, xt[s_ % 2], xt[s_ % 2].b, hT, 0, OPS1, SH1, tok0=sub * 128, col0=sub * D)
            load_b(0)
            hT_b(0)
            load_w_cast(w1pre, kpf(wff1_d)[:, 0:NPRE, :], NPRE, 4096)
            for s_ in range(NS):
                xs, yi = xt[s_ % 2], yin[s_ % 2]
                if s_ + 1 < NS:
                    load_b(s_ + 1)
                hk = lambda k: hT[:, k * W2:(k + 1) * W2]
                for br, (w_, yoff) in enumerate(((wA, 0), (wB, 4))):
                    pgs = [vbank() for _ in range(4)]

                    def fg(e, br=br, pgs=pgs):
                        r = None
                        for j in range(8):
                            for k in range(8):
                                c0 = k * 2048 + (br * 8 + j) * 128
                                r = e.matmul(sl(pgs[j // 2], j % 2, W2), lhsT=winG[:, c0:c0 + 128], rhs=hk(k), start=(k == 0), stop=(k == 7))
                        return r
                    sc.op("pe", fg, reads=[winG.b, hT.b], writes=[p_.b for p_ in pgs])
                    for q in range(4):
                        def fsg(e, br=br, q=q, pgs=pgs):
                            r = None
                            for jj in range(2):
                                j = q * 2 + jj
                                r = e.activation(out=sl(sigG, j, W2), in_=sl(pgs[q], jj, W2), func=AF.Sigmoid, bias=pc(PC_BGATE + br * 8 + j), scale=1.0)
                            return r
                        sc.op("act", fsg, reads=[pgs[q].b, pcol.b, sigG.b], writes=[sigG.b])
                    pbs = [vbank() for _ in range(4)]

                    def fbr(e, w_=w_, yoff=yoff, pbs=pbs, yi=yi):
                        r = None
                        for dch in range(8):
                            for k in range(4):
                                for sub in range(2):
                                    r = e.matmul(pbs[dch // 2][:, (dch % 2) * W2 + sub * 128:(dch % 2) * W2 + (sub + 1) * 128],
                                                 lhsT=w_[:, k * 1024 + dch * 128:k * 1024 + (dch + 1) * 128],
                                                 rhs=yi[:, sub * D + (yoff + k) * 128:sub * D + (yoff + k + 1) * 128], start=(k == 0), stop=(k == 3))
                        return r

                    def fbr2(e, w_=w_, yoff=yoff, pbs=pbs, yi=yi):
                        r = None
                        for dch in range(8):
                            for sub in range(2):
                                for k in range(4):
                                    r = e.matmul(pbs[dch // 2][:, (dch % 2) * W2 + sub * 128:(dch % 2) * W2 + (sub + 1) * 128],
                                                 lhsT=w_[:, k * 1024 + dch * 128:k * 1024 + (dch + 1) * 128],
                                                 rhs=yi[:, sub * D + (yoff + k) * 128:sub * D + (yoff + k + 1) * 128], start=(k == 0), stop=(k == 3))
                        return r
                    sc.op("pe", fbr2, reads=[w_.b, yi.b], writes=[p_.b for p_ in pbs])
                    dst = tAB[br]
                    for q in range(4):
                        sc.op("dve", lambda e, dst=dst, q=q, pbs=pbs: e.tensor_tensor(out=sl(dst, q, 512), in0=pbs[q][:, :], in1=sl(sigG, q, 512), op=ALU.mult),
                              reads=[pbs[q].b, sigG.b, dst.b], writes=[dst.b])
                sc.op("pool", lambda e: e.tensor_tensor(out=mg[:, :], in0=tAB[0][:, :], in1=tAB[1][:, :], op=ALU.add), reads=[tAB[0].b, tAB[1].b], writes=[mg.b])
                if s_ + 1 < NS:
                    hT_b(s_ + 1)
                for sub in range(2):
                    i = 2 * s_ + sub
                    ho = h1o[i % 2]
                    qm = [vbank(), vbank()]

                    def fmx(e, sub=sub, qm=qm):
                        r = None
                        for hf in range(2):
                            for k in range(8):
                                e.matmul(qm[hf][:, :], lhsT=mg[:, k * W2 + sub * 128:k * W2 + (sub + 1) * 128],
                                         rhs=wO[:, k * 1024 + hf * 512:k * 1024 + (hf + 1) * 512], start=(k == 0), stop=False)
                            r = e.matmul(qm[hf][:, :], lhsT=onesb[0:1, 0:128], rhs=browb[0:1, 512 + hf * 512:512 + (hf + 1) * 512], start=False, stop=True)
                        return r
                    sc.op("pe", fmx, reads=[mg.b, wO.b, onesb.b, browb.b], writes=[qm[0].b, qm[1].b])
                    for hf in range(2):
                        sc.op("dve", lambda e, hf=hf, qm=qm: e.tensor_tensor(out=tM[:, hf * 512:(hf + 1) * 512], in0=qm[hf][:, :], in1=gt1bc[:, hf * 512:(hf + 1) * 512], op=ALU.mult),
                              reads=[qm[hf].b, gt1bc.b, tM.b], writes=[tM.b])
                    sc.op("dve", lambda e, xs=xs, sub=sub: e.scalar_tensor_tensor(out=res[:, :], in0=xs[:, sub * D:(sub + 1) * D], scalar=ALPHA, in1=tM[:, :], op0=ALU.mult, op1=ALU.add),
                          reads=[xs.b, tM.b], writes=[res.b])
                    layernorm_rows(res, tM, ln1g, ln1b, smb, ho)
                    sc.dma(h1_d[i * 128:(i + 1) * 128, :], ho[:, :], ho.b, reads=[ho.b], writes=[h1_db[i]])
            sc.barrier()
            sc.flush()
        if os.environ.get("KSTOP") == "B":
            return nc

        with contextlib.ExitStack() as PC:
            w1b = mkT(PC, "w1b", [128, (8 - NPRE) * 4096], BF16)
            w2 = mkT(PC, "w2", [128, 32 * 1024], BF16)
            gt2bc = mkT(PC, "gt2bc", [128, D])
            ln2g = mkT(PC, "ln2g", [128, D])
            ln2b = mkT(PC, "ln2b", [128, D])
            hin = [mkT(PC, "hin%d" % i, [128, 2 * D]) for i in range(2)]
            hT2 = mkT(PC, "hT2", [128, 8 * W2], BF16)
            rl = [mkT(PC, "rl%d" % i, [128, 512]) for i in range(2)]
            hid = mkT(PC, "hid", [128, 32 * W2], BF16)
            tC = mkT(PC, "tC", [128, D])
            oo = [mkT(PC, "oo%d" % i, [128, D]) for i in range(2)]
            smc = [mkT(PC, "smC%d" % i, [128, 16]) for i in range(4)]
            load_w_cast(w1b, kpf(wff1_d)[:, NPRE:8, :], 8 - NPRE, 4096)
            load_w_cast(w2, kpf(wff2_d), 32, 1024)
            bcast_rows(gt2bc, None, "gt2bc", GT2, dg=oo[0], of_=oo[1])
            ln_rows(ln2g, "ln2g", 2)
            ln_rows(ln2b, "ln2b", 3)
            xv = lambda t_: t_[:, :].rearrange("p (s d) -> p s d", s=2)

            def load_c(s_):
                sc.dma(xv(hin[s_ % 2]), h1_d[s_ * 256:(s_ + 1) * 256, :].rearrange("(s p) d -> p s d", p=128), hin[s_ % 2].b,
                       reads=[h1_db[2 * s_], h1_db[2 * s_ + 1]], writes=[hin[s_ % 2].b])

            def hT_c(s_):
                for sub in range(2):
                    x_to_hT(None, hin[s_ % 2], hin[s_ % 2].b, hT2, 0, OPS2, SH2, tok0=sub * 128, col0=sub * D)
            load_c(0)
            if NS > 1:
                load_c(1)
            hT_c(0)
            for s_ in range(NS):
                hs = hin[s_ % 2]
                hk = lambda k: hT2[:, k * W2:(k + 1) * W2]
                for fq in range(16):
                    pf = vbank()
                    rr_ = rl[fq % 2]

                    def ff1(e, fq=fq, pf=pf):
                        r = None
                        for j in range(2):
                            f_ = fq * 2 + j
                            for k in range(8):
                                wt, kk_ = (w1pre, k) if k < NPRE else (w1b, k - NPRE)
                                r = e.matmul(sl(pf, j, W2), lhsT=wt[:, kk_ * 4096 + f_ * 128:kk_ * 4096 + (f_ + 1) * 128], rhs=hk(k), start=(k == 0), stop=(k == 7))
                        return r
                    sc.op("pe", ff1, reads=[w1pre.b, w1b.b, hT2.b], writes=[pf.b])

                    def frl(e, fq=fq, pf=pf, rr_=rr_):
                        r = None
                        for j in range(2):
                            r = e.activation(out=sl(rr_, j, W2), in_=sl(pf, j, W2), func=AF.Relu, bias=pc(PC_BFF1 + fq * 2 + j), scale=1.0)
                        return r
                    sc.op("act", frl, reads=[pf.b, pcol.b], writes=[rr_.b])
                    sc.op("pool", lambda e, fq=fq, rr_=rr_: e.tensor_tensor(out=hid[:, fq * 512:(fq + 1) * 512], in0=rr_[:, :], in1=rr_[:, :], op=ALU.mult),
                          reads=[rr_.b, hid.b], writes=[hid.b])
                if s_ + 1 < NS:
                    hT_c(s_ + 1)
                for sub in range(2):
                    i = 2 * s_ + sub
                    ot = oo[i % 2]
                    qo = [vbank(), vbank()]

                    def ff2(e, sub=sub, qo=qo):
                        r = None
                        for hf in range(2):
                            for f_ in range(32):
                                e.matmul(qo[hf][:, :], lhsT=hid[:, f_ * W2 + sub * 128:f_ * W2 + (sub + 1) * 128],
                                         rhs=w2[:, f_ * 1024 + hf * 512:f_ * 1024 + (hf + 1) * 512], start=(f_ == 0), stop=False)
                            r = e.matmul(qo[hf][:, :], lhsT=onesb[0:1, 0:128], rhs=browb[0:1, 1536 + hf * 512:1536 + (hf + 1) * 512], start=False, stop=True)
                        return r
                    sc.op("pe", ff2, reads=[hid.b, w2.b, onesb.b, browb.b], writes=[qo[0].b, qo[1].b])
                    for hf in range(2):
                        sc.op("dve", lambda e, hf=hf, qo=qo: e.tensor_tensor(out=tC[:, hf * 512:(hf + 1) * 512], in0=qo[hf][:, :], in1=gt2bc[:, hf * 512:(hf + 1) * 512], op=ALU.mult),
                              reads=[qo[hf].b, gt2bc.b, tC.b], writes=[tC.b])
                    sc.op("dve", lambda e, hs=hs, sub=sub: e.scalar_tensor_tensor(out=hs[:, sub * D:(sub + 1) * D], in0=hs[:, sub * D:(sub + 1) * D], scalar=ALPHA, in1=tC[:, :], op0=ALU.mult, op1=ALU.add),
                          reads=[hs.b, tC.b], writes=[hs.b])
                    layernorm_rows(hs, tC, ln2g, ln2b, smc, ot, c0=sub * D)
                    sc.dma(out_d[i * 128:(i + 1) * 128, :], ot[:, :], ot.b, reads=[ot.b], writes=[])
                if s_ + 2 < NS:
                    load_c(s_ + 2)
            sc.barrier()
            sc.flush()
            nc.all_engine_barrier()
    return nc


def build(S, n_tiles_c=2):
    box = []
    try:
        return _build(S, n_tiles_c, box)
    except StopBuild:
        return box[0]


def host_prep(inputs):
    f = lambda a: np.ascontiguousarray(np.asarray(a, np.float32))
    cols = lambda v, n: f(v).reshape(n, 128).T
    b_in, mu = f(inputs["b_in"][0]), f(inputs["mu_shift"][0])
    pcol = np.zeros((128, NPC), np.float32)
    pcol[:, PC_BU:PC_BU + 4] = cols(b_in[0:512], 4)
    pcol[:, PC_BRKV:PC_BRKV + 12] = cols(b_in[1024:2560], 12)
    pcol[0:64, PC_BXWXA] = b_in[2560:2624]
    pcol[0:96, PC_BXG] = b_in[2624:2720]
    pcol[:, PC_BGATE:PC_BGATE + 16] = cols(b_in[2720:4768], 16)
    pcol[:, PC_MURKV:PC_MURKV + 12] = cols(mu[0:1536], 12)
    pcol[0:64, PC_MUXWXA] = mu[1536:1600]
    pcol[0:96, PC_MUXG] = mu[1600:1696]
    for nm, c0 in (("w0", PC_W0), ("a0", PC_A0), ("k_k", PC_KK), ("k_a", PC_KA), ("r_k", PC_RK), ("gn_gain", PC_GNG),
                   ("gn_bias", PC_GNB), ("g_ln_v", PC_GLN), ("b_ln_v", PC_BLN)):
        pcol[:, c0:c0 + 4] = cols(f(inputs[nm][0]).reshape(-1), 4)
    pcol[:, PC_BFF1:PC_BFF1 + 32] = cols(inputs["b_ff1"][0], 32)
    pcol[:, PC_BADA:PC_BADA + 48] = cols(inputs["b_ada"][0], 48)
    s = np.arange(128)
    strict = (s[:, None] < s[None, :]).astype(np.float32)
    incl = (s[:, None] <= s[None, :]).astype(np.float32)
    low = (s[:, None] > s[None, :]).astype(np.float32)
    bd = ((s[:, None] // 64) == (s[None, :] // 64)).astype(np.float32)
    cst = np.zeros((128, NCS), np.float32)
    cst[:, CS_ID:CS_ID + 128] = np.eye(128, dtype=np.float32)
    cst[:, CS_M4:CS_M4 + 512] = np.concatenate([strict, incl, strict, incl], axis=1)
    cst[:, CS_ML4:CS_ML4 + 512] = np.concatenate([low] * 4, axis=1)
    cst[:, CS_BD4:CS_BD4 + 512] = np.concatenate([bd] * 4, axis=1)
    cst[:, CS_ONE] = 1.0
    cst[:, CS_ONE + 1] = LN_EPS
    cst[:, CS_ONE + 2] = GN_EPS
    brow = np.concatenate([b_in[512:1024], f(inputs["b_out"][0]), f(inputs["b_ff2"][0])])[None, :]
    lnrows = np.stack([f(inputs["ln1_g"][0]), f(inputs["ln1_b"][0]), f(inputs["ln2_g"][0]), f(inputs["ln2_b"][0])])
    wsT = f(f(inputs["w_spatial"][0]).transpose(2, 0, 1).reshape(128, 1024))
    bsp = f(inputs["b_spatial"][0]).reshape(4, 2, 128)
    bspb = f(np.repeat(bsp, 64, axis=1).transpose(1, 0, 2).reshape(128, 512))
    lw = f(np.concatenate([f(inputs["w_decay_up"][0]), f(inputs["w_aaa_up"][0])], axis=0))
    shared = {
        "w_ada": f(inputs["w_ada"][0]), "w_in": f(inputs["w_in"][0]), "pcol": pcol, "cst": cst, "brow": f(brow),
        "lnrows": f(lnrows), "wsT": wsT, "bspb": bspb, "lw": lw, "wg": f(inputs["w_gate_up"][0]),
        "w_branch_a": f(inputs["w_branch_a"][0]), "w_branch_b": f(inputs["w_branch_b"][0]), "w_out": f(inputs["w_out"][0]),
        "w_ff1": f(inputs["w_ff1"][0]), "w_ff2": f(inputs["w_ff2"][0]),
    }
    x, c = np.asarray(inputs["x"], np.float32), f(inputs["c"])
    maps = []
    for b in range(x.shape[0]):
        m = dict(shared)
        m["x"] = np.ascontiguousarray(x[b])
        m["ccol"] = f(np.repeat(c[b].reshape(8, 128).T, 2, axis=1))
        maps.append(m)
    return maps


def kernel(**inputs):
    x = np.asarray(inputs["x"])
    B, S, _ = x.shape
    maps = host_prep(inputs)
    nc = build(S)
    res = run_bass_kernel_spmd(nc, maps, core_ids=list(range(B)))
    return np.stack([np.asarray(r["out"]) for r in res.results], axis=0).astype(np.float32)
```

```python
import contextlib
import os
import numpy as np
import concourse.bass as bass
import concourse.mybir as mybir
from concourse.bass_utils import run_bass_kernel_spmd

F32 = mybir.dt.float32
BF16 = mybir.dt.bfloat16
AF = mybir.ActivationFunctionType
ALU = mybir.AluOpType

D = 1024
NCORES = 8
ALPHA = 2.0 ** 0.25
LN_EPS = 1e-5
GN_EPS = 64e-5
EXPM05 = 0.6065306597126334
IN_COLS = 4768
PC_BU, PC_BRKV, PC_BXWXA, PC_BXG, PC_BGATE = 0, 4, 16, 17, 18
PC_MURKV, PC_MUXWXA, PC_MUXG = 34, 46, 47
PC_W0, PC_A0, PC_KK, PC_KA, PC_RK, PC_GNG, PC_GNB, PC_GLN, PC_BLN = 48, 52, 56, 60, 64, 68, 72, 76, 80
PC_BFF1, PC_BADA, NPC = 84, 116, 164
CS_ID, CS_M4, CS_ML4, CS_BD4, CS_ONE, NCS = 0, 128, 640, 1152, 1664, 1792


class StopBuild(Exception):
    pass


class Buf:
    __slots__ = ("name", "writer", "readers", "dsem", "dcount", "ro", "vp")

    def __init__(self, name):
        self.name = name
        self.writer = None
        self.readers = []
        self.dsem = None
        self.dcount = 0
        self.ro = False
        self.vp = None


class Op:
    __slots__ = ("eng", "fn", "deps", "cost", "rid", "epoch", "idx", "dma", "succ", "npend", "ready", "fin", "vps")

    def __init__(self, eng, fn, deps, cost, rid, epoch, dma=None):
        self.eng = eng
        self.fn = fn
        self.deps = deps
        self.cost = cost
        self.rid = rid
        self.epoch = epoch
        self.idx = None
        self.dma = dma
        self.succ = []
        self.npend = 0
        self.ready = 0.0
        self.fin = 0.0
        self.vps = ()


class VP:
    class _V16:
        def __init__(self, vp):
            self.vp = vp

        def __getitem__(self, idx):
            return self.vp.sched.ps16_real[self.vp._k()][idx]

    def __init__(self, sched, n):
        self.sched = sched
        self.b = Buf("vps%d" % n)
        self.b.vp = self
        self.bank = None
        self.remaining = 0
        self.ops = []
        self.as16 = VP._V16(self)

    probing = False

    def _k(self):
        if self.bank is None:
            assert VP.probing, "virtual PSUM bank used before scheduling"
            return 0
        return self.bank

    def __getitem__(self, idx):
        return self.sched.ps_real[self._k()][idx]


DEF_COST = {"pe": 0.6, "act": 0.6, "dve": 0.6, "pool": 1.3, "sp": 3.0}


def _fsize(ap):
    try:
        v = ap.free_size
        v = v() if callable(v) else v
        return int(v)
    except Exception:
        try:
            n = 1
            for d in list(ap.shape)[1:]:
                n *= int(d)
            return n
        except Exception:
            return 512


class _CostProbe:
    def __init__(self, eng):
        self.eng = eng
        self.t = 0.0

    def __getattr__(self, name):
        def f(*a, **kw):
            try:
                if self.eng == "pe":
                    if name == "transpose":
                        self.t += 0.1
                    else:
                        rhs = kw.get("rhs", a[2] if len(a) > 2 else None)
                        n = _fsize(rhs)
                        fp32 = 4.0 if str(getattr(rhs, "dtype", "")).endswith("float32") else 1.0
                        self.t += (0.036 + 0.00036 * max(n, 64)) * fp32
                else:
                    src = kw.get("in_", kw.get("in0", kw.get("data1", kw.get("ap", a[0] if a else None))))
                    n = _fsize(src)
                    if self.eng == "pool":
                        self.t += 0.3 + 0.0019 * n
                    else:
                        k = 0.0021 if name in ("tensor_tensor_scan",) else (0.0064 if name == "reciprocal" else 0.00105)
                        self.t += 0.2 + k * n
            except Exception:
                self.t += DEF_COST.get(self.eng, 0.6)
            return None
        return f


class Sched:
    def __init__(self, nc, stack):
        self.nc = nc
        self.stack = stack
        self.E = {"pe": nc.tensor, "act": nc.scalar, "dve": nc.vector, "pool": nc.gpsimd, "sp": nc.sync}
        self.cnt = {k: 0 for k in self.E}
        self.sem = {k: nc.alloc_semaphore("sem_" + k) for k in ("pe", "act", "dve", "pool")}
        self.waited = {k: {} for k in self.E}
        self.dsems = []
        self.dpool = [nc.alloc_semaphore("dsem%d" % i) for i in range(56)]
        for sm in list(self.sem.values()) + self.dpool:
            nc.gpsimd.sem_clear(sm)
        nc.all_engine_barrier()
        self.ops = []
        self.epoch = 0
        self.rid = 0
        self.prio = "cp"
        self.reserve = 4
        self.bank_free = [0.0] * 8
        self.bank_last = [[] for _ in range(8)]
        self.nvp = 0
        self.ps_real = None
        self.ps16_real = None

    def vbank(self):
        self.nvp += 1
        return VP(self, self.nvp)

    def _deps(self, reads, writes):
        d = []
        self._kinds = {}
        for b in reads:
            if b.writer is not None and b.writer.epoch == self.epoch:
                d.append(b.writer)
                self._kinds[id(b.writer)] = "raw:" + b.name
        nowar = os.environ.get("KSCHED_NOWAR")
        for b in writes:
            if nowar and ((nowar == "1" and not b.name.startswith("ps")) or any(b.name.startswith(p) for p in nowar.split(","))):
                continue
            if b.writer is not None and b.writer.epoch == self.epoch:
                d.append(b.writer)
                self._kinds.setdefault(id(b.writer), "waw:" + b.name)
            for r in b.readers:
                if r.epoch == self.epoch:
                    d.append(r)
                    self._kinds.setdefault(id(r), "war:" + b.name)
        return d

    def _mark(self, op, reads, writes):
        for b in reads:
            if not b.ro:
                b.readers.append(op)
        for b in writes:
            b.writer = op
            b.readers = []

    def op(self, eng, fn, reads=(), writes=(), cost=None):
        ex = [b for b in reads if b.vp is not None]
        if ex:
            reads = [b for b in reads if b.vp is None]
            writes = list(writes) + [b for b in ex if b not in writes]
        self.rid += 1
        if cost is None:
            pr = _CostProbe(eng)
            VP.probing = True
            try:
                fn(pr)
                cost = max(pr.t, 0.1)
            except Exception:
                cost = DEF_COST[eng]
            VP.probing = False
        o = Op(eng, fn, self._deps(reads, writes), cost, self.rid, self.epoch)
        o.succ = self._kinds
        o.vps = list({id(b.vp): b.vp for b in list(reads) + list(writes) if b.vp is not None}.values())
        for vp in o.vps:
            vp.remaining += 1
        self.ops.append(o)
        self._mark(o, reads, writes)

    def dma(self, out, in_, owner, reads=(), writes=(), q="sp", cost=None):
        if owner.dsem is None:
            owner.dsem = self.dpool.pop()
            self.dsems.append(owner)
        owner.dcount += 16
        self.rid += 1
        o = Op(q, None, self._deps(reads, writes), cost if cost is not None else DEF_COST["sp"], self.rid, self.epoch,
               dma=(out, in_, owner.dsem, owner.dcount))
        self.ops.append(o)
        self._mark(o, reads, writes)

    def barrier(self):
        pass

    def _schedule(self):
        ops = self.ops
        kinds = {}
        for o in ops:
            if isinstance(o.succ, dict):
                kinds[id(o)] = o.succ
            o.succ = []
        for o in ops:
            o.deps = list({id(d): d for d in o.deps}.values())
            o.npend = len(o.deps)
            o.ready = 0.0
            for d in o.deps:
                d.succ.append(o)
        prio_mode = os.environ.get("KSCHED_PRIO", self.prio)
        for o in reversed(ops):
            o.fin = o.cost + max([s_.fin + 0.25 for s_ in o.succ] + [0.0])
        tailp = {id(o): (o.fin if prio_mode == "cp" else 0.0) for o in ops}
        if os.environ.get("KSCHED_DBG"):
            print("sched: critical path (infinite engines) = %.1f us" % max([o.fin for o in ops] + [0.0]))
        if os.environ.get("KSCHED_DCP") and len(ops) > 500:
            o = max(ops, key=lambda o: o.fin)
            cnt = {}
            while o.succ:
                nx = max(o.succ, key=lambda s_: s_.fin)
                kd = kinds.get(id(nx), {}).get(id(o), "?")
                if not kd.startswith("raw"):
                    cnt[kd] = cnt.get(kd, 0) + 1
                o = nx
            print("sched: non-RAW edges on dependency critical path:", sorted(cnt.items(), key=lambda kv: -kv[1])[:25])
        order = {k: [] for k in self.E}
        free_at = {k: 0.0 for k in self.E}
        rel = {k: [] for k in self.E}
        spq = [o for o in ops if o.eng == "sp"]
        sp_next = 0
        for o in ops:
            if o.npend == 0 and o.eng != "sp":
                rel[o.eng].append(o)
        nleft = len(ops)
        LAT = 0.25
        bank_free = self.bank_free
        bank_last = self.bank_last
        bank_occ = [None] * 8

        first = {}
        for o in ops:
            for vp in o.vps:
                first.setdefault(id(vp), o)
        allocq = sorted({id(o): o for o in first.values()}.values(), key=lambda o: o.rid)
        apos = {id(o): i for i, o in enumerate(allocq)}
        adone = [False] * len(allocq)
        anext = [0]
        RESERVE = int(os.environ.get("KSCHED_RESERVE", str(self.reserve)))

        def bank_time(o):
            need = [vp for vp in o.vps if vp.bank is None]
            if not need:
                return 0.0
            if id(o) in apos and apos[id(o)] != anext[0]:
                nfree = sum(1 for k in range(8) if bank_occ[k] is None)
                if nfree - len(need) < RESERVE:
                    return None
            free = sorted(bank_free[k] for k in range(8) if bank_occ[k] is None)
            if len(free) < len(need):
                return None
            return free[len(need) - 1]
        while nleft:
            best = None
            for e in self.E:
                if e == "sp":
                    if sp_next < len(spq) and spq[sp_next].npend == 0:
                        o = spq[sp_next]
                        cand = (max(free_at[e], o.ready), o.rid, o)
                    else:
                        continue
                else:
                    if not rel[e]:
                        continue
                    now = free_at[e]
                    est = []
                    for o in rel[e]:
                        bt = bank_time(o)
                        if bt is not None:
                            est.append((max(now, o.ready, bt), o))
                    if not est:
                        continue
                    ready_now = [o for (t_, o) in est if t_ <= now]
                    if ready_now:
                        o = min(ready_now, key=lambda o: (-tailp[id(o)], o.rid))
                        cand = (now, o.rid, o)
                    else:
                        t_, o = min(est, key=lambda to: (to[0], to[1].rid))
                        cand = (t_, o.rid, o)
                if best is None or cand[:2] < best[:2]:
                    best = cand
            assert best is not None, "scheduler deadlock (PSUM banks)"
            st, _, o = best
            e = o.eng
            if id(o) in apos:
                adone[apos[id(o)]] = True
                while anext[0] < len(allocq) and adone[anext[0]]:
                    anext[0] += 1
            for vp in o.vps:
                if vp.bank is None:
                    k = min((k for k in range(8) if bank_occ[k] is None), key=lambda k: bank_free[k])
                    vp.bank = k
                    bank_occ[k] = vp
                    o.deps = o.deps + [d for d in bank_last[k] if d.epoch == self.epoch]
            if e == "sp":
                sp_next += 1
                free_at[e] = st + 0.06
                o.fin = st + o.cost
            else:
                rel[e].remove(o)
                free_at[e] = st + o.cost
                o.fin = st + o.cost
            order[e].append(o)
            nleft -= 1
            for vp in o.vps:
                vp.ops.append(o)
                vp.remaining -= 1
                if vp.remaining == 0:
                    k = vp.bank
                    bank_free[k] = max(a.fin for a in vp.ops)
                    bank_last[k] = list(vp.ops)
                    bank_occ[k] = None
            for s_ in o.succ:
                s_.ready = max(s_.ready, o.fin + LAT)
                s_.npend -= 1
                if s_.npend == 0 and s_.eng != "sp":
                    rel[s_.eng].append(s_)
        if os.environ.get("KSCHED_DBG"):
            print("sched: n=%d makespan=%.1f us busy=%s" % (len(ops), max([o.fin for o in ops] + [0.0]),
                  {k: round(sum(o.cost for o in v), 1) for k, v in order.items()}))
        if os.environ.get("KSCHED_CP") and len(ops) > 500:
            prev_on_eng = {}
            for e, lst in order.items():
                for a, b in zip(lst, lst[1:]):
                    prev_on_eng[id(b)] = a
            o = max(ops, key=lambda o: o.fin)
            path = []
            while o is not None and len(path) < int(os.environ["KSCHED_CP"]):
                st = o.fin - o.cost
                why, nxt = "start", None
                best = None
                for d in o.deps:
                    if best is None or d.fin > best.fin:
                        best = d
                pe_ = prev_on_eng.get(id(o))
                if best is not None and abs((best.fin + 0.25) - st) < 1e-6:
                    why, nxt = "dep", best
                elif pe_ is not None:
                    why, nxt = "eng", pe_
                elif best is not None:
                    why, nxt = "dep?", best
                ln = o.fn.__code__.co_firstlineno if o.fn is not None else -1
                path.append("%8.1f %-4s cost=%5.2f line=%d via=%s" % (st, o.eng, o.cost, ln, why))
                o = nxt
            print("\n".join(path))
        return order

    def flush(self):
        nc = self.nc
        order = self._schedule()
        prog = {k: [] for k in self.E}
        for e, lst in order.items():
            for o in lst:
                if o.dma is None:
                    self.cnt[e] += 1
                    o.idx = self.cnt[e]
        for e, lst in order.items():
            eng = self.E[e]
            for o in lst:
                need = {}
                for d in o.deps:
                    if d.dma is None:
                        key, sm, val = ("eng", d.eng), self.sem[d.eng], d.idx
                    else:
                        key, sm, val = ("dma", id(d.dma[2])), d.dma[2], d.dma[3]
                    if self.waited[e].get(key, 0) >= val:
                        continue
                    if key not in need or need[key][1] < val:
                        need[key] = (sm, val)
                waits = []
                for key, (sm, val) in need.items():
                    self.waited[e][key] = val
                    waits.append((sm, val))
                if o.dma is None:
                    def run(eng=eng, waits=waits, fn=o.fn, sem=self.sem[e]):
                        for (sm, v) in waits:
                            eng.wait_ge(sm, v)
                        fn(eng).then_inc(sem, 1)
                else:
                    def run(eng=eng, waits=waits, d=o.dma):
                        for (sm, v) in waits:
                            eng.wait_ge(sm, v)
                        eng.dma_start(out=d[0], in_=d[1]).then_inc(d[2], 16)
                prog[e].append(run)
        snap = [(self.sem[k], self.cnt[k], ("eng", k)) for k in self.sem if self.cnt[k] > 0]
        snap += [(o.dsem, o.dcount, ("dma", id(o.dsem))) for o in self.dsems]
        for e in self.E:
            eng = self.E[e]
            ws = []
            for (sm, v, key) in snap:
                if self.waited[e].get(key, 0) < v:
                    self.waited[e][key] = v
                    ws.append((sm, v))

            def runb(eng=eng, ws=ws):
                for (sm, v) in ws:
                    eng.wait_ge(sm, v)
            prog[e].append(runb)
        with nc.Block() as block:
            @block.sync
            def _(e):
                for f in prog["sp"]:
                    f()

            @block.tensor
            def _(e):
                for f in prog["pe"]:
                    f()

            @block.scalar
            def _(e):
                for f in prog["act"]:
                    f()

            @block.vector
            def _(e):
                for f in prog["dve"]:
                    f()

            @block.gpsimd
            def _(e):
                for f in prog["pool"]:
                    f()
        self.ops = []
        self.epoch += 1
        self.bank_free = [0.0] * 8
        self.bank_last = [[] for _ in range(8)]


class T:
    def __init__(self, stack, nc, name, shape, dt, psum=False):
        mk = nc.psum_tensor if psum else nc.sbuf_tensor
        self.t = stack.enter_context(mk("s_" + name, list(shape), dt))
        self.b = Buf(name)

    def __getitem__(self, idx):
        return self.t[idx]


def _build(S, n_tiles_c, nc_box):
    NT = S // 128
    NSUP = NT // n_tiles_c
    TC = n_tiles_c * 128
    nc = bass.Bass("TRN2", target_bir_lowering=False)
    nc_box.append(nc)
    dram = lambda name, shape, dt=F32, kind="ExternalInput": nc.dram_tensor(name, list(shape), dt, kind=kind).ap()
    x_d = dram("x", [S, D])
    ccol_d = dram("ccol", [128, 16])
    wada_d = dram("w_ada", [D, 6 * D])
    win_d = dram("w_in", [D, IN_COLS])
    pcol_d = dram("pcol", [128, NPC])
    cst_d = dram("cst", [128, NCS])
    brow_d = dram("brow", [1, 2560])
    lnrow_d = dram("lnrows", [4, D])
    wst_d = dram("wsT", [128, 1024])
    bspb_d = dram("bspb", [128, 512])
    lw_d = dram("lw", [64, 512])
    wg_d = dram("wg", [96, 512])
    wa_d = dram("w_branch_a", [512, D])
    wb_d = dram("w_branch_b", [512, D])
    wout_d = dram("w_out", [D, D])
    wff1_d = dram("w_ff1", [D, 4 * D])
    wff2_d = dram("w_ff2", [4 * D, D])
    out_d = dram("out", [S, D], kind="ExternalOutput")
    yab_d = nc.dram_tensor("yab_s", [S, D], BF16).ap()
    h1_d = nc.dram_tensor("h1_s", [S, D], F32).ap()
    yab_db = [Buf("yabd%d" % i) for i in range(NT)]
    h1_db = [Buf("h1d%d" % i) for i in range(NT)]

    with contextlib.ExitStack() as G:
        G.enter_context(nc.cleanup_on_exit())
        G.enter_context(nc.allow_low_precision("bf16 matmuls with fp32 accumulation"))
        sc = Sched(nc, G)
        mkT = lambda st, name, shape, dt=F32: T(st, nc, name, shape, dt)

        def ck(tag):
            if os.environ.get("KSTOP") == tag:
                sc.barrier()
                sc.flush()
                raise StopBuild()
        cst = mkT(G, "cst", [128, NCS])
        pcol = mkT(G, "pcol", [128, NPC])
        modT = mkT(G, "modT", [128, 64])
        identb = mkT(G, "identb", [128, 128], BF16)
        onesb = mkT(G, "onesb", [1, 128], BF16)
        neghG = mkT(G, "neghG", [128, 2])
        browb = mkT(G, "browb", [1, 2560], BF16)
        psr = [T(G, nc, "ps%d" % i, [128, 512], F32, psum=True) for i in range(8)]
        sc.ps_real = [p.t for p in psr]
        sc.ps16_real = [p.t.bitcast(BF16) for p in psr]
        vbank = sc.vbank

        pc = lambda j, p0=0, p1=128: pcol[p0:p1, j:j + 1]
        ident = cst[:, CS_ID:CS_ID + 128]
        SH1, GT1, SH2, GT2, OPS1, OPS2 = 0, 16, 24, 40, 48, 56

        sc.dma(cst[:, :], cst_d[:, :], cst.b, writes=[cst.b])
        sc.dma(pcol[:, :], pcol_d[:, :], pcol.b, writes=[pcol.b])
        sc.op("dve", lambda e: e.tensor_copy(out=identb[:, :], in_=ident), reads=[cst.b], writes=[identb.b])
        sc.op("dve", lambda e: e.memset(onesb[:, :], 1.0), writes=[onesb.b])
        sc.op("dve", lambda e: e.memset(neghG[:, :], -0.5), writes=[neghG.b])

        with contextlib.ExitStack() as P0:
            ccol = mkT(P0, "ccol", [128, 16])
            cact = mkT(P0, "cact", [128, 16])
            csig = mkT(P0, "csig", [128, 16])
            browf = mkT(P0, "browf", [1, 2560])
            stg = [mkT(P0, "stgm%d" % i, [128, 8 * 256]) for i in range(8)]
            sc.dma(ccol[:, :], ccol_d[:, :], ccol.b, writes=[ccol.b])
            sc.dma(browf[:, :], brow_d[:, :], browf.b, writes=[browf.b])
            sc.op("act", lambda e: e.activation(out=csig[:, :], in_=ccol[:, :], func=AF.Sigmoid),
                  reads=[ccol.b], writes=[csig.b])
            sc.op("dve", lambda e: e.tensor_tensor(out=cact[:, :], in0=ccol[:, :], in1=csig[:, :], op=ALU.mult),
                  reads=[ccol.b, csig.b], writes=[cact.b])
            sc.op("dve", lambda e: e.tensor_copy(out=browb[:, :], in_=browf[:, :]), reads=[browf.b], writes=[browb.b])
            wada_v = wada_d.rearrange("(k p) f -> p k f", p=128)
            modrow = mkT(P0, "modrow", [2, 6 * D])
            prow = None
            for piece in range(24):
                sg = stg[piece % 8]
                sgv = sg[:, :].rearrange("p (k f) -> p k f", k=8)
                sc.dma(sgv, wada_v[:, :, piece * 256:(piece + 1) * 256], sg.b, writes=[sg.b])
                if piece % 2 == 0:
                    prow = vbank()

                def f(e, sgv=sgv, piece=piece, prow=prow):
                    r = None
                    for k in range(8):
                        r = e.matmul(prow[0:2, (piece % 2) * 256:(piece % 2 + 1) * 256], lhsT=cact[:, 2 * k:2 * k + 2],
                                     rhs=sgv[:, k, :], start=(k == 0), stop=(k == 7))
                    return r
                sc.op("pe", f, reads=[sg.b, cact.b], writes=[prow.b])
                if piece % 2 == 1:
                    g0 = (piece // 2) * 512
                    sc.op("act", lambda e, prow=prow, g0=g0: e.activation(out=modrow[0:2, g0:g0 + 512], in_=prow[0:2, :], func=AF.Identity),
                          reads=[prow.b, modrow.b], writes=[modrow.b])
            pm = vbank()

            def ftm(e):
                r = None
                for m in range(48):
                    r = e.transpose(out=pm[:, 2 * m:2 * m + 2], in_=modrow[0:2, m * 128:(m + 1) * 128], identity=ident[0:2, 0:2])
                return r
            sc.op("pe", ftm, reads=[modrow.b, cst.b], writes=[pm.b])
            sc.op("dve", lambda e: e.tensor_tensor(out=modT[:, 0:48], in0=pm[:, 0:96].rearrange("p (m two) -> p m two", two=2)[:, :, 0],
                                                   in1=pcol[:, PC_BADA:PC_BADA + 48], op=ALU.add),
                  reads=[pm.b, pcol.b], writes=[modT.b])
            sc.op("dve", lambda e: e.tensor_scalar(out=modT[:, 48:56], in0=modT[:, 8:16], scalar1=1.0, scalar2=None, op0=ALU.add),
                  reads=[modT.b], writes=[modT.b])
            sc.op("dve", lambda e: e.tensor_scalar(out=modT[:, 56:64], in0=modT[:, 32:40], scalar1=1.0, scalar2=None, op0=ALU.add),
                  reads=[modT.b], writes=[modT.b])
            sc.barrier()
            sc.flush()
        if os.environ.get("KSTOP") == "setup":
            return nc
        md = lambda j: modT[:, j:j + 1]

        def load_w_bf16(st, name, src_view, nk, ncols, stgs, engs=("pool", "dve")):
            w = st if isinstance(st, T) else mkT(st, name, [128, nk * ncols], BF16)
            CH = stgs[0].t.shape[1]
            n = 0
            for k in range(nk):
                for c0 in range(0, ncols, CH):
                    cw = min(CH, ncols - c0)
                    sg = stgs[n % len(stgs)]
                    sc.dma(sg[:, 0:cw], src_view[:, k, c0:c0 + cw], sg.b, writes=[sg.b])
                    eng = engs[n % len(engs)]
                    sc.op(eng, lambda e, sg=sg, cw=cw, k=k, c0=c0: (
                        e.tensor_copy(out=w[:, k * ncols + c0:k * ncols + c0 + cw], in_=sg[:, 0:cw]) if hasattr(e, "tensor_copy")
                        else e.activation(out=w[:, k * ncols + c0:k * ncols + c0 + cw], in_=sg[:, 0:cw], func=AF.Identity)),
                        reads=[sg.b], writes=[w.b])
                    n += 1
            return w

        def load_w_cast(w, src_view, nk, ncols):
            chunks = [(k, c0, min(2048, ncols - c0)) for k in range(nk) for c0 in range(0, ncols, 2048)]
            for n_, (k, c0, cw) in enumerate(chunks):
                last = n_ == len(chunks) - 1
                sc.dma(w[:, k * ncols + c0:k * ncols + c0 + cw], src_view[:, k, c0:c0 + cw], w.b,
                       writes=[w.b] if last else [], q="pool", cost=4.0)
            return w

        def x_to_hT(e_pe_reads, xt_ap, xt_b, hT, psb, ops_col, sh_col, ntok=128, tok0=0, col0=0):
            pa, pb = vbank(), vbank()
            W = hT.t.shape[1] // 8

            def f(e):
                r = None
                for k in range(8):
                    p = pa if k < 4 else pb
                    r = e.transpose(out=p[:, (k % 4) * 128:(k % 4 + 1) * 128], in_=xt_ap[:, col0 + k * 128:col0 + (k + 1) * 128], identity=ident)
                return r
            sc.op("pe", f, reads=[xt_b, cst.b], writes=[pa.b, pb.b])

            def g(e):
                r = None
                for k in range(8):
                    p = pa if k < 4 else pb
                    r = e.activation(out=hT[:, k * W + tok0:k * W + tok0 + 128], in_=p[:, (k % 4) * 128:(k % 4 + 1) * 128],
                                     func=AF.Identity, bias=md(sh_col + k), scale=md(ops_col + k))
                return r
            sc.op("act", g, reads=[pa.b, pb.b, modT.b], writes=[hT.b])

        def gelu_from(psrc_fn, nchunk, bias_fn, dst, tmp, parts=128):
            gx, gx2 = tmp[0], tmp[1]
            gt = gt2 = gs = gx2
            pbufs = psrc_fn("bufs")

            def f1(e):
                r = None
                for c in range(nchunk):
                    r = e.activation(out=gx[:, c * 128:(c + 1) * 128], in_=psrc_fn(c), func=AF.Identity, bias=bias_fn(c), scale=1.0)
                return r
            sc.op("act", f1, reads=pbufs + [pcol.b], writes=[gx.b])

            def f2(e):
                r = None
                for c in range(nchunk):
                    r = e.activation(out=gx2[:, c * 128:(c + 1) * 128], in_=psrc_fn(c), func=AF.Square, bias=bias_fn(c), scale=1.0)
                return r
            sc.op("act", f2, reads=pbufs + [pcol.b], writes=[gx2.b])
            n = nchunk * 128
            sc.op("pool", lambda e: e.tensor_scalar(out=gt[:, 0:n], in0=gx2[:, 0:n], scalar1=0.044715, scalar2=1.0, op0=ALU.mult, op1=ALU.add),
                  reads=[gx2.b], writes=[gt.b])
            sc.op("pool", lambda e: e.tensor_tensor(out=gt2[:, 0:n], in0=gt[:, 0:n], in1=gx[:, 0:n], op=ALU.mult),
                  reads=[gt.b, gx.b], writes=[gt2.b])
            sc.op("act", lambda e: e.activation(out=gs[:, 0:n], in_=gt2[:, 0:n], func=AF.Sigmoid, scale=1.5957691216057308),
                  reads=[gt2.b], writes=[gs.b])
            sc.op("pool", lambda e: e.tensor_tensor(out=dst[:, 0:n], in0=gx[:, 0:n], in1=gs[:, 0:n], op=ALU.mult),
                  reads=[gx.b, gs.b], writes=[dst.b])

        def layernorm_rows(src, dst_scaled, gb, bb, tmp_small, out_t, out_eng="pool", c0=0):
            st6, mv, sd, rstd = tmp_small
            sv = lambda a, b: src[:, c0 + a:c0 + b]
            sc.op("dve", lambda e: e.bn_stats(out=st6[:, 0:6], in_=sv(0, 512)), reads=[src.b], writes=[st6.b])
            sc.op("dve", lambda e: e.bn_stats(out=st6[:, 6:12], in_=sv(512, 1024)), reads=[src.b, st6.b], writes=[st6.b])
            sc.op("dve", lambda e: e.bn_aggr(out=mv[:, 0:2], in_=st6[:, 0:12]), reads=[st6.b], writes=[mv.b])
            sc.op("pool", lambda e: e.tensor_scalar(out=sd[:, 0:1], in0=mv[:, 1:2], scalar1=LN_EPS, scalar2=None, op0=ALU.add),
                  reads=[mv.b], writes=[sd.b])
            sc.op("pool", lambda e: e.tensor_tensor(out=rstd[:, 0:1], in0=sd[:, 0:1], in1=neghG[:, 0:1], op=ALU.pow),
                  reads=[sd.b, neghG.b], writes=[rstd.b])
            sc.op("dve", lambda e: e.tensor_scalar(out=dst_scaled[:, :], in0=sv(0, 1024), scalar1=mv[:, 0:1], scalar2=rstd[:, 0:1],
                                                   op0=ALU.subtract, op1=ALU.mult), reads=[src.b, mv.b, rstd.b], writes=[dst_scaled.b])
            sc.op(out_eng, lambda e: e.tensor_tensor(out=sv(0, 1024), in0=dst_scaled[:, :], in1=gb[:, :], op=ALU.mult),
                  reads=[dst_scaled.b, gb.b, src.b], writes=[src.b])
            sc.op(out_eng, lambda e: e.tensor_tensor(out=out_t[:, :], in0=sv(0, 1024), in1=bb[:, :], op=ALU.add),
                  reads=[src.b, bb.b], writes=[out_t.b])

        with contextlib.ExitStack() as PA:
            NA = 2720
            winA = mkT(PA, "winA", [128, 8 * NA], BF16)
            wcol = lambda k, c0, m: winA[:, k * NA + c0:k * NA + c0 + m]
            lwb = mkT(PA, "lwb", [64, 512], BF16)
            wgb = mkT(PA, "wgb", [96, 512], BF16)
            wsTb = mkT(PA, "wsTb", [128, 1024], BF16)
            B2 = mkT(PA, "B2", [128, 512])
            with contextlib.ExitStack() as TA:
                stgs = [mkT(TA, "stga%d" % i, [128, 1024]) for i in range(2)]
                load_w_cast(winA, win_d.rearrange("(k p) f -> p k f", p=128)[:, :, 0:NA], 8, NA)
                sc.dma(stgs[0][0:64, 0:512], lw_d[:, :], stgs[0].b, writes=[stgs[0].b])
                sc.op("dve", lambda e: e.tensor_copy(out=lwb[:, :], in_=stgs[0][0:64, 0:512]), reads=[stgs[0].b], writes=[lwb.b])
                sc.dma(stgs[1][0:96, 0:512], wg_d[:, :], stgs[1].b, writes=[stgs[1].b])
                sc.op("dve", lambda e: e.tensor_copy(out=wgb[:, :], in_=stgs[1][0:96, 0:512]), reads=[stgs[1].b], writes=[wgb.b])
                sc.dma(stgs[0][:, 0:1024], wst_d[:, :], stgs[0].b, writes=[stgs[0].b])

                def fws(e):
                    r = None
                    for g in range(8):
                        r = e.tensor_tensor(out=wsTb[:, g * 128:(g + 1) * 128], in0=stgs[0][:, g * 128:(g + 1) * 128],
                                            in1=cst[:, CS_M4 + 128:CS_M4 + 256], op=ALU.mult)
                    return r
                sc.op("dve", fws, reads=[stgs[0].b, cst.b], writes=[wsTb.b])
                sc.dma(B2[:, :], bspb_d[:, :], B2.b, writes=[B2.b])
                sc.barrier()
                sc.flush()
            xt = [mkT(PA, "xtA%d" % i, [128, D]) for i in range(2)]
            hT = mkT(PA, "hTA", [128, 8 * 128], BF16)
            g5 = [mkT(PA, "gel%d" % i, [128, 512]) for i in range(2)]
            uT = mkT(PA, "uT", [128, 512])
            vg = mkT(PA, "vg", [128, 512])
            vn = mkT(PA, "vn", [128, 512], BF16)
            onesv = mkT(PA, "onesv", [128, 512], BF16)
            onesf = mkT(PA, "onesf", [128, 128])
            sm = [mkT(PA, "smA%d" % i, [128, 64]) for i in range(6)]
            tmpa = mkT(PA, "tmpa", [128, 512])
            yab = [mkT(PA, "yab%d" % i, [128, 1024], BF16) for i in range(2)]
            zs = mkT(PA, "zs", [128, 14 * 129])
            zp = mkT(PA, "zp", [128, 14 * 128])
            txa = mkT(PA, "txa", [64, 128], BF16)
            txf = mkT(PA, "txf", [32, 128])
            negh = mkT(PA, "negh", [128, 8])
            sxg = mkT(PA, "sxg", [96, 128], BF16)
            sw = mkT(PA, "sw", [128, 512])
            a_ = mkT(PA, "a_", [128, 512])
            gT_2 = [mkT(PA, "gT_%d" % i_, [128, 512]) for i_ in range(2)]
            cum = mkT(PA, "cum", [128, 4 * 129])
            gam = mkT(PA, "gam", [128, 4 * 129])
            igam = mkT(PA, "igam", [128, 512])
            gC_2 = [mkT(PA, "gC_%d" % i_, [128, 4]) for i_ in range(2)]
            kq = mkT(PA, "kq", [128, 512])
            nrm = mkT(PA, "nrm", [128, 512])
            inv = mkT(PA, "inv", [128, 512])
            kk = mkT(PA, "kk", [128, 512])
            t1 = mkT(PA, "t1", [128, 512])
            kmod = mkT(PA, "kmod", [128, 512])
            t2 = mkT(PA, "t2", [128, 512])
            rkp = mkT(PA, "rkp", [128, 512])
            bonus_2 = [mkT(PA, "bonus_%d" % i_, [128, 512]) for i_ in range(2)]
            RA_2 = [mkT(PA, "RA_%d" % i_, [128, 1024], BF16) for i_ in range(2)]
            KTf_2 = [mkT(PA, "KTf_%d" % i_, [128, 512], BF16) for i_ in range(2)]
            BTf_2 = [mkT(PA, "BTf_%d" % i_, [128, 512], BF16) for i_ in range(2)]
            vb = mkT(PA, "vb", [128, 512], BF16)
            KBt_2 = [mkT(PA, "KBt_%d" % i_, [128, 1024], BF16) for i_ in range(2)]
            Vt_2 = [mkT(PA, "Vt_%d" % i_, [128, 512], BF16) for i_ in range(2)]
            AM_2 = [mkT(PA, "AM_%d" % i_, [128, 8 * 512], BF16) for i_ in range(2)]
            Pp = [mkT(PA, "Pp%d" % i, [128, 1024], BF16) for i in range(7)]
            PTp = [mkT(PA, "PTp%d" % i, [128, 1024], BF16) for i in range(2)]
            Xf = [mkT(PA, "Xf%d" % i, [128, 512]) for i in range(2)]
            Xb = [mkT(PA, "Xb%d" % i, [128, 512], BF16) for i in range(2)]
            Sf = mkT(PA, "Sf", [128, 512])
            Sbd = mkT(PA, "Sbd", [128, 512], BF16)
            st1 = mkT(PA, "st1", [128, 512])
            st2 = mkT(PA, "st2", [128, 512])
            gst = mkT(PA, "gst", [128, 48])
            gmv = mkT(PA, "gmv", [128, 16])
            gsd = mkT(PA, "gsd", [128, 8])
            grs = mkT(PA, "grs", [128, 8])
            ynb = mkT(PA, "ynb", [128, 512], BF16)
            yt = mkT(PA, "yt", [128, 512])
            yt2 = yt
            zsv = zs[:, :].rearrange("p (j t) -> p j t", j=14)
            zpv = zp[:, :].rearrange("p (j t) -> p j t", j=14)
            cumv = cum[:, :].rearrange("p (c t) -> p c t", c=4)
            gamv = gam[:, :].rearrange("p (c t) -> p c t", c=4)
            v3 = lambda t_, c=4: t_[:, :].rearrange("p (c t) -> p c t", c=c)
            for t_ in (zs, cum, Sf, zp):
                sc.op("pool", lambda e, t_=t_: e.memset(t_[:, :], 0.0), writes=[t_.b])
            sc.op("pool", lambda e: e.memset(Sbd[:, :], 0.0), writes=[Sbd.b])
            sc.op("pool", lambda e: e.memset(onesv[:, :], 1.0), writes=[onesv.b])
            sc.op("pool", lambda e: e.memset(onesf[:, :], 1.0), writes=[onesf.b])
            sc.op("pool", lambda e: e.memset(negh[:, :], -0.5), writes=[negh.b])
            m4b = mkT(PA, "m4b", [128, 512], BF16)
            sc.op("dve", lambda e: e.tensor_copy(out=m4b[:, :], in_=cst[:, CS_M4:CS_M4 + 512]), reads=[cst.b], writes=[m4b.b])

            def spatial(src_bf, pbank):
                def f(e):
                    r = None
                    for g in range(8):
                        c, pb = g // 2, (g % 2) * 64
                        r = e.matmul(pbank[pb:pb + 64, c * 128:(c + 1) * 128], lhsT=src_bf[:, g * 64:(g + 1) * 64],
                                     rhs=wsTb[:, g * 128:(g + 1) * 128], start=True, stop=True)
                    return r
                sc.op("pe", f, reads=[src_bf.b, wsTb.b], writes=[pbank.b])
            pb_ = vbank()
            spatial(onesv, pb_)

            def fB2(e):
                r = None
                for c in range(4):
                    r = e.scalar_tensor_tensor(out=B2[:, c * 128:(c + 1) * 128], in0=pb_[:, c * 128:(c + 1) * 128], scalar=pc(PC_BLN + c),
                                               in1=B2[:, c * 128:(c + 1) * 128], op0=ALU.mult, op1=ALU.add)
                return r
            sc.op("dve", fB2, reads=[pb_.b, pcol.b, B2.b], writes=[B2.b])

            def load_x(i):
                xs_ = xt[i % 2]
                sc.dma(xs_[:, :], x_d[i * 128:(i + 1) * 128, :], xs_.b, writes=[xs_.b])

            sl = lambda t_, c, w=128: t_[:, c * w:(c + 1) * w]
            ck("A0")
            hk = lambda k: hT[:, k * 128:(k + 1) * 128]

            def front_h(i):
                xs = xt[i % 2]
                x_to_hT(None, xs, xs.b, hT, 0, OPS1, SH1)
                yield

            def front_g(i):
                RA, KTf, BTf, KBt, Vt, gT, bonus, gC = RA_2[i % 2], KTf_2[i % 2], BTf_2[i % 2], KBt_2[i % 2], Vt_2[i % 2], gT_2[i % 2], bonus_2[i % 2], gC_2[i % 2]
                RAv = RA[:, :].rearrange("p (c q t) -> p c q t", c=4, q=2)
                ya = yab[i % 2]
                pu = vbank()

                def fu(e, pu=pu):
                    r = None
                    for c in range(4):
                        for k in range(8):
                            r = e.matmul(sl(pu, c), lhsT=wcol(k, c * 128, 128), rhs=hk(k), start=(k == 0), stop=(k == 7))
                    return r
                sc.op("pe", fu, reads=[winA.b, hT.b], writes=[pu.b])
                yield
                gelu_from(lambda c, pu=pu: [pu.b] if c == "bufs" else sl(pu, c), 4, lambda c: pc(PC_BU + c), uT, g5)
                yield
                ck("A2")
                pv = vbank()

                def fv(e, pv=pv):
                    for k in range(8):
                        e.matmul(pv[:, :], lhsT=hk(k), rhs=wcol(k, 512, 512), start=(k == 0), stop=False)
                    return e.matmul(pv[:, :], lhsT=onesb[0:1, 0:128], rhs=browb[0:1, 0:512], start=False, stop=True)
                sc.op("pe", fv, reads=[winA.b, hT.b, onesb.b, browb.b], writes=[pv.b])
                yield
                gelu_from(lambda c, pv=pv: [pv.b] if c == "bufs" else sl(pv, c), 4, lambda c: 0.0, vg, g5)
                yield
                st6, mv, sd, rstd = sm[0], sm[1], sm[2], sm[3]
                sc.op("dve", lambda e: e.bn_stats(out=st6[:, 0:6], in_=vg[:, :]), reads=[vg.b], writes=[st6.b])
                yield
                sc.op("dve", lambda e: e.bn_aggr(out=mv[:, 0:2], in_=st6[:, 0:6]), reads=[st6.b], writes=[mv.b])
                yield
                sc.op("pool", lambda e: e.tensor_scalar(out=sd[:, 0:1], in0=mv[:, 1:2], scalar1=LN_EPS, scalar2=None, op0=ALU.add),
                      reads=[mv.b], writes=[sd.b])
                yield
                sc.op("pool", lambda e: e.tensor_tensor(out=rstd[:, 0:1], in0=sd[:, 0:1], in1=negh[:, 0:1], op=ALU.pow), reads=[sd.b, negh.b], writes=[rstd.b])
                yield
                sc.op("dve", lambda e: e.tensor_scalar(out=vn[:, :], in0=vg[:, :], scalar1=mv[:, 0:1], scalar2=rstd[:, 0:1],
                                                       op0=ALU.subtract, op1=ALU.mult), reads=[vg.b, mv.b, rstd.b], writes=[vn.b])
                yield
                ck("A3")
                pS = vbank()
                spatial(vn, pS)
                yield

                def fya(e, pS=pS):
                    r = None
                    for c in range(4):
                        r = e.scalar_tensor_tensor(out=sl(tmpa, c), in0=sl(pS, c), scalar=pc(PC_GLN + c), in1=sl(B2, c), op0=ALU.mult, op1=ALU.add)
                    return r
                sc.op("dve", fya, reads=[pS.b, pcol.b, B2.b], writes=[tmpa.b])
                yield
                sc.op("dve", lambda e, ya=ya: e.tensor_tensor(out=ya[:, 0:512], in0=tmpa[:, :], in1=uT[:, :], op=ALU.mult),
                      reads=[tmpa.b, uT.b], writes=[ya.b])
                yield

            def front_r(i):
                RA, KTf, BTf, KBt, Vt, gT, bonus, gC = RA_2[i % 2], KTf_2[i % 2], BTf_2[i % 2], KBt_2[i % 2], Vt_2[i % 2], gT_2[i % 2], bonus_2[i % 2], gC_2[i % 2]
                RAv = RA[:, :].rearrange("p (c q t) -> p c q t", c=4, q=2)
                ya = yab[i % 2]
                ck("A4")
                pz = [vbank() for _ in range(4)]

                def fz(e, pz=pz):
                    r = None
                    for j in range(12):
                        for k in range(8):
                            r = e.matmul(sl(pz[j // 4], j % 4), lhsT=wcol(k, 1024 + j * 128, 128), rhs=hk(k), start=(k == 0), stop=(k == 7))
                    for k in range(8):
                        r = e.matmul(pz[3][0:64, 0:128], lhsT=wcol(k, 2560, 64), rhs=hk(k), start=(k == 0), stop=(k == 7))
                    for k in range(8):
                        r = e.matmul(pz[3][0:96, 128:256], lhsT=wcol(k, 2624, 96), rhs=hk(k), start=(k == 0), stop=(k == 7))
                    return r
                sc.op("pe", fz, reads=[winA.b, hT.b], writes=[b_.b for b_ in pz])
                yield

                def fzs(e, pz=pz):
                    for j in range(12):
                        e.activation(out=zsv[:, j, 1:129], in_=sl(pz[j // 4], j % 4), func=AF.Identity, bias=pc(PC_BRKV + j), scale=1.0)
                    e.activation(out=zsv[0:64, 12, 1:129], in_=pz[3][0:64, 0:128], func=AF.Identity, bias=pc(PC_BXWXA, 0, 64), scale=1.0)
                    return e.activation(out=zsv[0:96, 13, 1:129], in_=pz[3][0:96, 128:256], func=AF.Identity, bias=pc(PC_BXG, 0, 96), scale=1.0)
                sc.op("act", fzs, reads=[b_.b for b_ in pz] + [pcol.b], writes=[zs.b])
                yield
                sc.op("pool", lambda e: e.tensor_tensor(out=zpv[:, :, :], in0=zsv[:, :, 0:128], in1=zsv[:, :, 1:129], op=ALU.subtract),
                      reads=[zs.b], writes=[zp.b])
                yield

                def fzp(e):
                    for j in range(12):
                        e.scalar_tensor_tensor(out=zpv[:, j, :], in0=zpv[:, j, :], scalar=pc(PC_MURKV + j), in1=zsv[:, j, 1:129], op0=ALU.mult, op1=ALU.add)
                    e.scalar_tensor_tensor(out=zpv[0:64, 12, :], in0=zpv[0:64, 12, :], scalar=pc(PC_MUXWXA, 0, 64), in1=zsv[0:64, 12, 1:129], op0=ALU.mult, op1=ALU.add)
                    return e.scalar_tensor_tensor(out=zpv[0:96, 13, :], in0=zpv[0:96, 13, :], scalar=pc(PC_MUXG, 0, 96), in1=zsv[0:96, 13, 1:129], op0=ALU.mult, op1=ALU.add)
                sc.op("dve", fzp, reads=[zp.b, zs.b, pcol.b], writes=[zp.b])
                yield
                sc.op("pool", lambda e: e.tensor_copy(out=zsv[:, :, 0:1], in_=zsv[:, :, 128:129]), reads=[zs.b], writes=[zs.b])
                yield
                ck("A5")

                def flo(e):
                    e.activation(out=txf[0:32, :], in_=zpv[0:32, 12, :], func=AF.Sigmoid, scale=2.0)
                    e.activation(out=txa[32:64, :], in_=zpv[32:64, 12, :], func=AF.Identity)
                    return e.activation(out=sxg[0:96, :], in_=zpv[0:96, 13, :], func=AF.Sigmoid)
                sc.op("act", flo, reads=[zp.b, txa.b], writes=[txf.b, txa.b, sxg.b])
                yield
                sc.op("pool", lambda e: e.tensor_scalar(out=txa[0:32, :], in0=txf[0:32, :], scalar1=2.0, scalar2=-1.0, op0=ALU.mult, op1=ALU.add),
                      reads=[txf.b, txa.b], writes=[txa.b])
                yield
                ck("A5a")
                pw, pa_, pg = vbank(), vbank(), vbank()

                def flm(e, pw=pw, pa_=pa_, pg=pg):
                    r = None
                    for c in range(4):
                        e.matmul(sl(pw, c), lhsT=lwb[0:32, c * 128:(c + 1) * 128], rhs=txa[0:32, :], start=True, stop=True)
                    for c in range(4):
                        e.matmul(sl(pa_, c), lhsT=lwb[32:64, c * 128:(c + 1) * 128], rhs=txa[32:64, :], start=True, stop=True)
                    for c in range(4):
                        r = e.matmul(sl(pg, c), lhsT=wgb[0:96, c * 128:(c + 1) * 128], rhs=sxg[0:96, :], start=True, stop=True)
                    return r
                sc.op("pe", flm, reads=[lwb.b, wgb.b, txa.b, sxg.b], writes=[pw.b, pa_.b, pg.b])
                yield

                ck("A5b")
                def fsw(e, pw=pw, pa_=pa_, pg=pg):
                    for c in range(4):
                        e.activation(out=sl(sw, c), in_=sl(pw, c), func=AF.Sigmoid, bias=pc(PC_W0 + c), scale=1.0)
                    for c in range(4):
                        e.activation(out=sl(a_, c), in_=sl(pa_, c), func=AF.Sigmoid, bias=pc(PC_A0 + c), scale=1.0)
                    return e.activation(out=gT[:, :], in_=pg[:, :], func=AF.Identity)
                sc.op("act", fsw, reads=[pw.b, pa_.b, pg.b, pcol.b], writes=[sw.b, a_.b, gT.b])
                yield

                ck("A5c")
                def fcum(e):
                    r = None
                    for c in range(4):
                        r = e.tensor_tensor_scan(out=cumv[:, c, 1:129], data0=onesf[:, :], data1=sl(sw, c), initial=0.0, op0=ALU.mult, op1=ALU.add)
                    return r
                sc.op("dve", fcum, reads=[sw.b, onesf.b], writes=[cum.b])
                yield
                ck("A5d")
                sc.op("act", lambda e: e.activation(out=gam[:, :], in_=cum[:, :], func=AF.Exp, scale=-EXPM05), reads=[cum.b], writes=[gam.b])
                yield
                sc.op("act", lambda e: e.activation(out=v3(igam), in_=cumv[:, :, 1:129], func=AF.Exp, scale=EXPM05), reads=[cum.b], writes=[igam.b])
                yield
                ck("A5e")
                sc.op("dve", lambda e: e.tensor_copy(out=gC[:, :], in_=gamv[:, :, 128]), reads=[gam.b], writes=[gC.b])
                yield
                ck("A6")

                def fkq(e):
                    r = None
                    for c in range(4):
                        r = e.activation(out=sl(kq, c), in_=zpv[:, 4 + c, :], func=AF.Square, scale=pc(PC_KK + c))
                    return r
                sc.op("act", fkq, reads=[zp.b, pcol.b], writes=[kq.b])
                yield
                pn = vbank()
                sc.op("pe", lambda e, pn=pn: e.matmul(pn[:, :], lhsT=cst[:, CS_BD4:CS_BD4 + 128], rhs=kq[:, :], start=True, stop=True),
                      reads=[cst.b, kq.b], writes=[pn.b])
                yield
                sc.op("dve", lambda e, pn=pn: e.tensor_scalar(out=nrm[:, :], in0=pn[:, :], scalar1=1e-24, scalar2=None, op0=ALU.max), reads=[pn.b], writes=[nrm.b])
                yield
                sc.op("act", lambda e: e.activation(out=t2[:, :], in_=nrm[:, :], func=AF.Ln), reads=[nrm.b], writes=[t2.b])
                yield
                sc.op("act", lambda e: e.activation(out=inv[:, :], in_=t2[:, :], func=AF.Exp, scale=-0.5), reads=[t2.b], writes=[inv.b])
                yield

                def fkk(e):
                    r = None
                    for c in range(4):
                        r = e.scalar_tensor_tensor(out=sl(kk, c), in0=zpv[:, 4 + c, :], scalar=pc(PC_KK + c), in1=sl(inv, c), op0=ALU.mult, op1=ALU.mult)
                    return r
                sc.op("dve", fkk, reads=[zp.b, pcol.b, inv.b], writes=[kk.b])
                yield

                def ft1(e):
                    r = None
                    for c in range(4):
                        r = e.tensor_scalar(out=sl(t1, c), in0=sl(a_, c), scalar1=-1.0, scalar2=pc(PC_KA + c), op0=ALU.add, op1=ALU.mult)
                    return r
                sc.op("dve", ft1, reads=[a_.b, pcol.b], writes=[t1.b])
                yield
                sc.op("dve", lambda e: e.scalar_tensor_tensor(out=kmod[:, :], in0=t1[:, :], scalar=1.0, in1=zp[:, 512:1024], op0=ALU.add, op1=ALU.mult),
                      reads=[t1.b, zp.b], writes=[kmod.b])
                yield
                ck("A7")
                sc.op("pool", lambda e: e.tensor_tensor(out=RAv[:, :, 1, :], in0=zpv[:, 0:4, :], in1=gamv[:, :, 1:129], op=ALU.mult),
                      reads=[zp.b, gam.b], writes=[RA.b])
                yield
                sc.op("pool", lambda e: e.tensor_tensor(out=RAv[:, :, 0, :], in0=v3(kk), in1=gamv[:, :, 0:128], op=ALU.mult),
                      reads=[kk.b, gam.b, RA.b], writes=[RA.b])
                yield
                sc.op("pool", lambda e: e.tensor_tensor(out=KTf[:, :], in0=kmod[:, :], in1=igam[:, :], op=ALU.mult), reads=[kmod.b, igam.b], writes=[KTf.b])
                yield
                sc.op("pool", lambda e: e.tensor_tensor(out=t2[:, :], in0=kk[:, :], in1=a_[:, :], op=ALU.mult), reads=[kk.b, a_.b], writes=[t2.b])
                yield
                sc.op("dve", lambda e: e.scalar_tensor_tensor(out=BTf[:, :], in0=t2[:, :], scalar=-1.0, in1=igam[:, :], op0=ALU.mult, op1=ALU.mult),
                      reads=[t2.b, igam.b], writes=[BTf.b])
                yield
                sc.op("act", lambda e: e.activation(out=vb[:, :], in_=zp[:, 1024:1536], func=AF.Identity), reads=[zp.b], writes=[vb.b])
                yield

                def frk(e):
                    r = None
                    for c in range(4):
                        r = e.scalar_tensor_tensor(out=sl(rkp, c), in0=zpv[:, c, :], scalar=pc(PC_RK + c), in1=sl(kmod, c), op0=ALU.mult, op1=ALU.mult)
                    return r
                sc.op("dve", frk, reads=[zp.b, pcol.b, kmod.b], writes=[rkp.b])
                yield
                pbo = vbank()
                sc.op("pe", lambda e, pbo=pbo: e.matmul(pbo[:, :], lhsT=cst[:, CS_BD4:CS_BD4 + 128], rhs=rkp[:, :], start=True, stop=True),
                      reads=[cst.b, rkp.b], writes=[pbo.b])
                yield
                sc.op("dve", lambda e, pbo=pbo: e.tensor_tensor(out=bonus[:, :], in0=pbo[:, :], in1=zp[:, 1024:1536], op=ALU.mult),
                      reads=[pbo.b, zp.b], writes=[bonus.b])
                yield
                ck("A8")
                vp16 = vbank()
                p16, p16b = vp16.as16, vp16.b

                def ftr(e, p16=p16):
                    r = None
                    for c in range(4):
                        e.transpose(out=p16[:, c * 128:(c + 1) * 128], in_=sl(KTf, c), identity=identb[:, :])
                    for c in range(4):
                        r = e.transpose(out=p16[:, 512 + c * 128:512 + (c + 1) * 128], in_=sl(BTf, c), identity=identb[:, :])
                    return r
                sc.op("pe", ftr, reads=[KTf.b, BTf.b, identb.b], writes=[p16b])
                yield
                sc.op("act", lambda e, p16=p16: e.activation(out=KBt[:, :], in_=p16[:, 0:1024], func=AF.Identity), reads=[p16b], writes=[KBt.b])
                yield

                def ftv(e, p16=p16):
                    r = None
                    for c in range(4):
                        r = e.transpose(out=p16[:, c * 128:(c + 1) * 128], in_=sl(vb, c), identity=identb[:, :])
                    return r
                sc.op("pe", ftv, reads=[vb.b, identb.b], writes=[p16b])
                yield
                sc.op("dve", lambda e, p16=p16: e.tensor_copy(out=Vt[:, :], in_=p16[:, 0:512]), reads=[p16b], writes=[Vt.b])
                yield

            def tail(i):
                AM = AM_2[i % 2]
                AMv = AM[:, :].rearrange("p (h q t) -> p h q t", h=8, q=4)
                RA, KTf, BTf, KBt, Vt, gT, bonus, gC = RA_2[i % 2], KTf_2[i % 2], BTf_2[i % 2], KBt_2[i % 2], Vt_2[i % 2], gT_2[i % 2], bonus_2[i % 2], gC_2[i % 2]
                RAv = RA[:, :].rearrange("p (c q t) -> p c q t", c=4, q=2)
                ya = yab[i % 2]
                ck("A9")
                for c in range(4):
                    bx, by = vbank(), vbank()

                    def fA(e, c=c, bx=bx, by=by):
                        r = None
                        for (p0, bk) in ((0, bx), (64, by)):
                            rhs = RA[p0:p0 + 64, c * 256:(c + 1) * 256]
                            e.matmul(bk[:, 0:256], lhsT=BTf[p0:p0 + 64, c * 128:(c + 1) * 128], rhs=rhs, start=True, stop=True)
                            r = e.matmul(bk[:, 256:512], lhsT=KTf[p0:p0 + 64, c * 128:(c + 1) * 128], rhs=rhs, start=True, stop=True)
                        return r
                    sc.op("pe", fA, reads=[BTf.b, KTf.b, RA.b], writes=[bx.b, by.b])
                    yield
                    for (h, bk) in ((2 * c, bx), (2 * c + 1, by)):
                        sc.op("act", lambda e, h=h, bk=bk: e.activation(out=AM[:, h * 512:(h + 1) * 512], in_=bk[:, :], func=AF.Identity),
                              reads=[bk.b, AM.b], writes=[AM.b])
                        yield
                        sc.op("pool", lambda e, h=h: e.tensor_tensor(out=AM[:, h * 512:(h + 1) * 512], in0=AM[:, h * 512:(h + 1) * 512],
                                                                    in1=m4b[:, :], op=ALU.mult),
                              reads=[m4b.b, AM.b], writes=[AM.b])
                        yield
                be, bo = vbank(), vbank()

                def fNL(e, be=be, bo=bo):
                    r = None
                    for h in range(8):
                        c, p0 = h // 2, (h % 2) * 64
                        bk = be if h % 2 == 0 else bo
                        r = e.matmul(sl(bk, c), lhsT=RA[p0:p0 + 64, c * 256:c * 256 + 128], rhs=BTf[p0:p0 + 64, c * 128:(c + 1) * 128], start=True, stop=True)
                    return r
                sc.op("pe", fNL, reads=[RA.b, BTf.b], writes=[be.b, bo.b])
                yield
                PT0v = PTp[0][:, :].rearrange("p (c two t) -> p c two t", c=4, two=2)
                ml4 = cst[:, CS_ML4:CS_ML4 + 512].rearrange("p (c t) -> p c t", c=4)
                sc.op("dve", lambda e, be=be: e.tensor_tensor(out=PT0v[:, :, 0, :], in0=v3(be), in1=ml4, op=ALU.mult),
                      reads=[be.b, cst.b, PTp[0].b], writes=[PTp[0].b])
                yield
                sc.op("dve", lambda e, bo=bo: e.tensor_tensor(out=PT0v[:, :, 1, :], in0=v3(bo), in1=ml4, op=ALU.mult),
                      reads=[bo.b, cst.b, PTp[0].b], writes=[PTp[0].b])
                yield
                sc.op("act", lambda e: e.activation(out=v3(Pp[0], 8), in_=AMv[:, :, 0, :], func=AF.Identity), reads=[AM.b], writes=[Pp[0].b])
                yield
                ck("A10")
                pU = vbank()

                KV = os.environ.get("KVAR", "")

                def fU0(e, pU=pU):
                    r = None
                    for c in range(4):
                        if KV != "noS":
                            r = e.matmul(sl(pU, c), lhsT=RA[:, c * 256:c * 256 + 128], rhs=sl(Sbd, c), start=True, stop=(KV == "allstart"))
                        if KV == "noV":
                            continue
                        for h in (2 * c, 2 * c + 1):
                            r = e.matmul(sl(pU, h, 64), lhsT=AM[:, h * 512 + 256:h * 512 + 384], rhs=sl(Vt, h, 64),
                                         start=(KV in ("allstart", "noS")), stop=(h == 2 * c + 1) or KV in ("allstart", "noS"))
                    return r
                sc.op("pe", fU0, reads=[RA.b, Sbd.b, AM.b, Vt.b], writes=[pU.b])
                yield
                if KV == "noevac":
                    ck("A11")
                sc.op("act", lambda e, pU=pU: e.activation(out=Xf[0][:, :], in_=pU[:, :], func=AF.Identity), reads=[pU.b], writes=[Xf[0].b])
                yield
                if KV == "noevac2":
                    ck("A11")
                sc.op("dve", lambda e, pU=pU: e.tensor_copy(out=Xb[0][:, :], in_=pU[:, :]), reads=[pU.b], writes=[Xb[0].b])
                yield
                ck("A11")
                for k in range(7):
                    P, PT = Pp[k], PTp[k % 2]
                    xb_in, xf_in, xb_out, xf_out = Xb[k % 2], Xf[k % 2], Xb[(k + 1) % 2], Xf[(k + 1) % 2]
                    pX = vbank()

                    def fX(e, P=P, xb_in=xb_in, pX=pX):
                        r = None
                        for h in range(8):
                            r = e.matmul(sl(pX, h, 64), lhsT=sl(P, h), rhs=sl(xb_in, h, 64), start=True, stop=True)
                        return r
                    sc.op("pe", fX, reads=[P.b, xb_in.b], writes=[pX.b])
                    yield
                    sc.op("dve", lambda e, pX=pX, xf_in=xf_in, xb_out=xb_out: e.tensor_tensor(out=xb_out[:, :], in0=pX[:, :], in1=xf_in[:, :], op=ALU.add),
                          reads=[pX.b, xf_in.b], writes=[xb_out.b])
                    yield
                    if k < 6:
                        sc.op("dve", lambda e, pX=pX, xf_in=xf_in, xf_out=xf_out: e.tensor_tensor(out=xf_out[:, :], in0=pX[:, :], in1=xf_in[:, :], op=ALU.add),
                              reads=[pX.b, xf_in.b], writes=[xf_out.b])
                        yield
                    if k < 6:
                        Pn = Pp[k + 1]
                        q0, q1 = vbank(), vbank()

                        def fP(e, P=P, PT=PT, q0=q0, q1=q1):
                            r = None
                            for h in range(8):
                                r = e.matmul(sl(q0 if h < 4 else q1, h % 4), lhsT=sl(PT, h), rhs=sl(P, h), start=True, stop=True)
                            return r
                        sc.op("pe", fP, reads=[P.b, PT.b], writes=[q0.b, q1.b])
                        yield
                        sc.op("act", lambda e, Pn=Pn, q0=q0: e.activation(out=Pn[:, 0:512], in_=q0[:, :], func=AF.Identity), reads=[q0.b, Pn.b], writes=[Pn.b])
                        yield
                        sc.op("dve", lambda e, Pn=Pn, q1=q1: e.tensor_copy(out=Pn[:, 512:1024], in_=q1[:, :]), reads=[q1.b, Pn.b], writes=[Pn.b])
                        yield
                    if k < 5:
                        PTn = PTp[(k + 1) % 2]
                        q2, q3 = vbank(), vbank()

                        def fPT(e, P=P, PT=PT, q2=q2, q3=q3):
                            r = None
                            for h in range(8):
                                r = e.matmul(sl(q2 if h < 4 else q3, h % 4), lhsT=sl(P, h), rhs=sl(PT, h), start=True, stop=True)
                            return r
                        sc.op("pe", fPT, reads=[P.b, PT.b], writes=[q2.b, q3.b])
                        yield
                        sc.op("act", lambda e, PTn=PTn, q2=q2: e.activation(out=PTn[:, 0:512], in_=q2[:, :], func=AF.Identity), reads=[q2.b, PTn.b], writes=[PTn.b])
                        yield
                        sc.op("act", lambda e, PTn=PTn, q3=q3: e.activation(out=PTn[:, 512:1024], in_=q3[:, :], func=AF.Identity), reads=[q3.b, PTn.b], writes=[PTn.b])
                        yield
                Ub = Xb[1]
                ck("A12")
                pY = vbank()

                def fY(e, pY=pY):
                    r = None
                    for c in range(4):
                        e.matmul(sl(pY, c), lhsT=RA[:, c * 256 + 128:c * 256 + 256], rhs=sl(Sbd, c), start=True, stop=False)
                        for h in (2 * c, 2 * c + 1):
                            e.matmul(sl(pY, h, 64), lhsT=AM[:, h * 512 + 128:h * 512 + 256], rhs=sl(Ub, h, 64), start=False, stop=False)
                            r = e.matmul(sl(pY, h, 64), lhsT=AM[:, h * 512 + 384:h * 512 + 512], rhs=sl(Vt, h, 64), start=False, stop=(h == 2 * c + 1))
                    return r
                sc.op("pe", fY, reads=[RA.b, Sbd.b, AM.b, Ub.b, Vt.b], writes=[pY.b])
                yield
                pS2 = vbank()

                def fS(e, pS2=pS2):
                    r = None
                    for c in range(4):
                        e.matmul(sl(pS2, c), lhsT=KBt[:, 512 + c * 128:512 + (c + 1) * 128], rhs=sl(Ub, c), start=True, stop=False)
                        r = e.matmul(sl(pS2, c), lhsT=sl(KBt, c), rhs=sl(Vt, c), start=False, stop=True)
                    return r
                sc.op("pe", fS, reads=[KBt.b, Ub.b, Vt.b], writes=[pS2.b])
                yield
                sc.op("dve", lambda e, pS2=pS2: e.tensor_tensor(out=st1[:, :], in0=pS2[:, :], in1=cst[:, CS_BD4:CS_BD4 + 512], op=ALU.mult),
                      reads=[pS2.b, cst.b], writes=[st1.b])
                yield
                sc.op("pool", lambda e: e.tensor_tensor(out=st2[:, :], in0=st1[:, :], in1=Sf[:, :], op=ALU.add), reads=[st1.b, Sf.b], writes=[st2.b])
                yield

                def fSf(e):
                    r = None
                    for c in range(4):
                        r = e.tensor_scalar(out=sl(Sf, c), in0=sl(st2, c), scalar1=gC[:, c:c + 1], scalar2=None, op0=ALU.mult)
                    return r
                sc.op("dve", fSf, reads=[st2.b, gC.b], writes=[Sf.b])
                yield
                sc.op("act", lambda e: e.activation(out=Sbd[:, :], in_=Sf[:, :], func=AF.Identity), reads=[Sf.b], writes=[Sbd.b])
                yield
                ck("A13")

                def fgs(e, pY=pY):
                    r = None
                    for h in range(8):
                        r = e.bn_stats(out=gst[:, h * 6:h * 6 + 6], in_=sl(pY, h, 64))
                    return r
                sc.op("dve", fgs, reads=[pY.b], writes=[gst.b])
                yield

                def fga(e):
                    r = None
                    for h in range(8):
                        r = e.bn_aggr(out=gmv[:, 2 * h:2 * h + 2], in_=gst[:, h * 6:h * 6 + 6])
                    return r
                sc.op("dve", fga, reads=[gst.b], writes=[gmv.b])
                yield
                gmvv = gmv[:, :].rearrange("p (h two) -> p h two", two=2)
                sc.op("pool", lambda e: e.tensor_scalar(out=gsd[:, :], in0=gmvv[:, :, 1], scalar1=GN_EPS, scalar2=None, op0=ALU.add),
                      reads=[gmv.b], writes=[gsd.b])
                yield
                sc.op("pool", lambda e: e.tensor_tensor(out=grs[:, :], in0=gsd[:, :], in1=negh[:, :], op=ALU.pow), reads=[gsd.b, negh.b], writes=[grs.b])
                yield

                def fyn(e, pY=pY):
                    r = None
                    for h in range(8):
                        r = e.tensor_scalar(out=sl(ynb, h, 64), in0=sl(pY, h, 64), scalar1=gmv[:, 2 * h:2 * h + 1], scalar2=grs[:, h:h + 1],
                                            op0=ALU.subtract, op1=ALU.mult)
                    return r
                sc.op("dve", fyn, reads=[pY.b, gmv.b, grs.b], writes=[ynb.b])
                yield
                vp16 = vbank()
                p16, p16b = vp16.as16, vp16.b

                def fty(e, p16=p16):
                    r = None
                    for c in range(4):
                        r = e.transpose(out=p16[:, c * 128:(c + 1) * 128], in_=sl(ynb, c), identity=identb[:, :])
                    return r
                sc.op("pe", fty, reads=[ynb.b, identb.b], writes=[p16b])
                yield

                def fyt(e, p16=p16):
                    r = None
                    for c in range(4):
                        r = e.tensor_scalar(out=sl(yt, c), in0=p16[:, c * 128:(c + 1) * 128], scalar1=pc(PC_GNG + c), scalar2=pc(PC_GNB + c),
                                            op0=ALU.mult, op1=ALU.add)
                    return r
                sc.op("dve", fyt, reads=[p16b, pcol.b], writes=[yt.b])
                yield
                sc.op("pool", lambda e: e.tensor_tensor(out=yt2[:, :], in0=yt[:, :], in1=bonus[:, :], op=ALU.add), reads=[yt.b, bonus.b], writes=[yt2.b])
                yield
                sc.op("pool", lambda e, ya=ya: e.tensor_tensor(out=ya[:, 512:1024], in0=yt2[:, :], in1=gT[:, :], op=ALU.mult),
                      reads=[yt2.b, gT.b, ya.b], writes=[ya.b])
                yield
                sc.dma(yab_d[i * 128:(i + 1) * 128, :], ya[:, :], ya.b, reads=[ya.b], writes=[yab_db[i]])
                yield
                yield

            def rr(gens, steps):
                gens = list(gens)
                steps = list(steps)
                while gens:
                    for j in range(len(gens) - 1, -1, -1):
                        for _ in range(steps[j]):
                            try:
                                next(gens[j])
                            except StopIteration:
                                gens.pop(j)
                                steps.pop(j)
                                break
            load_x(0)
            if NT > 1:
                load_x(1)
            rr([front_h(0)], [1])
            rr([front_g(0), front_r(0)], [1, 2])
            for i in range(NT):
                gens, steps = [tail(i)], [1]
                if i + 1 < NT:
                    rr([front_h(i + 1)], [1])
                    if i + 2 < NT:
                        load_x(i + 2)
                    gens += [front_g(i + 1), front_r(i + 1)]
                    steps += [1, 2]
                rr(gens, steps)
            sc.barrier()
            sc.flush()
        if os.environ.get("KSTOP") == "A":
            return nc

        def bcast_rows(st, tmp, name, col0, dg=None, of_=None):
            tl = st
            dg = dg if dg is not None else mkT(tmp, name + "_dg", [128, D])
            ofT = of_ if of_ is not None else mkT(tmp, name + "_on", [128, 128])

            class _OF:
                b = ofT.b

                def __getitem__(self, idx):
                    return ofT[:, 0:128]
            of = _OF()
            sc.op("pool", lambda e: e.memset(of[:, :], 1.0), writes=[of.b])

            def fdg(e):
                r = None
                for m in range(8):
                    r = e.tensor_scalar(out=dg[:, m * 128:(m + 1) * 128], in0=ident, scalar1=md(col0 + m), scalar2=None, op0=ALU.mult)
                return r
            sc.op("dve", fdg, reads=[cst.b, modT.b], writes=[dg.b])
            q0, q1 = vbank(), vbank()

            def fbc(e):
                r = None
                for m in range(8):
                    r = e.matmul((q0 if m < 4 else q1)[:, (m % 4) * 128:(m % 4 + 1) * 128], lhsT=of[:, :], rhs=dg[:, m * 128:(m + 1) * 128], start=True, stop=True)
                return r
            sc.op("pe", fbc, reads=[of.b, dg.b], writes=[q0.b, q1.b])
            sc.op("dve", lambda e: e.tensor_copy(out=tl[:, 0:512], in_=q0[:, :]), reads=[q0.b], writes=[tl.b])
            sc.op("dve", lambda e: e.tensor_copy(out=tl[:, 512:1024], in_=q1[:, :]), reads=[q1.b, tl.b], writes=[tl.b])
            return tl

        def ln_rows(st, name, r):
            tl = st
            sc.dma(tl[:, :], lnrow_d[r:r + 1, :].partition_broadcast(128), tl.b, writes=[tl.b])
            return tl

        kpf = lambda d_: d_.rearrange("(k p) f -> p k f", p=128)
        sc.prio = "old"
        sc.reserve = 8
        NS = NT // 2
        W2 = 256
        sl = lambda t_, c, w=128: t_[:, c * w:(c + 1) * w]

        NPRE = 5
        w1pre = mkT(G, "w1pre", [128, NPRE * 4096], BF16)
        with contextlib.ExitStack() as PB:
            winGs = [mkT(PB, "winG%d" % i_, [128, 8 * 1024], BF16) for i_ in range(2)]
            wA = mkT(PB, "wA", [128, 4 * 1024], BF16)
            wB = mkT(PB, "wB", [128, 4 * 1024], BF16)
            wO = mkT(PB, "wO", [128, 8 * 1024], BF16)
            gt1bc = mkT(PB, "gt1bc", [128, D])
            ln1g = mkT(PB, "ln1g", [128, D])
            ln1b = mkT(PB, "ln1b", [128, D])
            xt = [mkT(PB, "xtB%d" % i, [128, 2 * D]) for i in range(2)]
            yin = [mkT(PB, "yin%d" % i, [128, 2 * D], BF16) for i in range(2)]
            hT = mkT(PB, "hTB", [128, 8 * W2], BF16)
            sigG = mkT(PB, "sigG", [128, 8 * W2])
            tAB = [mkT(PB, "tAB%d" % i, [128, 8 * W2]) for i in range(2)]
            mg = mkT(PB, "mg", [128, 8 * W2], BF16)
            tM = mkT(PB, "tM", [128, D])
            res = mkT(PB, "res", [128, D])
            h1o = [mkT(PB, "h1o%d" % i, [128, D]) for i in range(2)]
            smb = [mkT(PB, "smB%d" % i, [128, 16]) for i in range(4)]
            load_w_cast(winGs[0], kpf(win_d)[:, :, 2720:3744], 8, 1024)
            load_w_cast(wA, kpf(wa_d), 4, 1024)
            load_w_cast(winGs[1], kpf(win_d)[:, :, 3744:4768], 8, 1024)
            load_w_cast(wB, kpf(wb_d), 4, 1024)
            load_w_cast(wO, kpf(wout_d), 8, 1024)
            bcast_rows(gt1bc, None, "gt1bc", GT1, dg=h1o[0], of_=h1o[1])
            ln_rows(ln1g, "ln1g", 0)
            ln_rows(ln1b, "ln1b", 1)
            xv = lambda t_: t_[:, :].rearrange("p (s d) -> p s d", s=2)

            def load_b(s_):
                sc.dma(xv(xt[s_ % 2]), x_d[s_ * 256:(s_ + 1) * 256, :].rearrange("(s p) d -> p s d", p=128), xt[s_ % 2].b, writes=[xt[s_ % 2].b])
                sc.dma(xv(yin[s_ % 2]), yab_d[s_ * 256:(s_ + 1) * 256, :].rearrange("(s p) d -> p s d", p=128), yin[s_ % 2].b,
                       reads=[yab_db[2 * s_], yab_db[2 * s_ + 1]], writes=[yin[s_ % 2].b])

            def hT_b(s_):
                for sub in range(2):
                    x_to_hT(None, xt[s_ % 2], xt[s_ % 2].b, hT, 0, OPS1, SH1, tok0=sub * 128, col0=sub * D)
            load_b(0)
            hT_b(0)
            load_w_cast(w1pre, kpf(wff1_d)[:, 0:NPRE, :], NPRE, 4096)
            for s_ in range(NS):
                xs, yi = xt[s_ % 2], yin[s_ % 2]
                if s_ + 1 < NS:
                    load_b(s_ + 1)
                hk = lambda k: hT[:, k * W2:(k + 1) * W2]
                for br, (w_, yoff) in enumerate(((wA, 0), (wB, 4))):
                    pgs = [vbank() for _ in range(4)]

                    def fg(e, br=br, pgs=pgs):
                        r = None
                        for j in range(8):
                            for k in range(8):
                                c0 = k * 1024 + j * 128
                                r = e.matmul(sl(pgs[j // 2], j % 2, W2), lhsT=winGs[br][:, c0:c0 + 128], rhs=hk(k), start=(k == 0), stop=(k == 7))
                        return r
                    sc.op("pe", fg, reads=[winGs[br].b, hT.b], writes=[p_.b for p_ in pgs])
                    for q in range(4):
                        def fsg(e, br=br, q=q, pgs=pgs):
                            r = None
                            for jj in range(2):
                                j = q * 2 + jj
                                r = e.activation(out=sl(sigG, j, W2), in_=sl(pgs[q], jj, W2), func=AF.Sigmoid, bias=pc(PC_BGATE + br * 8 + j), scale=1.0)
                            return r
                        sc.op("act", fsg, reads=[pgs[q].b, pcol.b, sigG.b], writes=[sigG.b])
                    pbs = [vbank() for _ in range(4)]

                    def fbr(e, w_=w_, yoff=yoff, pbs=pbs, yi=yi):
                        r = None
                        for dch in range(8):
                            for k in range(4):
                                for sub in range(2):
                                    r = e.matmul(pbs[dch // 2][:, (dch % 2) * W2 + sub * 128:(dch % 2) * W2 + (sub + 1) * 128],
                                                 lhsT=w_[:, k * 1024 + dch * 128:k * 1024 + (dch + 1) * 128],
                                                 rhs=yi[:, sub * D + (yoff + k) * 128:sub * D + (yoff + k + 1) * 128], start=(k == 0), stop=(k == 3))
                        return r

                    def fbr2(e, w_=w_, yoff=yoff, pbs=pbs, yi=yi):
                        r = None
                        for dch in range(8):
                            for sub in range(2):
                                for k in range(4):
                                    r = e.matmul(pbs[dch // 2][:, (dch % 2) * W2 + sub * 128:(dch % 2) * W2 + (sub + 1) * 128],
                                                 lhsT=w_[:, k * 1024 + dch * 128:k * 1024 + (dch + 1) * 128],
                                                 rhs=yi[:, sub * D + (yoff + k) * 128:sub * D + (yoff + k + 1) * 128], start=(k == 0), stop=(k == 3))
                        return r
                    sc.op("pe", fbr2, reads=[w_.b, yi.b], writes=[p_.b for p_ in pbs])
                    dst = tAB[br]
                    for q in range(4):
                        sc.op("dve", lambda e, dst=dst, q=q, pbs=pbs: e.tensor_tensor(out=sl(dst, q, 512), in0=pbs[q][:, :], in1=sl(sigG, q, 512), op=ALU.mult),
                              reads=[pbs[q].b, sigG.b, dst.b], writes=[dst.b])
                sc.op("pool", lambda e: e.tensor_tensor(out=mg[:, :], in0=tAB[0][:, :], in1=tAB[1][:, :], op=ALU.add), reads=[tAB[0].b, tAB[1].b], writes=[mg.b])
                if s_ + 1 < NS:
                    hT_b(s_ + 1)
                for sub in range(2):
                    i = 2 * s_ + sub
                    ho = h1o[i % 2]
                    qm = [vbank(), vbank()]

                    def fmx(e, sub=sub, qm=qm):
                        r = None
                        for hf in range(2):
                            for k in range(8):
                                e.matmul(qm[hf][:, :], lhsT=mg[:, k * W2 + sub * 128:k * W2 + (sub + 1) * 128],
                                         rhs=wO[:, k * 1024 + hf * 512:k * 1024 + (hf + 1) * 512], start=(k == 0), stop=False)
                            r = e.matmul(qm[hf][:, :], lhsT=onesb[0:1, 0:128], rhs=browb[0:1, 512 + hf * 512:512 + (hf + 1) * 512], start=False, stop=True)
                        return r
                    sc.op("pe", fmx, reads=[mg.b, wO.b, onesb.b, browb.b], writes=[qm[0].b, qm[1].b])
                    for hf in range(2):
                        sc.op("dve", lambda e, hf=hf, qm=qm: e.tensor_tensor(out=tM[:, hf * 512:(hf + 1) * 512], in0=qm[hf][:, :], in1=gt1bc[:, hf * 512:(hf + 1) * 512], op=ALU.mult),
                              reads=[qm[hf].b, gt1bc.b, tM.b], writes=[tM.b])
                    sc.op("dve", lambda e, xs=xs, sub=sub: e.scalar_tensor_tensor(out=res[:, :], in0=xs[:, sub * D:(sub + 1) * D], scalar=ALPHA, in1=tM[:, :], op0=ALU.mult, op1=ALU.add),
                          reads=[xs.b, tM.b], writes=[res.b])
                    layernorm_rows(res, tM, ln1g, ln1b, smb, ho)
                    sc.dma(h1_d[i * 128:(i + 1) * 128, :], ho[:, :], ho.b, reads=[ho.b], writes=[h1_db[i]])
            sc.barrier()
            sc.flush()
        if os.environ.get("KSTOP") == "B":
            return nc

        with contextlib.ExitStack() as PC:
            w1b = mkT(PC, "w1b", [128, (8 - NPRE) * 4096], BF16)
            w2 = mkT(PC, "w2", [128, 32 * 1024], BF16)
            gt2bc = mkT(PC, "gt2bc", [128, D])
            ln2g = mkT(PC, "ln2g", [128, D])
            ln2b = mkT(PC, "ln2b", [128, D])
            hin = [mkT(PC, "hin%d" % i, [128, 2 * D]) for i in range(2)]
            hT2 = mkT(PC, "hT2", [128, 8 * W2], BF16)
            rl = [mkT(PC, "rl%d" % i, [128, 512]) for i in range(2)]
            hid = mkT(PC, "hid", [128, 32 * W2], BF16)
            tC = mkT(PC, "tC", [128, D])
            oo = [mkT(PC, "oo%d" % i, [128, D]) for i in range(2)]
            smc = [mkT(PC, "smC%d" % i, [128, 16]) for i in range(4)]
            load_w_cast(w1b, kpf(wff1_d)[:, NPRE:8, :], 8 - NPRE, 4096)
            load_w_cast(w2, kpf(wff2_d), 32, 1024)
            bcast_rows(gt2bc, None, "gt2bc", GT2, dg=oo[0], of_=oo[1])
            ln_rows(ln2g, "ln2g", 2)
            ln_rows(ln2b, "ln2b", 3)
            xv = lambda t_: t_[:, :].rearrange("p (s d) -> p s d", s=2)

            def load_c(s_):
                sc.dma(xv(hin[s_ % 2]), h1_d[s_ * 256:(s_ + 1) * 256, :].rearrange("(s p) d -> p s d", p=128), hin[s_ % 2].b,
                       reads=[h1_db[2 * s_], h1_db[2 * s_ + 1]], writes=[hin[s_ % 2].b])

            def hT_c(s_):
                for sub in range(2):
                    x_to_hT(None, hin[s_ % 2], hin[s_ % 2].b, hT2, 0, OPS2, SH2, tok0=sub * 128, col0=sub * D)
            load_c(0)
            if NS > 1:
                load_c(1)
            hT_c(0)
            for s_ in range(NS):
                hs = hin[s_ % 2]
                hk = lambda k: hT2[:, k * W2:(k + 1) * W2]
                for fq in range(16):
                    pf = vbank()
                    rr_ = rl[fq % 2]

                    def ff1(e, fq=fq, pf=pf):
                        r = None
                        for j in range(2):
                            f_ = fq * 2 + j
                            for k in range(8):
                                wt, kk_ = (w1pre, k) if k < NPRE else (w1b, k - NPRE)
                                r = e.matmul(sl(pf, j, W2), lhsT=wt[:, kk_ * 4096 + f_ * 128:kk_ * 4096 + (f_ + 1) * 128], rhs=hk(k), start=(k == 0), stop=(k == 7))
                        return r
                    sc.op("pe", ff1, reads=[w1pre.b, w1b.b, hT2.b], writes=[pf.b])

                    def frl(e, fq=fq, pf=pf, rr_=rr_):
                        r = None
                        for j in range(2):
                            r = e.activation(out=sl(rr_, j, W2), in_=sl(pf, j, W2), func=AF.Relu, bias=pc(PC_BFF1 + fq * 2 + j), scale=1.0)
                        return r
                    sc.op("act", frl, reads=[pf.b, pcol.b], writes=[rr_.b])
                    sc.op("pool", lambda e, fq=fq, rr_=rr_: e.tensor_tensor(out=hid[:, fq * 512:(fq + 1) * 512], in0=rr_[:, :], in1=rr_[:, :], op=ALU.mult),
                          reads=[rr_.b, hid.b], writes=[hid.b])
                if s_ + 1 < NS:
                    hT_c(s_ + 1)
                for sub in range(2):
                    i = 2 * s_ + sub
                    ot = oo[i % 2]
                    qo = [vbank(), vbank()]

                    def ff2(e, sub=sub, qo=qo):
                        r = None
                        for hf in range(2):
                            for f_ in range(32):
                                e.matmul(qo[hf][:, :], lhsT=hid[:, f_ * W2 + sub * 128:f_ * W2 + (sub + 1) * 128],
                                         rhs=w2[:, f_ * 1024 + hf * 512:f_ * 1024 + (hf + 1) * 512], start=(f_ == 0), stop=False)
                            r = e.matmul(qo[hf][:, :], lhsT=onesb[0:1, 0:128], rhs=browb[0:1, 1536 + hf * 512:1536 + (hf + 1) * 512], start=False, stop=True)
                        return r
                    sc.op("pe", ff2, reads=[hid.b, w2.b, onesb.b, browb.b], writes=[qo[0].b, qo[1].b])
                    for hf in range(2):
                        sc.op("dve", lambda e, hf=hf, qo=qo: e.tensor_tensor(out=tC[:, hf * 512:(hf + 1) * 512], in0=qo[hf][:, :], in1=gt2bc[:, hf * 512:(hf + 1) * 512], op=ALU.mult),
                              reads=[qo[hf].b, gt2bc.b, tC.b], writes=[tC.b])
                    sc.op("dve", lambda e, hs=hs, sub=sub: e.scalar_tensor_tensor(out=hs[:, sub * D:(sub + 1) * D], in0=hs[:, sub * D:(sub + 1) * D], scalar=ALPHA, in1=tC[:, :], op0=ALU.mult, op1=ALU.add),
                          reads=[hs.b, tC.b], writes=[hs.b])
                    layernorm_rows(hs, tC, ln2g, ln2b, smc, ot, c0=sub * D)
                    sc.dma(out_d[i * 128:(i + 1) * 128, :], ot[:, :], ot.b, reads=[ot.b], writes=[])
                if s_ + 2 < NS:
                    load_c(s_ + 2)
            sc.barrier()
            sc.flush()
            nc.all_engine_barrier()
    return nc


def build(S, n_tiles_c=2):
    box = []
    try:
        return _build(S, n_tiles_c, box)
    except StopBuild:
        return box[0]


def host_prep(inputs):
    f = lambda a: np.ascontiguousarray(np.asarray(a, np.float32))
    cols = lambda v, n: f(v).reshape(n, 128).T
    b_in, mu = f(inputs["b_in"][0]), f(inputs["mu_shift"][0])
    pcol = np.zeros((128, NPC), np.float32)
    pcol[:, PC_BU:PC_BU + 4] = cols(b_in[0:512], 4)
    pcol[:, PC_BRKV:PC_BRKV + 12] = cols(b_in[1024:2560], 12)
    pcol[0:64, PC_BXWXA] = b_in[2560:2624]
    pcol[0:96, PC_BXG] = b_in[2624:2720]
    pcol[:, PC_BGATE:PC_BGATE + 16] = cols(b_in[2720:4768], 16)
    pcol[:, PC_MURKV:PC_MURKV + 12] = cols(mu[0:1536], 12)
    pcol[0:64, PC_MUXWXA] = mu[1536:1600]
    pcol[0:96, PC_MUXG] = mu[1600:1696]
    for nm, c0 in (("w0", PC_W0), ("a0", PC_A0), ("k_k", PC_KK), ("k_a", PC_KA), ("r_k", PC_RK), ("gn_gain", PC_GNG),
                   ("gn_bias", PC_GNB), ("g_ln_v", PC_GLN), ("b_ln_v", PC_BLN)):
        pcol[:, c0:c0 + 4] = cols(f(inputs[nm][0]).reshape(-1), 4)
    pcol[:, PC_BFF1:PC_BFF1 + 32] = cols(inputs["b_ff1"][0], 32)
    pcol[:, PC_BADA:PC_BADA + 48] = cols(inputs["b_ada"][0], 48)
    s = np.arange(128)
    strict = (s[:, None] < s[None, :]).astype(np.float32)
    incl = (s[:, None] <= s[None, :]).astype(np.float32)
    low = (s[:, None] > s[None, :]).astype(np.float32)
    bd = ((s[:, None] // 64) == (s[None, :] // 64)).astype(np.float32)
    cst = np.zeros((128, NCS), np.float32)
    cst[:, CS_ID:CS_ID + 128] = np.eye(128, dtype=np.float32)
    cst[:, CS_M4:CS_M4 + 512] = np.concatenate([strict, incl, strict, incl], axis=1)
    cst[:, CS_ML4:CS_ML4 + 512] = np.concatenate([low] * 4, axis=1)
    cst[:, CS_BD4:CS_BD4 + 512] = np.concatenate([bd] * 4, axis=1)
    cst[:, CS_ONE] = 1.0
    cst[:, CS_ONE + 1] = LN_EPS
    cst[:, CS_ONE + 2] = GN_EPS
    brow = np.concatenate([b_in[512:1024], f(inputs["b_out"][0]), f(inputs["b_ff2"][0])])[None, :]
    lnrows = np.stack([f(inputs["ln1_g"][0]), f(inputs["ln1_b"][0]), f(inputs["ln2_g"][0]), f(inputs["ln2_b"][0])])
    wsT = f(f(inputs["w_spatial"][0]).transpose(2, 0, 1).reshape(128, 1024))
    bsp = f(inputs["b_spatial"][0]).reshape(4, 2, 128)
    bspb = f(np.repeat(bsp, 64, axis=1).transpose(1, 0, 2).reshape(128, 512))
    lw = f(np.concatenate([f(inputs["w_decay_up"][0]), f(inputs["w_aaa_up"][0])], axis=0))
    shared = {
        "w_ada": f(inputs["w_ada"][0]), "w_in": f(inputs["w_in"][0]), "pcol": pcol, "cst": cst, "brow": f(brow),
        "lnrows": f(lnrows), "wsT": wsT, "bspb": bspb, "lw": lw, "wg": f(inputs["w_gate_up"][0]),
        "w_branch_a": f(inputs["w_branch_a"][0]), "w_branch_b": f(inputs["w_branch_b"][0]), "w_out": f(inputs["w_out"][0]),
        "w_ff1": f(inputs["w_ff1"][0]), "w_ff2": f(inputs["w_ff2"][0]),
    }
    x, c = np.asarray(inputs["x"], np.float32), f(inputs["c"])
    maps = []
    for b in range(x.shape[0]):
        m = dict(shared)
        m["x"] = np.ascontiguousarray(x[b])
        m["ccol"] = f(np.repeat(c[b].reshape(8, 128).T, 2, axis=1))
        maps.append(m)
    return maps


def kernel(**inputs):
    x = np.asarray(inputs["x"])
    B, S, _ = x.shape
    maps = host_prep(inputs)
    nc = build(S)
    res = run_bass_kernel_spmd(nc, maps, core_ids=list(range(B)))
    return np.stack([np.asarray(r["out"]) for r in res.results], axis=0).astype(np.float32)
```
